# Optimizing a Trainium2 kernel written in Bass

```python
import math
import jax, jax.numpy as jnp
from jax import lax
import numpy as np

D_MODEL = 2048
BATCH = 2
SEQ = 4096
DEPTH = 4
DEC_BATCH = 32
DEC_SEQ = 4
PAST_LEN = 16384
PAGE_SIZE = 128

N_EVEN = (DEPTH + 1) // 2
N_ODD = DEPTH // 2

RET_HEADS = 8
RET_DK = 128
RET_DV = 128
RET_CHUNK = 128
RET_WIDTH = RET_HEADS * RET_DV

SWA_Q_HEADS = 16
SWA_KV_HEADS = 2
SWA_GROUP = SWA_Q_HEADS // SWA_KV_HEADS
SWA_HD = 64
WINDOW = 128
SWA_BLOCK = 128
SWA_WIDTH = SWA_Q_HEADS * SWA_HD

EVEN_SPLITS = (RET_HEADS * RET_DK, RET_HEADS * RET_DK, RET_WIDTH, RET_WIDTH,
               SWA_WIDTH, SWA_KV_HEADS * SWA_HD, SWA_KV_HEADS * SWA_HD)
EVEN_IN = sum(EVEN_SPLITS)
MIX_WIDTH = RET_WIDTH + SWA_WIDTH

SSM_GROUP_CH = 16
SSM_GROUPS = D_MODEL // SSM_GROUP_CH
SSM_P = 64

D_FF = 5632
CONV_W = 3

NORM_EPS = 1e-6

kernel_name = "hybrid_retention_swa_s5_convffn_step"


def rmsnorm(x, g):
    xf = x.astype(jnp.float32)
    y = xf * lax.rsqrt(jnp.mean(xf * xf, axis=-1, keepdims=True) + NORM_EPS)
    return (y * g.astype(jnp.float32)).astype(x.dtype)


def alibi_slopes(n):
    return 2.0 ** (-8.0 * jnp.arange(1, n + 1, dtype=jnp.float32) / n)


def retention(q, k, v, s0):
    b, L = q.shape[0], q.shape[1]
    c = math.gcd(L, RET_CHUNK)
    nc = L // c
    lg = jnp.log(1.0 - 2.0 ** (-5.0 - jnp.arange(RET_HEADS, dtype=jnp.float32)))
    idx = jnp.arange(c, dtype=jnp.float32)
    diff = idx[:, None] - idx[None, :]
    intra = jnp.where(diff >= 0, jnp.exp(lg[:, None, None] * jnp.maximum(diff, 0.0)), 0.0)
    read = jnp.exp(lg[None, :] * (idx[:, None] + 1.0))
    write = jnp.exp(lg[None, :] * (c - 1.0 - idx[:, None]))
    chunk_decay = jnp.exp(lg * c)
    qf = q.astype(jnp.float32)
    kf = k.astype(jnp.float32) * (RET_DK ** -0.5)
    vf = v.astype(jnp.float32)

    def to_chunks(t):
        return t.reshape((b, nc, c) + t.shape[2:]).swapaxes(0, 1)

    def step(s, inp):
        qc, kc, vc = inp
        sc = jnp.einsum('bihd,bjhd->bhij', qc, kc) * intra[None]
        o = (jnp.einsum('bhij,bjhe->bihe', sc, vc)
             + jnp.einsum('bihd,bhde->bihe', qc, s) * read[None, :, :, None])
        s = (s * chunk_decay[None, :, None, None]
             + jnp.einsum('bjhd,bjhe->bhde', kc * write[None, :, :, None], vc))
        return s, o

    s_fin, o = lax.scan(step, s0.astype(jnp.float32), (to_chunks(qf), to_chunks(kf), to_chunks(vf)))
    o = o.swapaxes(0, 1).reshape(b, L, RET_HEADS, RET_DV)
    return o, s_fin


def swa_attention(q, k, v, q_pos, k_pos, sinks):
    dist = q_pos[:, :, None] - k_pos[:, None, :]
    valid = (dist >= 0) & (dist <= WINDOW) & (k_pos[:, None, :] >= 0)
    slopes = alibi_slopes(SWA_Q_HEADS).reshape(SWA_KV_HEADS, SWA_GROUP)
    s = jnp.einsum('bnqhgd,bnshd->bnhgqs', q.astype(jnp.float32), k.astype(jnp.float32)) * (SWA_HD ** -0.5)
    s = s - slopes[None, None, :, :, None, None] * dist[None, :, None, None].astype(jnp.float32)
    s = jnp.where(valid[None, :, None, None], s, -jnp.inf)
    sink = sinks.astype(jnp.float32).reshape(SWA_KV_HEADS, SWA_GROUP)[None, None, :, :, None, None]
    m = jnp.maximum(jnp.max(s, axis=-1, keepdims=True), sink)
    p = jnp.exp(s - m)
    denom = jnp.sum(p, axis=-1, keepdims=True) + jnp.exp(sink - m)
    return jnp.einsum('bnhgqs,bnshd->bnqhgd', p / denom, v.astype(jnp.float32))


def even_mixer(h, w_in, w_out, sinks, ret_state, win_k, win_v):
    b, L, _ = h.shape
    proj = h @ w_in
    split_at = np.cumsum(np.array(EVEN_SPLITS))[:-1].tolist()
    rq, rk, rv, rg, aq, ak, av = jnp.split(proj, split_at, axis=-1)

    if ret_state is None:
        ret_state = jnp.zeros((b, RET_HEADS, RET_DK, RET_DV), jnp.float32)
    o_ret, s_ret = retention(rq.reshape(b, L, RET_HEADS, RET_DK), rk.reshape(b, L, RET_HEADS, RET_DK),
                             rv.reshape(b, L, RET_HEADS, RET_DV), ret_state)
    o_ret = o_ret * lax.rsqrt(jnp.mean(o_ret * o_ret, axis=-1, keepdims=True) + NORM_EPS)
    o_ret = o_ret.reshape(b, L, RET_WIDTH) * jax.nn.silu(rg.astype(jnp.float32))

    q = aq.reshape(b, L, SWA_KV_HEADS, SWA_GROUP, SWA_HD)
    k = ak.reshape(b, L, SWA_KV_HEADS, SWA_HD)
    v = av.reshape(b, L, SWA_KV_HEADS, SWA_HD)
    if win_k is None:
        nb = L // SWA_BLOCK

        def blocks_with_prev(t):
            prev = jnp.concatenate([jnp.zeros_like(t[:, :SWA_BLOCK]), t[:, :L - SWA_BLOCK]], axis=1)
            shp = (b, nb, SWA_BLOCK) + t.shape[2:]
            return jnp.concatenate([prev.reshape(shp), t.reshape(shp)], axis=2)

        qb = q.reshape(b, nb, SWA_BLOCK, SWA_KV_HEADS, SWA_GROUP, SWA_HD)
        kb, vb = blocks_with_prev(k), blocks_with_prev(v)
        start = jnp.arange(nb, dtype=jnp.int32) * SWA_BLOCK
        q_pos = start[:, None] + jnp.arange(SWA_BLOCK, dtype=jnp.int32)[None]
        k_pos = start[:, None] - SWA_BLOCK + jnp.arange(2 * SWA_BLOCK, dtype=jnp.int32)[None]
        keep = min(WINDOW, L)
        new_k, new_v = k[:, L - keep:], v[:, L - keep:]
    else:
        win = win_k.shape[1]
        kc = jnp.concatenate([win_k.astype(k.dtype), k], axis=1)
        vc = jnp.concatenate([win_v.astype(v.dtype), v], axis=1)
        qb, kb, vb = q[:, None], kc[:, None], vc[:, None]
        q_pos = (win + jnp.arange(L, dtype=jnp.int32))[None]
        k_pos = jnp.arange(win + L, dtype=jnp.int32)[None]
        new_k, new_v = kc[:, L:], vc[:, L:]
    o_swa = swa_attention(qb, kb, vb, q_pos, k_pos, sinks).reshape(b, L, SWA_WIDTH)

    out = jnp.concatenate([o_ret, o_swa], axis=-1).astype(h.dtype) @ w_out
    return out, s_ret, new_k, new_v


def ssm_mixer(h, lam_re, lam_im, log_step, b_re, b_im, c_re, c_im, d_skip, w_glu, st_re, st_im):
    b, L, _ = h.shape
    u = h.astype(jnp.float32).reshape(b, L, SSM_GROUPS, SSM_GROUP_CH)
    lam = lax.complex(lam_re.astype(jnp.float32), lam_im.astype(jnp.float32))
    delta = jnp.exp(log_step.astype(jnp.float32))[:, None]
    lam_bar = jnp.exp(lam * delta)
    coef = (lam_bar - 1.0) / lam
    b_bar = coef[..., None] * lax.complex(b_re.astype(jnp.float32), b_im.astype(jnp.float32))
    bu = lax.complex(jnp.einsum('blgh,gph->blgp', u, jnp.real(b_bar)),
                     jnp.einsum('blgh,gph->blgp', u, jnp.imag(b_bar)))
    a = jnp.broadcast_to(lam_bar[None, None], (1, L, SSM_GROUPS, SSM_P))

    def combine(e1, e2):
        a1, b1 = e1
        a2, b2 = e2
        return a1 * a2, a2 * b1 + b2

    a_cum, hs = lax.associative_scan(combine, (a, bu), axis=1)
    if st_re is not None:
        h0 = lax.complex(st_re.astype(jnp.float32), st_im.astype(jnp.float32))
        hs = hs + a_cum * h0[:, None]
    y = (jnp.einsum('blgp,ghp->blgh', jnp.real(hs), c_re.astype(jnp.float32))
         - jnp.einsum('blgp,ghp->blgh', jnp.imag(hs), c_im.astype(jnp.float32)))
    y = y.reshape(b, L, D_MODEL) + d_skip.astype(jnp.float32) * h.astype(jnp.float32)
    z = jax.nn.gelu(y).astype(h.dtype)
    za, zb = jnp.split(z @ w_glu, 2, axis=-1)
    out = za * jax.nn.sigmoid(zb)
    h_last = hs[:, -1]
    return out, jnp.real(h_last), jnp.imag(h_last)


def conv_ffn(h, w_a, w_g, conv_w, conv_b, w_down, prev):
    b, L, _ = h.shape
    a = h @ w_a
    g = h @ w_g
    if prev is None:
        prev = jnp.zeros((b, CONV_W - 1, D_FF), a.dtype)
    padded = jnp.concatenate([prev.astype(a.dtype), a], axis=1)
    conv = conv_b.astype(a.dtype)
    for j in range(CONV_W):
        conv = conv + conv_w[j] * padded[:, j:j + L]
    out = (jax.nn.gelu(conv) * g) @ w_down
    return out, padded[:, L:]


def run_group(x, p, st):
    new_ret, new_wk, new_wv, new_sre, new_sim, new_conv = [], [], [], [], [], []
    for layer in range(DEPTH):
        i = layer // 2
        h = rmsnorm(x, p['norm_mix_pre'][layer])
        if layer % 2 == 0:
            mix, s_ret, wk, wv = even_mixer(
                h, p['w_in_even'][i], p['w_out_even'][i], p['swa_sinks'][i],
                None if st is None else st['ret'][i],
                None if st is None else st['win_k'][i],
                None if st is None else st['win_v'][i])
            new_ret.append(s_ret)
            new_wk.append(wk)
            new_wv.append(wv)
        else:
            mix, sre, sim = ssm_mixer(
                h, p['ssm_lam_re'][i], p['ssm_lam_im'][i], p['ssm_log_step'][i],
                p['ssm_b_re'][i], p['ssm_b_im'][i], p['ssm_c_re'][i], p['ssm_c_im'][i],
                p['ssm_d'][i], p['w_glu'][i],
                None if st is None else st['ssm_re'][i],
                None if st is None else st['ssm_im'][i])
            new_sre.append(sre)
            new_sim.append(sim)
        x = x + rmsnorm(mix.astype(x.dtype), p['norm_mix_post'][layer])
        h = rmsnorm(x, p['norm_ffn_pre'][layer])
        f, cs = conv_ffn(h, p['ffn_w_a'][layer], p['ffn_w_g'][layer], p['ffn_conv_w'][layer],
                         p['ffn_conv_b'][layer], p['ffn_w_down'][layer],
                         None if st is None else st['conv'][layer])
        new_conv.append(cs)
        x = x + rmsnorm(f.astype(x.dtype), p['norm_ffn_post'][layer])
    return (x, jnp.stack(new_ret), jnp.stack(new_wk), jnp.stack(new_wv),
            jnp.stack(new_sre), jnp.stack(new_sim), jnp.stack(new_conv))


def setup_inputs(seed: int = 0) -> dict:
    key = jax.random.key(seed)
    ks = iter(list(jax.random.split(key, 32)))

    def nrm(shape, scale):
        return jax.random.normal(next(ks), shape, jnp.float32) * scale

    win_buf = min(WINDOW, PAST_LEN)
    x_prompt = nrm((BATCH, SEQ, D_MODEL), 1.0)
    x_sample = nrm((DEC_BATCH, DEC_SEQ, D_MODEL), 1.0)
    state_ret = nrm((N_EVEN, DEC_BATCH, RET_HEADS, RET_DK, RET_DV), 1.0)
    cache_swa_k = nrm((N_EVEN, DEC_BATCH, win_buf, SWA_KV_HEADS, SWA_HD), 1.0)
    cache_swa_v = nrm((N_EVEN, DEC_BATCH, win_buf, SWA_KV_HEADS, SWA_HD), 1.0)
    state_ssm_re = nrm((N_ODD, DEC_BATCH, SSM_GROUPS, SSM_P), 0.1)
    state_ssm_im = nrm((N_ODD, DEC_BATCH, SSM_GROUPS, SSM_P), 0.1)
    state_ffn_conv = nrm((DEPTH, DEC_BATCH, CONV_W - 1, D_FF), 1.0)
    norm_mix_pre = 1.0 + nrm((DEPTH, D_MODEL), 0.02)
    norm_mix_post = 1.0 + nrm((DEPTH, D_MODEL), 0.02)
    norm_ffn_pre = 1.0 + nrm((DEPTH, D_MODEL), 0.02)
    norm_ffn_post = 1.0 + nrm((DEPTH, D_MODEL), 0.02)
    w_in_even = nrm((N_EVEN, D_MODEL, EVEN_IN), D_MODEL ** -0.5)
    w_out_even = nrm((N_EVEN, MIX_WIDTH, D_MODEL), MIX_WIDTH ** -0.5)
    swa_sinks = nrm((N_EVEN, SWA_Q_HEADS), 1.0)
    ssm_lam_re = -0.5 + nrm((N_ODD, SSM_GROUPS, SSM_P), 0.01)
    ssm_lam_im = (jnp.pi * jnp.arange(SSM_P, dtype=jnp.float32))[None, None] + nrm((N_ODD, SSM_GROUPS, SSM_P), 0.01)
    ssm_log_step = jax.random.uniform(next(ks), (N_ODD, SSM_GROUPS), jnp.float32,
                                      minval=math.log(1e-3), maxval=math.log(1e-1))
    ssm_b_re = nrm((N_ODD, SSM_GROUPS, SSM_P, SSM_GROUP_CH), (2.0 * SSM_GROUP_CH) ** -0.5)
    ssm_b_im = nrm((N_ODD, SSM_GROUPS, SSM_P, SSM_GROUP_CH), (2.0 * SSM_GROUP_CH) ** -0.5)
    ssm_c_re = nrm((N_ODD, SSM_GROUPS, SSM_GROUP_CH, SSM_P), SSM_P ** -0.5)
    ssm_c_im = nrm((N_ODD, SSM_GROUPS, SSM_GROUP_CH, SSM_P), SSM_P ** -0.5)
    ssm_d = nrm((N_ODD, D_MODEL), 1.0)
    w_glu = nrm((N_ODD, D_MODEL, 2 * D_MODEL), D_MODEL ** -0.5)
    ffn_w_a = nrm((DEPTH, D_MODEL, D_FF), D_MODEL ** -0.5)
    ffn_w_g = nrm((DEPTH, D_MODEL, D_FF), D_MODEL ** -0.5)
    ffn_conv_w = nrm((DEPTH, CONV_W, D_FF), CONV_W ** -0.5)
    ffn_conv_b = nrm((DEPTH, D_FF), 0.01)
    ffn_w_down = nrm((DEPTH, D_FF, D_MODEL), D_FF ** -0.5)
    return {
        'x_prompt': x_prompt, 'x_sample': x_sample,
        'state_ret': state_ret, 'cache_swa_k': cache_swa_k, 'cache_swa_v': cache_swa_v,
        'state_ssm_re': state_ssm_re, 'state_ssm_im': state_ssm_im, 'state_ffn_conv': state_ffn_conv,
        'norm_mix_pre': norm_mix_pre, 'norm_mix_post': norm_mix_post,
        'norm_ffn_pre': norm_ffn_pre, 'norm_ffn_post': norm_ffn_post,
        'w_in_even': w_in_even, 'w_out_even': w_out_even, 'swa_sinks': swa_sinks,
        'ssm_lam_re': ssm_lam_re, 'ssm_lam_im': ssm_lam_im, 'ssm_log_step': ssm_log_step,
        'ssm_b_re': ssm_b_re, 'ssm_b_im': ssm_b_im, 'ssm_c_re': ssm_c_re, 'ssm_c_im': ssm_c_im,
        'ssm_d': ssm_d, 'w_glu': w_glu,
        'ffn_w_a': ffn_w_a, 'ffn_w_g': ffn_w_g, 'ffn_conv_w': ffn_conv_w, 'ffn_conv_b': ffn_conv_b,
        'ffn_w_down': ffn_w_down,
    }


def reference(x_prompt, x_sample, state_ret, cache_swa_k, cache_swa_v, state_ssm_re, state_ssm_im,
              state_ffn_conv, norm_mix_pre, norm_mix_post, norm_ffn_pre, norm_ffn_post,
              w_in_even, w_out_even, swa_sinks, ssm_lam_re, ssm_lam_im, ssm_log_step,
              ssm_b_re, ssm_b_im, ssm_c_re, ssm_c_im, ssm_d, w_glu,
              ffn_w_a, ffn_w_g, ffn_conv_w, ffn_conv_b, ffn_w_down):
    params = dict(norm_mix_pre=norm_mix_pre, norm_mix_post=norm_mix_post,
                  norm_ffn_pre=norm_ffn_pre, norm_ffn_post=norm_ffn_post,
                  w_in_even=w_in_even, w_out_even=w_out_even, swa_sinks=swa_sinks,
                  ssm_lam_re=ssm_lam_re, ssm_lam_im=ssm_lam_im, ssm_log_step=ssm_log_step,
                  ssm_b_re=ssm_b_re, ssm_b_im=ssm_b_im, ssm_c_re=ssm_c_re, ssm_c_im=ssm_c_im,
                  ssm_d=ssm_d, w_glu=w_glu, ffn_w_a=ffn_w_a, ffn_w_g=ffn_w_g,
                  ffn_conv_w=ffn_conv_w, ffn_conv_b=ffn_conv_b, ffn_w_down=ffn_w_down)
    sample_state = dict(ret=state_ret, win_k=cache_swa_k, win_v=cache_swa_v,
                        ssm_re=state_ssm_re, ssm_im=state_ssm_im, conv=state_ffn_conv)
    y_prompt, ret_p, wk_p, wv_p, sre_p, sim_p, conv_p = run_group(x_prompt, params, None)
    y_sample, ret_s, wk_s, wv_s, sre_s, sim_s, conv_s = run_group(x_sample, params, sample_state)
    return (y_prompt, y_sample, ret_p, ret_s, wk_p, wk_s, wv_p, wv_s,
            sre_p, sre_s, sim_p, sim_s, conv_p, conv_s)
```

```python
import math
import numpy as np
import concourse.bass as bass
import concourse.mybir as mybir
from concourse.bass_utils import run_bass_kernel_spmd

F32 = mybir.dt.float32
BF16 = mybir.dt.bfloat16
AF = mybir.ActivationFunctionType
ALU = mybir.AluOpType
AX = mybir.AxisListType

D = 2048
KT = 16
DEPTH = 4
DFF = 5632
MFF = 44
EVEN_IN = 5376
TP = 128
SEQ = 4096
NPT = SEQ // TP
EPS = 1e-6
WSLOT = 5632
NSLOT = 3
ARENA = 86 * 1024


class Buf:
    def __init__(self, name, t):
        self.name = name
        self.t = t
        self.last_w = None
        self.readers = {}

    def __getitem__(self, k):
        return self.t[k]


class DSem:
    def __init__(self, ctx, name):
        self.key = name
        ctx.sems[name] = ctx.nc.alloc_semaphore(name)
        self.count = 0
        ctx.dsems.append(self)


class Ctx:
    def __init__(self, nc):
        self.nc = nc
        self.engs = {"pe": nc.tensor, "act": nc.scalar, "dve": nc.vector, "pool": nc.gpsimd, "sp": nc.sync}
        self.sems = {}
        self.ecnt = {}
        for k in ("pe", "act", "dve", "pool"):
            self.sems[k] = nc.alloc_semaphore("e_" + k)
            self.ecnt[k] = 0
        self.known = {k: {} for k in self.engs}
        self.nbuf = 0
        self.dsems = []
        self.ar_base = None
        self.ar_off = 0
        self.ar_size = 0

    def init_arena(self, nbytes):
        base = (self.nc.sbuf_base + 31) // 32 * 32
        self.nc.alloc_sbuf_tensor("arena", [128, nbytes // 4], F32)
        self.ar_base = base
        self.ar_size = nbytes
        self.ar_off = 0

    def ar(self, name, shape, dt):
        esz = 2 if dt == BF16 else 4
        n = esz
        for d in shape[1:]:
            n *= d
        n = (n + 31) // 32 * 32
        assert self.ar_off + n <= self.ar_size, (name, self.ar_off, n, self.ar_size)
        self.nbuf += 1
        t = self.nc.alloc_sbuf_tensor_at("%s_%d" % (name, self.nbuf), list(shape), dt, offset=self.ar_base + self.ar_off)
        self.ar_off += n
        return Buf(name, t)

    def barrier(self):
        evs = [(k, v) for k, v in self.ecnt.items() if v] + [(d.key, d.count) for d in self.dsems if d.count]
        for e in self.engs:
            self._wait(e, evs)

    def phase(self):
        self.barrier()
        self.ar_off = 0

    def sb(self, name, shape, dt):
        self.nbuf += 1
        return Buf(name, self.nc.alloc_sbuf_tensor("%s_%d" % (name, self.nbuf), list(shape), dt))

    def ps(self, name, shape, dt=F32):
        self.nbuf += 1
        b = Buf(name, self.nc.alloc_psum_tensor("%s_%d" % (name, self.nbuf), list(shape), dt))
        b.is_psum = True
        return b

    def _wait(self, eng, evs):
        need = {}
        for (k, v) in evs:
            if eng == "pe" and k == "pe":
                continue
            if self.known[eng].get(k, 0) >= v:
                continue
            if need.get(k, 0) < v:
                need[k] = v
        for k, v in need.items():
            self.engs[eng].wait_ge(self.sems[k], v)
            self.known[eng][k] = v

    def _deps(self, reads, writes):
        evs = []
        for b in reads:
            if b.last_w is not None:
                evs.append(b.last_w)
            if getattr(b, "is_psum", False):
                evs.extend(b.readers.items())
        for b in writes:
            if b.last_w is not None:
                evs.append(b.last_w)
            evs.extend(b.readers.items())
        return evs

    def _mark(self, ev, reads, writes):
        for b in writes:
            b.last_w = ev
            b.readers = {}
        for b in reads:
            if b in writes:
                continue
            if b.readers.get(ev[0], 0) < ev[1]:
                b.readers[ev[0]] = ev[1]

    def op(self, eng, fn, reads=(), writes=()):
        self._wait(eng, self._deps(reads, writes))
        ins = fn(self.engs[eng])
        self.ecnt[eng] += 1
        ins.then_inc(self.sems[eng], 1)
        self._mark((eng, self.ecnt[eng]), reads, writes)

    def dma(self, q, out, in_, dsem, reads=(), writes=(), slow=False):
        evs = self._deps(reads, writes)
        if dsem.count:
            evs.append((dsem.key, dsem.count))
        self._wait(q, evs)
        kw = {"allow_slow_non_contiguous": True} if slow else {}
        ins = self.engs[q].dma_start(out=out, in_=in_, **kw)
        dsem.count += 16
        ins.then_inc(self.sems[dsem.key], 16)
        self._mark((dsem.key, dsem.count), reads, writes)

    def finish(self):
        allv = {}
        for k in ("pe", "act", "dve", "pool"):
            if self.ecnt[k]:
                allv[k] = self.ecnt[k]
        for k, s in self.sems.items():
            if k not in self.ecnt:
                allv[k] = self._dcount[k].count if k in self._dcount else 0
        self._wait("sp", [(k, v) for k, v in allv.items() if v])


def host_tables():
    tabs = {}
    lg = np.log(1.0 - 2.0 ** (-5.0 - np.arange(8, dtype=np.float64)))
    j = np.arange(128)[:, None]
    i = np.arange(128)[None, :]
    intraT = np.zeros((128, 8, 128), np.float32)
    for h in range(8):
        intraT[:, h, :] = np.where(i >= j, np.exp(lg[h] * np.maximum(i - j, 0)), 0.0) * (128 ** -0.5)
    tabs["intraT"] = intraT
    readB = np.zeros((128, 8, 128), np.float32)
    for h in range(8):
        readB[:, h, :] = np.exp(lg[h] * (np.arange(128) + 1.0))[None, :]
    tabs["readB"] = readB
    wr = np.zeros((128, 16), np.float32)
    for h in range(8):
        wr[:, h] = np.exp(lg[h] * (127.0 - np.arange(128))) * (128 ** -0.5)
        wr[:4, 8 + h] = np.exp(lg[h] * (3.0 - np.arange(4))) * (128 ** -0.5)
    tabs["writeT"] = wr
    q = np.arange(128)[:, None]
    k = np.arange(256)[None, :]
    dist = 128 + q - k
    valid = (dist >= 0) & (dist <= 128)
    nd = np.where(valid, -dist.astype(np.float32), -1.0e7).astype(np.float32)
    nd2 = nd.copy()
    nd2[:, :128] = -1.0e7
    tabs["negdist"] = np.concatenate([nd, nd2], axis=1).astype(np.float32)
    tabs["ident"] = np.eye(128, dtype=np.float32)
    return tabs


RET_DECAY = [float(1.0 - 2.0 ** (-5.0 - h)) for h in range(8)]
SLOPES = [float(2.0 ** (-8.0 * (h + 1) / 16)) for h in range(16)]


_CFG = dict(ntiles=NPT, depth=DEPTH, do_sample=True)
_DBG = dict(cut=99, nblk=21)


def build(ntiles=NPT, depth=DEPTH, do_sample=True):
    nc = bass.Bass("TRN2", target_bir_lowering=False)
    cx = Ctx(nc)
    nl4, nle, nlo = depth, (depth + 1) // 2, max(depth // 2, 1)

    def din(name, shape):
        return nc.dram_tensor(name, list(shape), F32, kind="ExternalInput").ap()

    def dout(name, shape):
        return nc.dram_tensor(name, list(shape), F32, kind="ExternalOutput").ap()

    xp = din("xp", [SEQ, D])
    xs = din("xs", [4, 4, D])
    st_ret = din("st_ret", [2, 4, 8, 128, 128])
    st_k = din("st_k", [2, 4, 128, 128])
    st_v = din("st_v", [2, 4, 128, 128])
    st_sre = din("st_sre", [2, 4, 128, 64])
    st_sim = din("st_sim", [2, 4, 128, 64])
    st_conv = din("st_conv", [4, 4, 2, DFF])
    g_mix_pre = din("g_mix_pre", [4, D])
    g_mix_post = din("g_mix_post", [4, D])
    g_ffn_pre = din("g_ffn_pre", [4, D])
    g_ffn_post = din("g_ffn_post", [4, D])
    w_in = din("w_in", [nle, D, EVEN_IN])
    w_out = din("w_out", [nle, D, D])
    sinks = din("sinks", [2, 16])
    lam_re = din("lam_re", [2, 128, 64])
    lam_im = din("lam_im", [2, 128, 64])
    log_step = din("log_step", [2, 128])
    b_re = din("b_re", [2, 128, 64, 16])
    b_im = din("b_im", [2, 128, 64, 16])
    c_re = din("c_re", [2, 128, 16, 64])
    c_im = din("c_im", [2, 128, 16, 64])
    ssm_d = din("ssm_d", [2, D])
    w_glu = din("w_glu", [nlo, D, 2 * D])
    w_a = din("w_a", [nl4, D, DFF])
    w_g = din("w_g", [nl4, D, DFF])
    conv_w = din("conv_w", [4, 3, DFF])
    conv_b = din("conv_b", [4, DFF])
    w_down = din("w_down", [nl4, DFF, D])
    t_intraT = din("t_intraT", [128, 8, 128])
    t_readB = din("t_readB", [128, 8, 128])
    t_writeT = din("t_writeT", [128, 16])
    t_negdist = din("t_negdist", [128, 512])
    t_ident = din("t_ident", [128, 128])

    yp = dout("yp", [SEQ, D])
    ys = dout("ys", [4, 4, D])
    o_ret_p = dout("o_ret_p", [2, 8, 128, 128])
    o_ret_s = dout("o_ret_s", [2, 4, 8, 128, 128])
    o_k_p = dout("o_k_p", [2, 128, 128])
    o_k_s = dout("o_k_s", [2, 4, 128, 128])
    o_v_p = dout("o_v_p", [2, 128, 128])
    o_v_s = dout("o_v_s", [2, 4, 128, 128])
    o_sre_p = dout("o_sre_p", [2, 128, 64])
    o_sre_s = dout("o_sre_s", [2, 4, 128, 64])
    o_sim_p = dout("o_sim_p", [2, 128, 64])
    o_sim_s = dout("o_sim_s", [2, 4, 128, 64])
    o_conv_p = dout("o_conv_p", [4, 2, DFF])
    o_conv_s = dout("o_conv_s", [4, 4, 2, DFF])

    xT = cx.sb("xT", [128, KT, TP], F32)
    hT = cx.sb("hT", [128, KT, TP], BF16)
    mixT = cx.sb("mixT", [128, KT, TP], F32)
    wsl = [cx.sb("wsl%d" % i, [128, WSLOT], BF16) for i in range(NSLOT)]
    wsem = [DSem(cx, "wsem%d" % i) for i in range(NSLOT)]
    wstate = {"i": 0}
    misc = DSem(cx, "misc")
    osem = DSem(cx, "osem")
    gains = cx.sb("gains", [128, 4, 4, KT], F32)
    cw = cx.sb("cw", [128, 4, 3, MFF], F32)
    cb = cx.sb("cb", [128, 4, MFF], F32)
    dsk = cx.sb("dsk", [128, 2, KT], F32)
    sinkB = cx.sb("sinkB", [128, 32], F32)
    intraT = cx.sb("intraT", [128, 8, 128], F32)
    readB = cx.sb("readB", [128, 8, 128], F32)
    writeT = cx.sb("writeT", [128, 16], F32)
    negdist = cx.sb("negdist", [128, 512], F32)
    ident32 = cx.sb("ident32", [128, 128], F32)
    ident = cx.sb("ident", [128, 128], BF16)
    onesD = cx.sb("onesD", [128, 128], F32)
    ones128 = cx.sb("ones128", [128, 128], F32)

    def ld(dst, dst_ap, src_ap, slow=False):
        cx.dma("sp", dst_ap, src_ap, misc, writes=[dst], slow=slow)

    for li in range(4):
        for ni, g in enumerate((g_mix_pre, g_mix_post, g_ffn_pre, g_ffn_post)):
            ld(gains, gains[:, ni, li, :], g[li].rearrange("(k p) -> p k", p=128), slow=True)
        for j in range(3):
            ld(cw, cw[:, li, j, :], conv_w[li, j].rearrange("(m p) -> p m", p=128), slow=True)
        ld(cb, cb[:, li, :], conv_b[li].rearrange("(m p) -> p m", p=128), slow=True)
    for i in range(2):
        ld(dsk, dsk[:, i, :], ssm_d[i].rearrange("(k p) -> p k", p=128), slow=True)
    ld(sinkB, sinkB[:, :], sinks.rearrange("a b -> (a b)").partition_broadcast(128), slow=True)
    ld(intraT, intraT[:], t_intraT)
    ld(readB, readB[:], t_readB)
    ld(writeT, writeT[:], t_writeT)
    ld(negdist, negdist[:], t_negdist)
    ld(ident32, ident32[:], t_ident)
    cx.op("dve", lambda e: e.tensor_copy(out=ident[:], in_=ident32[:]), reads=[ident32], writes=[ident])
    cx.op("dve", lambda e: e.memset(onesD[:], 1.0 / D), writes=[onesD])
    cx.op("dve", lambda e: e.memset(ones128[:], 1.0 / 128), writes=[ones128])

    Sp = [cx.sb("Sp", [128, 8, 128], F32) for i in range(2)]
    Sb = cx.sb("Sb", [128, 8, 128], BF16)
    kTe = [cx.sb("kTe", [64, 2, 128 + TP], BF16) for i in range(2)]
    vsw = [cx.sb("vsw", [128, 1 + TP // 128, 128], BF16) for i in range(2)]
    H3p = [cx.sb("H3p", [64, 3, 128], F32) for i in range(2)]
    A2 = [cx.sb("A2", [64, 2, 128], F32) for i in range(2)]
    B2 = [cx.sb("B2", [64, 2, 128], F32) for i in range(2)]
    C64 = [cx.sb("C64", [64, 128, 2, 16], BF16) for i in range(2)]
    gd = cx.sb("gd", [128, 2, KT], F32)
    negpi = cx.sb("negpi", [128, 1], F32)
    xin = cx.sb("xin", [128, D], F32)
    usem = [DSem(cx, "usem%d" % j) for j in range(8)]
    ysem = [DSem(cx, "ysem%d" % j) for j in range(8)]
    B16d = [nc.dram_tensor("B16d%d" % i, [16, 128 * 128], BF16) for i in range(2)]
    aprev = [[cx.sb("aprev", [128, MFF, 2], F32) for s in range(4)] for l in range(4)]

    PB = [cx.ps("pb%d" % i, [128, 512], F32) for i in range(8)]

    def scr(name, nblk, P, n):
        return Buf(name, nc.dram_tensor(name, [nblk, P, n], BF16))
    sc_ag = [scr("sc_ag%d" % l, MFF, 128, 4096) for l in range(depth)]
    sc_dn = [scr("sc_dn%d" % l, KT, 128, MFF * 128) for l in range(depth)]
    sc_in = [scr("sc_in%d" % i, 21, 128, 4096) for i in range(nle)]
    sc_or = [scr("sc_or%d" % i, KT, 128, 1024) for i in range(nle)]
    sc_os = [scr("sc_os%d" % i, KT, 64, 2048) for i in range(nle)]
    sc_gl = [scr("sc_gl%d" % i, KT, 128, 4096) for i in range(depth // 2)]
    ssem = [DSem(cx, "ssem%d" % i) for i in range(NSLOT)]
    lsem = [DSem(cx, "lsem%d" % i) for i in range(NSLOT)]

    def convert(scb, blk, P, parts):
        i = wstate["i"] % NSLOT
        wstate["i"] += 1
        b = wsl[i]
        off = 0
        for (src_ap, nk, ncols) in parts:
            view = b.t[0:P, off:off + nk * ncols].rearrange("p (k n) -> p k n", n=ncols)
            cx.dma("pool", view, src_ap.rearrange("(k p) n -> p k n", p=P), wsem[i], writes=[b])
            off += nk * ncols
        cx.dma("sp", scb.t.ap()[blk, 0:P, 0:off], b.t[0:P, 0:off], ssem[i], reads=[b], writes=[scb])

    def weight_prologue():
        for l in range(depth):
            if l % 2 == 0:
                i = l // 2
                for blk in range(21):
                    convert(sc_in[i], blk, 128, [(w_in[i][:, blk * 256:(blk + 1) * 256], KT, 256)])
                for dm in range(KT):
                    convert(sc_or[i], dm, 128, [(w_out[i][0:1024, dm * 128:(dm + 1) * 128], 8, 128)])
                    convert(sc_os[i], dm, 64, [(w_out[i][1024:2048, dm * 128:(dm + 1) * 128], 16, 128)])
            else:
                i = l // 2
                for dm in range(KT):
                    convert(sc_gl[i], dm, 128, [(w_glu[i][:, dm * 128:(dm + 1) * 128], KT, 128),
                                                (w_glu[i][:, D + dm * 128:D + (dm + 1) * 128], KT, 128)])
            for m in range(MFF):
                convert(sc_ag[l], m, 128, [(w_a[l][:, m * 128:(m + 1) * 128], KT, 128), (w_g[l][:, m * 128:(m + 1) * 128], KT, 128)])
            for dm in range(KT):
                convert(sc_dn[l], dm, 128, [(w_down[l][:, dm * 128:(dm + 1) * 128], MFF, 128)])

    def wget(scb, blk, P, n):
        i = wstate["i"] % NSLOT
        wstate["i"] += 1
        b = wsl[i]
        cx.dma("sp", b.t[0:P, 0:n], scb.t.ap()[blk, 0:P, 0:n], lsem[i], reads=[scb], writes=[b])
        return b

    tmpn = [cx.sb("tmpn", [128, TP], F32) for _ in range(2)]
    ssq = cx.sb("ssq", [128, TP], F32)
    rstd = cx.sb("rstd", [128, TP], F32)

    def rms_stats(src, T, nk, ones, pbank):
        for k in range(nk):
            tb = tmpn[k % 2]
            cx.op("act", lambda e, k=k, tb=tb: e.activation(out=tb[:, 0:T], in_=src[:, k, 0:T], func=AF.Square),
                  reads=[src], writes=[tb])
            if k == 0:
                cx.op("dve", lambda e, tb=tb: e.tensor_copy(out=ssq[:, 0:T], in_=tb[:, 0:T]), reads=[tb], writes=[ssq])
            else:
                cx.op("dve", lambda e, tb=tb: e.tensor_add(out=ssq[:, 0:T], in0=ssq[:, 0:T], in1=tb[:, 0:T]),
                      reads=[tb, ssq], writes=[ssq])
        cx.op("pe", lambda e: e.matmul(pbank[:, 0:T], ones[:, :], ssq[:, 0:T], start=True, stop=True),
              reads=[ones, ssq], writes=[pbank])
        cx.op("act", lambda e: e.activation(out=rstd[:, 0:T], in_=pbank[:, 0:T], func=AF.Sqrt, bias=EPS_AP[:, 0:1]),
              reads=[pbank, EPS_B], writes=[rstd])
        cx.op("dve", lambda e: e.reciprocal(out=rstd[:, 0:T], in_=rstd[:, 0:T]), reads=[rstd], writes=[rstd])

    EPS_B = cx.sb("epsb", [128, 1], F32)
    EPS_AP = EPS_B
    cx.op("dve", lambda e: e.memset(EPS_B[:], EPS), writes=[EPS_B])

    def prenorm(ni, li, T):
        rms_stats(xT, T, KT, onesD, PB[7])
        for k in range(KT):
            cx.op("dve", lambda e, k=k: e.scalar_tensor_tensor(
                out=hT[:, k, 0:T], in0=xT[:, k, 0:T], scalar=gains[:, ni, li, k:k + 1], in1=rstd[:, 0:T],
                op0=ALU.mult, op1=ALU.mult), reads=[xT, gains, rstd], writes=[hT])

    def postnorm_add(ni, li, T):
        rms_stats(mixT, T, KT, onesD, PB[7])
        for k in range(KT):
            tb = tmpn[k % 2]
            cx.op("dve", lambda e, k=k, tb=tb: e.scalar_tensor_tensor(
                out=tb[:, 0:T], in0=mixT[:, k, 0:T], scalar=gains[:, ni, li, k:k + 1], in1=rstd[:, 0:T],
                op0=ALU.mult, op1=ALU.mult), reads=[mixT, gains, rstd], writes=[tb])
            cx.op("dve", lambda e, k=k, tb=tb: e.tensor_add(out=xT[:, k, 0:T], in0=xT[:, k, 0:T], in1=tb[:, 0:T]),
                  reads=[tb, xT], writes=[xT])

    def mm_fm(pb, wview, c0, M, rhs_buf, rhs_fn, nk, T, extra_reads=()):
        def fn(e):
            ins = None
            for k in range(nk):
                ins = e.matmul(pb[0:M, 0:T], wview[0][:, k, c0:c0 + M], rhs_fn(k), start=(k == 0), stop=(k == nk - 1))
            return ins
        cx.op("pe", fn, reads=[wview[1], rhs_buf] + list(extra_reads), writes=[pb])

    def ffn(li, seqs, T):
        cx.phase()
        act = cx.ar("act", [128, MFF, TP], BF16)
        aext = [cx.ar("aext", [128, TP + 8], F32) for _ in range(2)]
        cacc = [cx.ar("cacc", [128, TP + 8], F32) for _ in range(2)]
        prenorm(2, li, T)
        TE = T + 2 * len(seqs)
        for m in range(MFF):
            bA = bG = wget(sc_ag[li], m, 128, 4096)
            vAG = bA.t[:, 0:4096].rearrange("p (j k n) -> p j k n", j=2, n=128)
            vA, vG = vAG[:, 0], vAG[:, 1]
            pa = PB[(2 * m) % 6]
            pg = PB[(2 * m + 1) % 6]
            mm_fm(pa, (vA, bA), 0, 128, hT, lambda k: hT[:, k, 0:T], KT, T)
            mm_fm(pg, (vG, bG), 0, 128, hT, lambda k: hT[:, k, 0:T], KT, T)
            ae = aext[m % 2]
            ca = cacc[m % 2]
            for si, (slot, c0, L) in enumerate(seqs):
                e0 = c0 + 2 * si
                cx.op("act", lambda e, e0=e0, slot=slot: e.copy(out=ae[:, e0:e0 + 2], in_=aprev[li][slot][:, m, :]),
                      reads=[aprev[li][slot]], writes=[ae])
                cx.op("act", lambda e, e0=e0, c0=c0, L=L: e.copy(out=ae[:, e0 + 2:e0 + 2 + L], in_=pa[:, c0:c0 + L]),
                      reads=[pa], writes=[ae])
            cx.op("dve", lambda e: e.tensor_scalar(out=ca[:, 2:TE], in0=ae[:, 2:TE], scalar1=cw[:, li, 2, m:m + 1],
                                                   scalar2=cb[:, li, m:m + 1], op0=ALU.mult, op1=ALU.add),
                  reads=[ae, cw, cb], writes=[ca])
            cx.op("dve", lambda e: e.scalar_tensor_tensor(out=ca[:, 2:TE], in0=ae[:, 1:TE - 1], scalar=cw[:, li, 1, m:m + 1],
                                                          in1=ca[:, 2:TE], op0=ALU.mult, op1=ALU.add),
                  reads=[ae, cw, ca], writes=[ca])
            cx.op("dve", lambda e: e.scalar_tensor_tensor(out=ca[:, 2:TE], in0=ae[:, 0:TE - 2], scalar=cw[:, li, 0, m:m + 1],
                                                          in1=ca[:, 2:TE], op0=ALU.mult, op1=ALU.add),
                  reads=[ae, cw, ca], writes=[ca])
            cx.op("act", lambda e: e.activation(out=ca[:, 2:TE], in_=ca[:, 2:TE], func=AF.Gelu), reads=[ca], writes=[ca])
            for si, (slot, c0, L) in enumerate(seqs):
                e0 = c0 + 2 * si
                cx.op("dve", lambda e, e0=e0, c0=c0, L=L: e.tensor_tensor(
                    out=act[:, m, c0:c0 + L], in0=ca[:, e0 + 2:e0 + 2 + L], in1=pg[:, c0:c0 + L], op=ALU.mult),
                    reads=[ca, pg], writes=[act])
                cx.op("act", lambda e, e0=e0, L=L, slot=slot: e.copy(out=aprev[li][slot][:, m, :], in_=ae[:, e0 + L:e0 + L + 2]),
                      reads=[ae], writes=[aprev[li][slot]])
        for dm in range(KT):
            bW, vW = wload_down(li, dm)
            pb = PB[dm % 6]
            mm_fm(pb, (vW, bW), 0, 128, act, lambda k: act[:, k, 0:T], MFF, T)
            cx.op("act", lambda e, pb=pb, dm=dm: e.copy(out=mixT[:, dm, 0:T], in_=pb[:, 0:T]), reads=[pb], writes=[mixT])
        postnorm_add(3, li, T)

    def wload_down(li, dm):
        b = wget(sc_dn[li], dm, 128, MFF * 128)
        return b, b.t[:, 0:MFF * 128].rearrange("p (k n) -> p k n", n=128)

    def even_mixer(i, li, seqs, T, prompt, first, last):
        cx.phase()
        rqT = cx.ar("rqT", [128, 8, TP], BF16)
        rkT = cx.ar("rkT", [128, 8, TP], BF16)
        rgT = cx.ar("rgT", [128, 8, TP], BF16)
        ktok = cx.ar("ktok", [128, 4, 1024], BF16)
        vtok = cx.ar("vtok", [128, 4, 1024], BF16)
        aqT = cx.ar("aqT", [64, 16, TP], BF16)
        kvtok32 = cx.ar("kvtok32", [128, 4, 256], F32)
        vcur = cx.ar("vcur", [128, 4, 128], BF16)
        oret = cx.ar("oret", [128, 8, TP], F32)
        oTret = cx.ar("oTret", [128, 8, TP], BF16)
        oTswa = cx.ar("oTswa", [64, 16, TP], BF16)
        scT = cx.ar("scT", [128, 128], BF16)
        qr = cx.ar("qr", [128, 128], BF16)
        sb32 = cx.ar("sb32", [128, 256], F32)
        p32 = cx.ar("p32", [128, 256], F32)
        pn = cx.ar("pn", [128, 256], BF16)
        pT = cx.ar("pT", [128, 256], BF16)
        sm = cx.ar("sm", [128, 8], F32)
        if prompt:
            Ssl = [Sp[i]]
        else:
            Ssl = [cx.ar("Ss", [128, 8, 128], F32) for _ in range(4)]
            kTs = cx.ar("kTs", [64, 2, 4 * 132], BF16)
            kst32 = cx.ar("kst32", [64, 2, 128], F32)
            vst32 = cx.ar("vst32", [128, 128], F32)
            vprev_s = [cx.ar("vprev_s", [128, 128], BF16) for _ in range(4)]
            cx.op("dve", lambda e: e.memset(kTs[:], 0.0), writes=[kTs])
            for s_ in range(4):
                cx.dma("sp", Ssl[s_][:], st_ret[i, s_].rearrange("h d e -> d h e"), misc, writes=[Ssl[s_]])
                for kvh in range(2):
                    cx.dma("sp", kst32[:, kvh, :], st_k[i, s_][:, kvh * 64:(kvh + 1) * 64].rearrange("t d -> d t"), misc,
                           writes=[kst32], slow=True)
                cx.op("act", lambda e, s_=s_: e.copy(out=kTs[:, :, s_ * 132:s_ * 132 + 128], in_=kst32[:]), reads=[kst32], writes=[kTs])
                cx.dma("sp", vst32[:], st_v[i, s_], misc, writes=[vst32])
                cx.op("act", lambda e, s_=s_: e.copy(out=vprev_s[s_][:], in_=vst32[:]), reads=[vst32], writes=[vprev_s[s_]])
                cx.dma("sp", o_v_s[i, s_, 0:124, :], vst32[4:128, :], osem, reads=[vst32])
                cx.dma("sp", vst32[:], st_k[i, s_], misc, writes=[vst32])
                cx.dma("sp", o_k_s[i, s_, 0:124, :], vst32[4:128, :], osem, reads=[vst32])
        cx.op("dve", lambda e: e.memset(vcur[:], 0.0), writes=[vcur])
        cx.op("dve", lambda e: e.memset(ktok[:], 0.0), writes=[ktok])
        cx.op("dve", lambda e: e.memset(vtok[:], 0.0), writes=[vtok])

        def keyview(li_, prompt_, si, L):
            if prompt_:
                return kTe[li_], kTe[li_].t[:, :, 0:128 + L]
            return kTs, kTs.t[:, :, si * 132:si * 132 + 128 + L]

        prenorm(0, li, T)
        wcol = 0 if prompt else 8

        def cut_here():
            cx.op("dve", lambda e: e.memset(mixT[:], 0.0), writes=[mixT])
            postnorm_add(1, li, T)
        if _DBG["cut"] <= 0:
            return cut_here()
        for blk in range(min(21, _DBG['nblk'])):
            c = blk * 256
            bW = wget(sc_in[i], blk, 128, 4096)
            vW = bW.t[:, 0:4096].rearrange("p (k n) -> p k n", n=256)
            if c < 1024 or 1024 <= c < 2048 or 3072 <= c < 4096:
                for hh in range(2):
                    h = (c % 1024) // 128 + hh
                    pb = PB[(2 * blk + hh) % 4]
                    mm_fm(pb, (vW, bW), hh * 128, 128, hT, lambda k: hT[:, k, 0:T], KT, T)
                    if c < 1024:
                        cx.op("act", lambda e, pb=pb, h=h: e.copy(out=rqT[:, h, 0:T], in_=pb[:, 0:T]), reads=[pb], writes=[rqT])
                    elif c < 2048:
                        cx.op("act", lambda e, pb=pb, h=h: e.copy(out=rkT[:, h, 0:T], in_=pb[:, 0:T]), reads=[pb], writes=[rkT])
                    else:
                        cx.op("act", lambda e, pb=pb, h=h: e.activation(out=rgT[:, h, 0:T], in_=pb[:, 0:T], func=AF.Silu),
                              reads=[pb], writes=[rgT])
            if 1024 <= c < 3072 or c >= 5120:
                for si, (slot, c0, L) in enumerate(seqs):
                    pb = PB[4 + (si % 2)]

                    def fn(e, pb=pb, c0=c0, L=L):
                        ins = None
                        for k in range(KT):
                            ins = e.matmul(pb[0:L, 0:256], hT[:, k, c0:c0 + L], vW[:, k, 0:256], start=(k == 0), stop=(k == KT - 1))
                        return ins
                    cx.op("pe", fn, reads=[bW, hT], writes=[pb])
                    if c < 2048:
                        off = c - 1024
                        for hh in range(2):
                            h = off // 128 + hh
                            cx.op("dve", lambda e, pb=pb, L=L, si=si, off=off, hh=hh, h=h: e.tensor_scalar(
                                out=ktok[0:L, si, off + hh * 128:off + (hh + 1) * 128], in0=pb[0:L, hh * 128:(hh + 1) * 128],
                                scalar1=writeT[0:L, wcol + h:wcol + h + 1], scalar2=None, op0=ALU.mult),
                                reads=[pb, writeT], writes=[ktok])
                    elif c < 3072:
                        off = c - 2048
                        cx.op("act", lambda e, pb=pb, L=L, si=si, off=off: e.copy(out=vtok[0:L, si, off:off + 256], in_=pb[0:L, 0:256]),
                              reads=[pb], writes=[vtok])
                    else:
                        cx.op("act", lambda e, pb=pb, L=L, si=si: e.copy(out=kvtok32[0:L, si, :], in_=pb[0:L, 0:256]),
                              reads=[pb], writes=[kvtok32])
                        cx.op("dve", lambda e, L=L, si=si: e.tensor_copy(out=vcur[0:L, si, :], in_=kvtok32[0:L, si, 128:256]),
                              reads=[kvtok32], writes=[vcur])
            if 4096 <= c < 5120:
                for hh in range(4):
                    hq = (c - 4096) // 64 + hh
                    pb = PB[(blk + hh) % 4]
                    mm_fm(pb, (vW, bW), hh * 64, 64, hT, lambda k: hT[:, k, 0:T], KT, T)
                    cx.op("act", lambda e, pb=pb, hq=hq: e.mul(out=aqT[0:64, hq, 0:T], in_=pb[0:64, 0:T], mul=0.125),
                          reads=[pb], writes=[aqT])
            if c >= 5120:
                for kvh in range(2):
                    pb = PB[kvh]
                    mm_fm(pb, (vW, bW), kvh * 64, 64, hT, lambda k: hT[:, k, 0:T], KT, T)
                    for si, (slot, c0, L) in enumerate(seqs):
                        kb, kv = keyview(i, prompt, si, L)
                        cx.op("act", lambda e, pb=pb, kv=kv, kvh=kvh, c0=c0, L=L: e.copy(out=kv[0:64, kvh, 128:128 + L], in_=pb[0:64, c0:c0 + L]),
                              reads=[pb], writes=[kb])
        if _DBG["cut"] <= 1:
            return cut_here()
        for si, (slot, c0, L) in enumerate(seqs):
            Sst = Ssl[si]
            cx.op("act", lambda e, Sst=Sst: e.copy(out=Sb[:], in_=Sst[:]), reads=[Sst], writes=[Sb])
            for h in range(8):
                p0, p1, p2 = PB[0 + (h % 2) * 3], PB[1 + (h % 2) * 3], PB[2 + (h % 2) * 3]
                cx.op("pe", lambda e, p0=p0, h=h, c0=c0, L=L: e.matmul(p0[0:L, 0:L], rkT[:, h, c0:c0 + L], rqT[:, h, c0:c0 + L], start=True, stop=True),
                      reads=[rkT, rqT], writes=[p0])
                cx.op("dve", lambda e, p0=p0, h=h, L=L: e.tensor_tensor(out=scT[0:L, 0:L], in0=p0[0:L, 0:L], in1=intraT[0:L, h, 0:L], op=ALU.mult),
                      reads=[p0, intraT], writes=[scT])
                cx.op("dve", lambda e, h=h, c0=c0, L=L: e.tensor_tensor(out=qr[:, 0:L], in0=rqT[:, h, c0:c0 + L], in1=readB[:, h, 0:L], op=ALU.mult),
                      reads=[rqT, readB], writes=[qr])

                def fo(e, p1=p1, h=h, si=si, L=L):
                    e.matmul(p1[:, 0:L], vtok[0:L, si, h * 128:(h + 1) * 128], scT[0:L, 0:L], start=True, stop=False)
                    return e.matmul(p1[:, 0:L], Sb[:, h, :], qr[:, 0:L], start=False, stop=True)
                cx.op("pe", fo, reads=[vtok, scT, Sb, qr], writes=[p1])
                cx.op("act", lambda e, p1=p1, h=h, c0=c0, L=L: e.copy(out=oret[:, h, c0:c0 + L], in_=p1[:, 0:L]), reads=[p1], writes=[oret])
                cx.op("pe", lambda e, p2=p2, h=h, si=si, L=L: e.matmul(p2[:, 0:128], ktok[0:L, si, h * 128:(h + 1) * 128],
                                                                     vtok[0:L, si, h * 128:(h + 1) * 128], start=True, stop=True),
                      reads=[ktok, vtok], writes=[p2])
                cx.op("dve", lambda e, p2=p2, h=h, Sst=Sst, L=L: e.scalar_tensor_tensor(
                    out=Sst[:, h, :], in0=Sst[:, h, :], scalar=float(RET_DECAY[h] ** L), in1=p2[:, 0:128], op0=ALU.mult, op1=ALU.add),
                    reads=[p2, Sst], writes=[Sst])
        if _DBG["cut"] <= 2:
            return cut_here()
        for h in range(8):
            tb = tmpn[h % 2]
            cx.op("act", lambda e, tb=tb, h=h: e.activation(out=tb[:, 0:T], in_=oret[:, h, 0:T], func=AF.Square), reads=[oret], writes=[tb])
            cx.op("pe", lambda e, tb=tb: e.matmul(PB[7][:, 0:T], ones128[:, :], tb[:, 0:T], start=True, stop=True),
                  reads=[ones128, tb], writes=[PB[7]])
            cx.op("act", lambda e: e.activation(out=rstd[:, 0:T], in_=PB[7][:, 0:T], func=AF.Sqrt, bias=EPS_B[:, 0:1]),
                  reads=[PB[7], EPS_B], writes=[rstd])
            cx.op("dve", lambda e: e.reciprocal(out=rstd[:, 0:T], in_=rstd[:, 0:T]), reads=[rstd], writes=[rstd])
            cx.op("dve", lambda e, tb=tb, h=h: e.tensor_tensor(out=tb[:, 0:T], in0=oret[:, h, 0:T], in1=rstd[:, 0:T], op=ALU.mult),
                  reads=[oret, rstd], writes=[tb])
            cx.op("dve", lambda e, tb=tb, h=h: e.tensor_tensor(out=oTret[:, h, 0:T], in0=tb[:, 0:T], in1=rgT[:, h, 0:T], op=ALU.mult),
                  reads=[tb, rgT], writes=[oTret])
        if _DBG["cut"] <= 3:
            return cut_here()
        ndoff = 256 if first else 0
        for si, (slot, c0, L) in enumerate(seqs):
            kb, kv = keyview(i, prompt, si, L)
            vpb = vsw[i] if prompt else vprev_s[si]
            vpv = vsw[i].t[:, 0, :] if prompt else vprev_s[si].t[:, :]
            NK = 128 + L
            for hq in range(16):
                kvh = hq // 8
                sc = i * 16 + hq
                pS, pTt, pO = PB[(hq % 2) * 3], PB[(hq % 2) * 3 + 1], PB[(hq % 2) * 3 + 2]
                cx.op("pe", lambda e, pS=pS, hq=hq, kvh=kvh, c0=c0, L=L, kv=kv: e.matmul(
                    pS[0:L, 0:NK], aqT[0:64, hq, c0:c0 + L], kv[0:64, kvh, 0:NK], start=True, stop=True), reads=[aqT, kb], writes=[pS])
                cx.op("dve", lambda e, pS=pS, hq=hq, L=L: e.scalar_tensor_tensor(
                    out=sb32[0:L, 0:NK], in0=negdist[0:L, ndoff:ndoff + NK], scalar=SLOPES[hq], in1=pS[0:L, 0:NK], op0=ALU.mult, op1=ALU.add),
                    reads=[negdist, pS], writes=[sb32])
                cx.op("dve", lambda e, L=L: e.tensor_reduce(out=sm[0:L, 0:1], in_=sb32[0:L, 0:NK], axis=AX.X, op=ALU.max), reads=[sb32], writes=[sm])
                cx.op("dve", lambda e, L=L, sc=sc: e.tensor_scalar(out=sm[0:L, 1:2], in0=sm[0:L, 0:1], scalar1=sinkB[0:L, sc:sc + 1], scalar2=-1.0,
                                                                  op0=ALU.max, op1=ALU.mult), reads=[sm, sinkB], writes=[sm])
                cx.op("act", lambda e, L=L: e.activation(out=p32[0:L, 0:NK], in_=sb32[0:L, 0:NK], func=AF.Exp, bias=sm[0:L, 1:2]),
                      reads=[sb32, sm], writes=[p32])
                cx.op("act", lambda e, L=L, sc=sc: e.activation(out=sm[0:L, 2:3], in_=sinkB[0:L, sc:sc + 1], func=AF.Exp, bias=sm[0:L, 1:2]),
                      reads=[sinkB, sm], writes=[sm])
                cx.op("dve", lambda e, L=L: e.tensor_reduce(out=sm[0:L, 3:4], in_=p32[0:L, 0:NK], axis=AX.X, op=ALU.add), reads=[p32, sm], writes=[sm])
                cx.op("dve", lambda e, L=L: e.tensor_add(out=sm[0:L, 4:5], in0=sm[0:L, 3:4], in1=sm[0:L, 2:3]), reads=[sm], writes=[sm])
                cx.op("dve", lambda e, L=L: e.reciprocal(out=sm[0:L, 5:6], in_=sm[0:L, 4:5]), reads=[sm], writes=[sm])
                cx.op("dve", lambda e, L=L: e.tensor_scalar(out=pn[0:L, 0:NK], in0=p32[0:L, 0:NK], scalar1=sm[0:L, 5:6], scalar2=None, op0=ALU.mult),
                      reads=[p32, sm], writes=[pn])

                def ft(e, pTt=pTt, L=L):
                    e.matmul(pTt[:, 0:L], pn[0:L, 0:128], ident[0:L, 0:L], start=True, stop=True)
                    return e.matmul(pTt[0:L, 128:128 + L], pn[0:L, 128:128 + L], ident[0:L, 0:L], start=True, stop=True)
                cx.op("pe", ft, reads=[pn, ident], writes=[pTt])
                cx.op("act", lambda e, pTt=pTt, L=L: e.copy(out=pT[:, 0:L], in_=pTt[:, 0:L]), reads=[pTt], writes=[pT])
                cx.op("act", lambda e, pTt=pTt, L=L: e.copy(out=pT[0:L, 128:128 + L], in_=pTt[0:L, 128:128 + L]), reads=[pTt], writes=[pT])

                def fv(e, pO=pO, kvh=kvh, si=si, L=L, vpv=vpv):
                    e.matmul(pO[0:64, 0:L], vpv[:, kvh * 64:(kvh + 1) * 64], pT[:, 0:L], start=True, stop=False)
                    return e.matmul(pO[0:64, 0:L], vcur[0:L, si, kvh * 64:(kvh + 1) * 64], pT[0:L, 128:128 + L], start=False, stop=True)
                cx.op("pe", fv, reads=[vpb, vcur, pT], writes=[pO])
                cx.op("act", lambda e, pO=pO, hq=hq, c0=c0, L=L: e.copy(out=oTswa[0:64, hq, c0:c0 + L], in_=pO[0:64, 0:L]), reads=[pO], writes=[oTswa])
        if _DBG["cut"] <= 4:
            return cut_here()
        if prompt:
            cx.op("act", lambda e: e.copy(out=kTe[i][:, :, 0:128], in_=kTe[i][:, :, TP:TP + 128]), reads=[kTe[i]], writes=[kTe[i]])
            cx.op("act", lambda e: e.copy(out=vsw[i][:, 0, :], in_=vcur[:, 0, :]), reads=[vcur], writes=[vsw[i]])
        if prompt and last:
            cx.dma("sp", o_ret_p[i].rearrange("h d e -> d h e"), Sp[i][:], osem, reads=[Sp[i]])
            cx.dma("sp", o_k_p[i], kvtok32[:, 0, 0:128], osem, reads=[kvtok32])
            cx.dma("sp", o_v_p[i], kvtok32[:, 0, 128:256], osem, reads=[kvtok32])
        if not prompt:
            for s_ in range(4):
                cx.dma("sp", o_ret_s[i, s_].rearrange("h d e -> d h e"), Ssl[s_][:], osem, reads=[Ssl[s_]])
                cx.dma("sp", o_k_s[i, s_, 124:128, :], kvtok32[0:4, s_, 0:128], osem, reads=[kvtok32])
                cx.dma("sp", o_v_s[i, s_, 124:128, :], kvtok32[0:4, s_, 128:256], osem, reads=[kvtok32])
        for dm in range(KT):
            bR = wget(sc_or[i], dm, 128, 1024)
            vR = bR.t[:, 0:1024].rearrange("p (k n) -> p k n", n=128)
            bS_ = wget(sc_os[i], dm, 64, 2048)
            vS = bS_.t[0:64, 0:2048].rearrange("p (k n) -> p k n", n=128)
            pb = PB[dm % 6]

            def fw(e, pb=pb, vR=vR, vS=vS):
                ins = None
                for k in range(8):
                    ins = e.matmul(pb[:, 0:T], vR[:, k, :], oTret[:, k, 0:T], start=(k == 0), stop=False)
                for hq in range(16):
                    ins = e.matmul(pb[:, 0:T], vS[0:64, hq, :], oTswa[0:64, hq, 0:T], start=False, stop=(hq == 15))
                return ins
            cx.op("pe", fw, reads=[bR, bS_, oTret, oTswa], writes=[pb])
            cx.op("act", lambda e, pb=pb, dm=dm: e.copy(out=mixT[:, dm, 0:T], in_=pb[:, 0:T]), reads=[pb], writes=[mixT])
        postnorm_add(1, li, T)

    PI = math.pi

    def s5_prologue(i):
        cx.phase()
        lr = cx.ar("lr", [64, 128], F32)
        lim = cx.ar("lim", [64, 128], F32)
        dl = cx.ar("dl", [64, 128], F32)
        ar_ = cx.ar("ar_", [64, 128], F32)
        ai = cx.ar("ai", [64, 128], F32)
        t1 = cx.ar("t1", [64, 128], F32)
        t2 = cx.ar("t2", [64, 128], F32)
        lbre = cx.ar("lbre", [64, 128], F32)
        lbim = cx.ar("lbim", [64, 128], F32)
        cre = cx.ar("cre", [64, 128], F32)
        cim = cx.ar("cim", [64, 128], F32)
        bre = cx.ar("bre", [64, 128, 16], F32)
        bim = cx.ar("bim", [64, 128, 16], F32)
        Bre = cx.ar("Bre", [64, 128, 16], F32)
        Bim = cx.ar("Bim", [64, 128, 16], F32)
        tb = cx.ar("tb", [64, 128, 16], F32)
        B16 = cx.ar("B16", [16, 128, 2, 64], BF16)
        cx.dma("sp", lr[:], lam_re[i].rearrange("g p -> p g"), misc, writes=[lr], slow=True)
        cx.dma("sp", lim[:], lam_im[i].rearrange("g p -> p g"), misc, writes=[lim], slow=True)
        cx.dma("sp", dl[:], log_step[i].partition_broadcast(64), misc, writes=[dl], slow=True)
        for g0 in range(0, 128, 32):
            cx.dma("sp", bre[:, g0:g0 + 32, :], b_re[i][g0:g0 + 32].rearrange("g p h -> p g h"), misc, writes=[bre], slow=True)
            cx.dma("sp", bim[:, g0:g0 + 32, :], b_im[i][g0:g0 + 32].rearrange("g p h -> p g h"), misc, writes=[bim], slow=True)
        cx.op("dve", lambda e: e.memset(negpi[:], -PI), writes=[negpi])
        cx.op("act", lambda e: e.activation(out=dl[:], in_=dl[:], func=AF.Exp), reads=[dl], writes=[dl])
        cx.op("dve", lambda e: e.tensor_tensor(out=ar_[:], in0=lr[:], in1=dl[:], op=ALU.mult), reads=[lr, dl], writes=[ar_])
        cx.op("dve", lambda e: e.tensor_tensor(out=ai[:], in0=lim[:], in1=dl[:], op=ALU.mult), reads=[lim, dl], writes=[ai])
        cx.op("act", lambda e: e.activation(out=ar_[:], in_=ar_[:], func=AF.Exp), reads=[ar_], writes=[ar_])
        ti32 = cx.ar("ti32", [64, 128], mybir.dt.int32)
        tf = cx.ar("tf", [64, 128], F32)
        tm = cx.ar("tm", [64, 128], F32)

        def sin_of(dst, shift):
            cx.op("dve", lambda e: e.tensor_scalar(out=dst[:], in0=ai[:], scalar1=shift, scalar2=None, op0=ALU.add), reads=[ai], writes=[dst])
            cx.op("dve", lambda e: e.tensor_scalar(out=tf[:], in0=dst[:], scalar1=1.0 / (2 * PI), scalar2=None, op0=ALU.mult), reads=[dst], writes=[tf])
            cx.op("dve", lambda e: e.tensor_copy(out=ti32[:], in_=tf[:]), reads=[tf], writes=[ti32])
            cx.op("dve", lambda e: e.tensor_copy(out=tf[:], in_=ti32[:]), reads=[ti32], writes=[tf])
            cx.op("dve", lambda e: e.scalar_tensor_tensor(out=dst[:], in0=tf[:], scalar=-2 * PI, in1=dst[:], op0=ALU.mult, op1=ALU.add),
                  reads=[tf, dst], writes=[dst])
            cx.op("dve", lambda e: e.tensor_scalar(out=tm[:], in0=dst[:], scalar1=PI, scalar2=-2 * PI, op0=ALU.is_gt, op1=ALU.mult), reads=[dst], writes=[tm])
            cx.op("dve", lambda e: e.tensor_add(out=dst[:], in0=dst[:], in1=tm[:]), reads=[dst, tm], writes=[dst])
            cx.op("dve", lambda e: e.tensor_scalar(out=tm[:], in0=dst[:], scalar1=-PI, scalar2=2 * PI, op0=ALU.is_lt, op1=ALU.mult), reads=[dst], writes=[tm])
            cx.op("dve", lambda e: e.tensor_add(out=dst[:], in0=dst[:], in1=tm[:]), reads=[dst, tm], writes=[dst])
            cx.op("act", lambda e: e.activation(out=dst[:], in_=dst[:], func=AF.Sin), reads=[dst], writes=[dst])
        sin_of(t1, 0.0)
        sin_of(t2, 0.5 * PI)
        cx.op("dve", lambda e: e.tensor_tensor(out=lbim[:], in0=ar_[:], in1=t1[:], op=ALU.mult), reads=[ar_, t1], writes=[lbim])
        cx.op("dve", lambda e: e.tensor_tensor(out=lbre[:], in0=ar_[:], in1=t2[:], op=ALU.mult), reads=[ar_, t2], writes=[lbre])
        for r_ in range(2):
            cx.op("dve", lambda e, r_=r_: e.tensor_copy(out=A2[i][:, r_, :], in_=lbre[:]), reads=[lbre], writes=[A2[i]])
        cx.op("dve", lambda e: e.tensor_scalar(out=B2[i][:, 0, :], in0=lbim[:], scalar1=-1.0, scalar2=None, op0=ALU.mult),
              reads=[lbim], writes=[B2[i]])
        cx.op("dve", lambda e: e.tensor_copy(out=B2[i][:, 1, :], in_=lbim[:]), reads=[lbim], writes=[B2[i]])
        cx.op("dve", lambda e: e.tensor_scalar(out=t1[:], in0=lbre[:], scalar1=-1.0, scalar2=None, op0=ALU.add), reads=[lbre], writes=[t1])
        cx.op("dve", lambda e: e.tensor_tensor(out=t2[:], in0=lr[:], in1=lr[:], op=ALU.mult), reads=[lr], writes=[t2])
        cx.op("dve", lambda e: e.tensor_tensor(out=ai[:], in0=lim[:], in1=lim[:], op=ALU.mult), reads=[lim], writes=[ai])
        cx.op("dve", lambda e: e.tensor_add(out=t2[:], in0=t2[:], in1=ai[:]), reads=[t2, ai], writes=[t2])
        cx.op("dve", lambda e: e.reciprocal(out=t2[:], in_=t2[:]), reads=[t2], writes=[t2])
        cx.op("dve", lambda e: e.tensor_tensor(out=cre[:], in0=t1[:], in1=lr[:], op=ALU.mult), reads=[t1, lr], writes=[cre])
        cx.op("dve", lambda e: e.tensor_tensor(out=ai[:], in0=lbim[:], in1=lim[:], op=ALU.mult), reads=[lbim, lim], writes=[ai])
        cx.op("dve", lambda e: e.tensor_add(out=cre[:], in0=cre[:], in1=ai[:]), reads=[cre, ai], writes=[cre])
        cx.op("dve", lambda e: e.tensor_tensor(out=cre[:], in0=cre[:], in1=t2[:], op=ALU.mult), reads=[cre, t2], writes=[cre])
        cx.op("dve", lambda e: e.tensor_tensor(out=cim[:], in0=lbim[:], in1=lr[:], op=ALU.mult), reads=[lbim, lr], writes=[cim])
        cx.op("dve", lambda e: e.tensor_tensor(out=ai[:], in0=t1[:], in1=lim[:], op=ALU.mult), reads=[t1, lim], writes=[ai])
        cx.op("dve", lambda e: e.tensor_sub(out=cim[:], in0=cim[:], in1=ai[:]), reads=[cim, ai], writes=[cim])
        cx.op("dve", lambda e: e.tensor_tensor(out=cim[:], in0=cim[:], in1=t2[:], op=ALU.mult), reads=[cim, t2], writes=[cim])
        creB = cre.t[:].unsqueeze(2).to_broadcast([64, 128, 16])
        cimB = cim.t[:].unsqueeze(2).to_broadcast([64, 128, 16])
        cx.op("dve", lambda e: e.tensor_tensor(out=Bre[:], in0=bre[:], in1=creB, op=ALU.mult), reads=[bre, cre], writes=[Bre])
        cx.op("dve", lambda e: e.tensor_tensor(out=tb[:], in0=bim[:], in1=cimB, op=ALU.mult), reads=[bim, cim], writes=[tb])
        cx.op("dve", lambda e: e.tensor_sub(out=Bre[:], in0=Bre[:], in1=tb[:]), reads=[Bre, tb], writes=[Bre])
        cx.op("dve", lambda e: e.tensor_tensor(out=Bim[:], in0=bim[:], in1=creB, op=ALU.mult), reads=[bim, cre], writes=[Bim])
        cx.op("dve", lambda e: e.tensor_tensor(out=tb[:], in0=bre[:], in1=cimB, op=ALU.mult), reads=[bre, cim], writes=[tb])
        cx.op("dve", lambda e: e.tensor_add(out=Bim[:], in0=Bim[:], in1=tb[:]), reads=[Bim, tb], writes=[Bim])
        for g0 in range(0, 128, 4):
            pb = PB[(g0 // 4) % 4]

            def ftr(e, pb=pb, g0=g0):
                ins = None
                for gg in range(4):
                    for ri, src in enumerate((Bre, Bim)):
                        ins = e.transpose(pb[0:16, (gg * 2 + ri) * 64:(gg * 2 + ri + 1) * 64], src[:, g0 + gg, :], ident32[0:64, 0:64])
                return ins
            cx.op("pe", ftr, reads=[Bre, Bim, ident32], writes=[pb])
            cx.op("act", lambda e, pb=pb, g0=g0: e.copy(out=B16[0:16, g0:g0 + 4, :, :], in_=pb[0:16, 0:512]), reads=[pb], writes=[B16])
        cx.dma("sp", B16d[i].ap(), B16[:], osem, reads=[B16])
        cx.phase()
        cnat = cx.ar("cnat", [128, 2, 16, 64], F32)
        cx.dma("sp", cnat[:, 0, :, :], c_re[i], misc, writes=[cnat])
        cx.dma("sp", cnat[:, 1, :, :], c_im[i], misc, writes=[cnat])
        for h in range(16):
            pb = PB[4 + h % 2]

            def ftc(e, pb=pb, h=h):
                e.transpose(pb[0:64, 0:128], cnat[:, 0, h, :], ident32[:, :])
                return e.transpose(pb[0:64, 128:256], cnat[:, 1, h, :], ident32[:, :])
            cx.op("pe", ftc, reads=[cnat, ident32], writes=[pb])
            cx.op("act", lambda e, pb=pb, h=h: e.copy(out=C64[i][:, :, 0, h], in_=pb[0:64, 0:128]), reads=[pb], writes=[C64[i]])
            cx.op("act", lambda e, pb=pb, h=h: e.mul(out=C64[i][:, :, 1, h], in_=pb[0:64, 128:256], mul=-1.0), reads=[pb], writes=[C64[i]])
        cx.op("dve", lambda e: e.tensor_tensor(out=gd[:, i, :], in0=gains[:, 0, 2 * i + 1, :], in1=dsk[:, i, :], op=ALU.mult),
              reads=[gains, dsk], writes=[gd])

    def s5_mixer(i, li, seqs, T, prompt, last):
        cx.phase()
        W = 16 if prompt else 4
        B16 = cx.ar("B16", [16, 128, 2, 64], BF16)
        u16w = cx.ar("u16w", [16, 128, W], BF16)
        Z = cx.ar("Z", [64, 2, 128, W], F32)
        Hh = cx.ar("Hh", [64, 2, 128, W], BF16)
        y16w = cx.ar("y16w", [16, 128, W], F32)
        yT = cx.ar("yT", [128, KT, TP], F32)
        s1 = cx.ar("s1", [64, 2, 128], F32)
        s2 = cx.ar("s2", [64, 2, 128], F32)
        sig = [cx.ar("sig", [128, TP], F32) for _ in range(2)]
        cx.dma("sp", B16[:], B16d[i].ap(), misc, writes=[B16])
        if prompt:
            H3s = [H3p[i]]
        else:
            H3s = [cx.ar("H3s", [64, 3, 128], F32) for _ in range(4)]
            for s_ in range(4):
                cx.dma("sp", H3s[s_][:, 0, :], st_sre[i, s_].rearrange("g p -> p g"), misc, writes=[H3s[s_]], slow=True)
                cx.dma("sp", H3s[s_][:, 1, :], st_sim[i, s_].rearrange("g p -> p g"), misc, writes=[H3s[s_]], slow=True)
                cx.op("act", lambda e, s_=s_: e.copy(out=H3s[s_][:, 2, :], in_=H3s[s_][:, 0, :]), reads=[H3s[s_]], writes=[H3s[s_]])
        prenorm(0, li, T)
        u16v = u16w.t[:].rearrange("h (k g) w -> h k g w", g=8)
        y16v = y16w.t[:].rearrange("h (k g) w -> h k g w", g=8)
        for si, (slot, c0, L) in enumerate(seqs):
            H3 = H3s[si]
            for t0 in range(0, L, W):
                a0 = c0 + t0
                for gl in range(8):
                    cx.dma("sp", u16v[:, :, gl, :], hT[gl * 16:(gl + 1) * 16, :, a0:a0 + W], usem[gl], reads=[hT], writes=[u16w])
                for gc in range(8):
                    pa, pbk = PB[(gc % 2) * 2], PB[(gc % 2) * 2 + 1]

                    def fb(e, pa=pa, pbk=pbk, gc=gc):
                        ins = None
                        for gg in range(16):
                            g = gc * 16 + gg
                            e.matmul(pa[0:64, gg * W:(gg + 1) * W], B16[0:16, g, 0, :], u16w[0:16, g, :], start=True, stop=True)
                            ins = e.matmul(pbk[0:64, gg * W:(gg + 1) * W], B16[0:16, g, 1, :], u16w[0:16, g, :], start=True, stop=True)
                        return ins
                    cx.op("pe", fb, reads=[B16, u16w], writes=[pa, pbk])
                    cx.op("act", lambda e, pa=pa, gc=gc: e.copy(out=Z[:, 0, gc * 16:(gc + 1) * 16, :], in_=pa[0:64, 0:16 * W]), reads=[pa], writes=[Z])
                    cx.op("act", lambda e, pbk=pbk, gc=gc: e.copy(out=Z[:, 1, gc * 16:(gc + 1) * 16, :], in_=pbk[0:64, 0:16 * W]), reads=[pbk], writes=[Z])
                for t in range(W):
                    cx.op("dve", lambda e: e.tensor_tensor(out=s1[:], in0=A2[i][:], in1=H3[:, 0:2, :], op=ALU.mult), reads=[A2[i], H3], writes=[s1])
                    cx.op("dve", lambda e: e.tensor_tensor(out=s2[:], in0=B2[i][:], in1=H3[:, 1:3, :], op=ALU.mult), reads=[B2[i], H3], writes=[s2])
                    cx.op("dve", lambda e: e.tensor_add(out=s1[:], in0=s1[:], in1=s2[:]), reads=[s1, s2], writes=[s1])
                    cx.op("dve", lambda e, t=t: e.tensor_add(out=H3[:, 0:2, :], in0=s1[:], in1=Z[:, :, :, t]), reads=[s1, Z], writes=[H3])
                    cx.op("act", lambda e: e.copy(out=H3[:, 2, :], in_=H3[:, 0, :]), reads=[H3], writes=[H3])
                    cx.op("act", lambda e, t=t: e.copy(out=Hh[:, :, :, t], in_=H3[:, 0:2, :]), reads=[H3], writes=[Hh])
                for gc in range(8):
                    pc = PB[4 + gc % 2]

                    def fc(e, pc=pc, gc=gc):
                        ins = None
                        for gg in range(16):
                            g = gc * 16 + gg
                            e.matmul(pc[0:16, gg * W:(gg + 1) * W], C64[i][:, g, 0, :], Hh[:, 0, g, :], start=True, stop=False)
                            ins = e.matmul(pc[0:16, gg * W:(gg + 1) * W], C64[i][:, g, 1, :], Hh[:, 1, g, :], start=False, stop=True)
                        return ins
                    cx.op("pe", fc, reads=[C64[i], Hh], writes=[pc])
                    cx.op("act", lambda e, pc=pc, gc=gc: e.copy(out=y16w[0:16, gc * 16:(gc + 1) * 16, :], in_=pc[0:16, 0:16 * W]), reads=[pc], writes=[y16w])
                for gl in range(8):
                    cx.dma("sp", yT[gl * 16:(gl + 1) * 16, :, a0:a0 + W], y16v[:, :, gl, :], ysem[gl], reads=[y16w], writes=[yT])
        if prompt and last:
            cx.dma("sp", o_sre_p[i].rearrange("g p -> p g"), H3p[i][:, 0, :], osem, reads=[H3p[i]], slow=True)
            cx.dma("sp", o_sim_p[i].rearrange("g p -> p g"), H3p[i][:, 1, :], osem, reads=[H3p[i]], slow=True)
        if not prompt:
            for s_ in range(4):
                cx.dma("sp", o_sre_s[i, s_].rearrange("g p -> p g"), H3s[s_][:, 0, :], osem, reads=[H3s[s_]], slow=True)
                cx.dma("sp", o_sim_s[i, s_].rearrange("g p -> p g"), H3s[s_][:, 1, :], osem, reads=[H3s[s_]], slow=True)
        for k in range(KT):
            tb = tmpn[k % 2]
            cx.op("dve", lambda e, k=k, tb=tb: e.scalar_tensor_tensor(
                out=tb[:, 0:T], in0=xT[:, k, 0:T], scalar=gd[:, i, k:k + 1], in1=rstd[:, 0:T], op0=ALU.mult, op1=ALU.mult),
                reads=[xT, gd, rstd], writes=[tb])
            cx.op("dve", lambda e, k=k, tb=tb: e.tensor_add(out=tb[:, 0:T], in0=tb[:, 0:T], in1=yT[:, k, 0:T]), reads=[tb, yT], writes=[tb])
            cx.op("act", lambda e, k=k, tb=tb: e.activation(out=hT[:, k, 0:T], in_=tb[:, 0:T], func=AF.Gelu), reads=[tb], writes=[hT])
        for dm in range(KT):
            bA = bB = wget(sc_gl[i], dm, 128, 4096)
            vAB = bA.t[:, 0:4096].rearrange("p (j k n) -> p j k n", j=2, n=128)
            vA, vB = vAB[:, 0], vAB[:, 1]
            pa, pbk = PB[(2 * dm) % 4], PB[(2 * dm + 1) % 4]
            mm_fm(pa, (vA, bA), 0, 128, hT, lambda k: hT[:, k, 0:T], KT, T)
            mm_fm(pbk, (vB, bB), 0, 128, hT, lambda k: hT[:, k, 0:T], KT, T)
            sg = sig[dm % 2]
            cx.op("act", lambda e, pbk=pbk, sg=sg: e.activation(out=sg[:, 0:T], in_=pbk[:, 0:T], func=AF.Sigmoid), reads=[pbk], writes=[sg])
            cx.op("dve", lambda e, pa=pa, sg=sg, dm=dm: e.tensor_tensor(out=mixT[:, dm, 0:T], in0=sg[:, 0:T], in1=pa[:, 0:T], op=ALU.mult),
                  reads=[sg, pa], writes=[mixT])
        postnorm_add(1, li, T)

    def load_x(src2d, T):
        cx.dma("sp", xin[0:T, :], src2d, misc, writes=[xin])
        for k4 in range(0, KT, 4):
            pb = PB[(k4 // 4) % 2]

            def ft(e, pb=pb, k4=k4):
                ins = None
                for j in range(4):
                    ins = e.transpose(pb[:, j * 128:j * 128 + T], xin[0:T, (k4 + j) * 128:(k4 + j + 1) * 128], ident32[0:T, 0:T])
                return ins
            cx.op("pe", ft, reads=[xin, ident32], writes=[pb])
            for j in range(4):
                cx.op("act", lambda e, pb=pb, k4=k4, j=j: e.copy(out=xT[:, k4 + j, 0:T], in_=pb[:, j * 128:j * 128 + T]), reads=[pb], writes=[xT])

    def store_y(dst2d, T):
        for k4 in range(0, KT, 4):
            pb = PB[(k4 // 4) % 2]

            def ft(e, pb=pb, k4=k4):
                ins = None
                for j in range(4):
                    ins = e.transpose(pb[0:T, j * 128:(j + 1) * 128], xT[:, k4 + j, 0:T], ident32[:, :])
                return ins
            cx.op("pe", ft, reads=[xT, ident32], writes=[pb])
            cx.op("act", lambda e, pb=pb, k4=k4: e.copy(out=xin[0:T, k4 * 128:(k4 + 4) * 128], in_=pb[0:T, 0:512]), reads=[pb], writes=[xin])
        cx.dma("sp", dst2d, xin[0:T, :], osem, reads=[xin])

    def zero_states():
        for l in range(4):
            cx.op("dve", lambda e, l=l: e.memset(aprev[l][0][:], 0.0), writes=[aprev[l][0]])
        for i in range(2):
            cx.op("dve", lambda e, i=i: e.memset(Sp[i][:], 0.0), writes=[Sp[i]])
            cx.op("dve", lambda e, i=i: e.memset(kTe[i][:], 0.0), writes=[kTe[i]])
            cx.op("dve", lambda e, i=i: e.memset(vsw[i][:], 0.0), writes=[vsw[i]])
            cx.op("dve", lambda e, i=i: e.memset(H3p[i][:], 0.0), writes=[H3p[i]])

    def run_layers(seqs, T, first, prompt, last):
        for l in range(depth):
            if l % 2 == 0:
                even_mixer(l // 2, l, seqs, T, prompt, first, last)
            else:
                s5_mixer(l // 2, l, seqs, T, prompt, last)
            ffn(l, seqs, T)

    cx.init_arena(ARENA)
    zero_states()
    weight_prologue()
    for i in range(depth // 2):
        s5_prologue(i)
    for ti in range(ntiles):
        load_x(xp[ti * TP:(ti + 1) * TP, :], TP)
        run_layers([(0, 0, TP)], TP, ti == 0, True, ti == ntiles - 1)
        store_y(yp[ti * TP:(ti + 1) * TP, :], TP)
    for l in range(depth):
        for j in range(2):
            cx.dma("sp", o_conv_p[l, j].rearrange("(m p) -> p m", p=128), aprev[l][0][:, :, j], osem,
                   reads=[aprev[l][0]], slow=True)
    if do_sample:
        for l in range(depth):
            for s_ in range(4):
                for j in range(2):
                    cx.dma("sp", aprev[l][s_][:, :, j], st_conv[l, s_, j].rearrange("(m p) -> p m", p=128), misc,
                           writes=[aprev[l][s_]], slow=True)
        load_x(xs.rearrange("s t d -> (s t) d"), 16)
        run_layers([(s_, 4 * s_, 4) for s_ in range(4)], 16, False, False, True)
        store_y(ys.rearrange("s t d -> (s t) d"), 16)
        for l in range(depth):
            for s_ in range(4):
                for j in range(2):
                    cx.dma("sp", o_conv_s[l, s_, j].rearrange("(m p) -> p m", p=128), aprev[l][s_][:, :, j], osem,
                           reads=[aprev[l][s_]], slow=True)
    cx.barrier()
    return nc


_TABS = None


def kernel(**inp):
    global _TABS
    if _TABS is None:
        _TABS = host_tables()
    f = lambda a: np.ascontiguousarray(np.asarray(a, dtype=np.float32))
    nc = build(**_CFG)
    dp_ = _CFG['depth']
    nl4, nle, nlo = dp_, (dp_ + 1) // 2, max(dp_ // 2, 1)
    in_maps = []
    for c in range(8):
        sl = slice(4 * c, 4 * c + 4)
        m = {
            "xp": f(inp["x_prompt"][c % 2]), "xs": f(inp["x_sample"][sl]),
            "st_ret": f(inp["state_ret"][:, sl]),
            "st_k": f(np.asarray(inp["cache_swa_k"])[:, sl].reshape(2, 4, 128, 128)),
            "st_v": f(np.asarray(inp["cache_swa_v"])[:, sl].reshape(2, 4, 128, 128)),
            "st_sre": f(inp["state_ssm_re"][:, sl]), "st_sim": f(inp["state_ssm_im"][:, sl]),
            "st_conv": f(inp["state_ffn_conv"][:, sl]),
            "g_mix_pre": f(inp["norm_mix_pre"]), "g_mix_post": f(inp["norm_mix_post"]),
            "g_ffn_pre": f(inp["norm_ffn_pre"]), "g_ffn_post": f(inp["norm_ffn_post"]),
            "w_in": f(inp["w_in_even"][:nle]), "w_out": f(inp["w_out_even"][:nle]), "sinks": f(inp["swa_sinks"]),
            "lam_re": f(inp["ssm_lam_re"]), "lam_im": f(inp["ssm_lam_im"]), "log_step": f(inp["ssm_log_step"]),
            "b_re": f(inp["ssm_b_re"]), "b_im": f(inp["ssm_b_im"]), "c_re": f(inp["ssm_c_re"]), "c_im": f(inp["ssm_c_im"]),
            "ssm_d": f(inp["ssm_d"]), "w_glu": f(inp["w_glu"][:nlo]),
            "w_a": f(inp["ffn_w_a"][:nl4]), "w_g": f(inp["ffn_w_g"][:nl4]), "conv_w": f(inp["ffn_conv_w"]),
            "conv_b": f(inp["ffn_conv_b"]), "w_down": f(inp["ffn_w_down"][:nl4]),
            "t_intraT": _TABS["intraT"], "t_readB": _TABS["readB"], "t_writeT": _TABS["writeT"],
            "t_negdist": _TABS["negdist"], "t_ident": _TABS["ident"],
        }
        in_maps.append(m)
    res = run_bass_kernel_spmd(nc, in_maps, core_ids=list(range(8)))
    r = res.results
    st = lambda name, cores: np.stack([r[c][name] for c in cores], axis=1)
    cat = lambda name: np.concatenate([r[c][name] for c in range(8)], axis=1 if r[0][name].ndim > 3 or name != "ys" else 0)
    y_prompt = np.stack([r[0]["yp"], r[1]["yp"]], 0)
    y_sample = np.concatenate([r[c]["ys"] for c in range(8)], 0)
    ret_p = st("o_ret_p", [0, 1])
    ret_s = np.concatenate([r[c]["o_ret_s"] for c in range(8)], 1)
    k_p = st("o_k_p", [0, 1]).reshape(2, 2, 128, 2, 64)
    k_s = np.concatenate([r[c]["o_k_s"] for c in range(8)], 1).reshape(2, 32, 128, 2, 64)
    v_p = st("o_v_p", [0, 1]).reshape(2, 2, 128, 2, 64)
    v_s = np.concatenate([r[c]["o_v_s"] for c in range(8)], 1).reshape(2, 32, 128, 2, 64)
    sre_p = st("o_sre_p", [0, 1])
    sre_s = np.concatenate([r[c]["o_sre_s"] for c in range(8)], 1)
    sim_p = st("o_sim_p", [0, 1])
    sim_s = np.concatenate([r[c]["o_sim_s"] for c in range(8)], 1)
    conv_p = st("o_conv_p", [0, 1])
    conv_s = np.concatenate([r[c]["o_conv_s"] for c in range(8)], 1)
    outs = (y_prompt, y_sample, ret_p, ret_s, k_p, k_s, v_p, v_s, sre_p, sre_s, sim_p, sim_s, conv_p, conv_s)
    return tuple(np.ascontiguousarray(o, dtype=np.float32) for o in outs)
```

```python
import math
import numpy as np
import concourse.bass as bass
import concourse.mybir as mybir
from concourse.bass_utils import run_bass_kernel_spmd

F32 = mybir.dt.float32
BF16 = mybir.dt.bfloat16
AF = mybir.ActivationFunctionType
ALU = mybir.AluOpType
AX = mybir.AxisListType

D = 2048
KT = 16
DEPTH = 4
DFF = 5632
MFF = 44
EVEN_IN = 5376
TP = 128
SEQ = 4096
NPT = SEQ // TP
EPS = 1e-6
WSLOT = 5632
NSLOT = 3
ARENA = 86 * 1024


class Buf:
    def __init__(self, name, t):
        self.name = name
        self.t = t
        self.last_w = None
        self.readers = {}

    def __getitem__(self, k):
        return self.t[k]


class DSem:
    def __init__(self, ctx, name):
        self.key = name
        ctx.sems[name] = ctx.nc.alloc_semaphore(name)
        self.count = 0
        ctx.dsems.append(self)


class Ctx:
    def __init__(self, nc):
        self.nc = nc
        self.engs = {"pe": nc.tensor, "act": nc.scalar, "dve": nc.vector, "pool": nc.gpsimd, "sp": nc.sync}
        self.sems = {}
        self.ecnt = {}
        for k in ("pe", "act", "dve", "pool"):
            self.sems[k] = nc.alloc_semaphore("e_" + k)
            self.ecnt[k] = 0
        self.known = {k: {} for k in self.engs}
        self.nbuf = 0
        self.dsems = []
        self.ar_base = None
        self.ar_off = 0
        self.ar_size = 0

    def init_arena(self, nbytes):
        base = (self.nc.sbuf_base + 31) // 32 * 32
        self.nc.alloc_sbuf_tensor("arena", [128, nbytes // 4], F32)
        self.ar_base = base
        self.ar_size = nbytes
        self.ar_off = 0

    def ar(self, name, shape, dt):
        esz = 2 if dt == BF16 else 4
        n = esz
        for d in shape[1:]:
            n *= d
        n = (n + 31) // 32 * 32
        assert self.ar_off + n <= self.ar_size, (name, self.ar_off, n, self.ar_size)
        self.nbuf += 1
        t = self.nc.alloc_sbuf_tensor_at("%s_%d" % (name, self.nbuf), list(shape), dt, offset=self.ar_base + self.ar_off)
        self.ar_off += n
        return Buf(name, t)

    def barrier(self):
        evs = [(k, v) for k, v in self.ecnt.items() if v] + [(d.key, d.count) for d in self.dsems if d.count]
        for e in self.engs:
            self._wait(e, evs)

    def phase(self):
        self.barrier()
        self.ar_off = 0

    def sb(self, name, shape, dt):
        self.nbuf += 1
        return Buf(name, self.nc.alloc_sbuf_tensor("%s_%d" % (name, self.nbuf), list(shape), dt))

    def ps(self, name, shape, dt=F32):
        self.nbuf += 1
        b = Buf(name, self.nc.alloc_psum_tensor("%s_%d" % (name, self.nbuf), list(shape), dt))
        b.is_psum = True
        return b

    def _wait(self, eng, evs):
        need = {}
        for (k, v) in evs:
            if eng == "pe" and k == "pe":
                continue
            if self.known[eng].get(k, 0) >= v:
                continue
            if need.get(k, 0) < v:
                need[k] = v
        for k, v in need.items():
            self.engs[eng].wait_ge(self.sems[k], v)
            self.known[eng][k] = v

    def _deps(self, reads, writes):
        evs = []
        for b in reads:
            if b.last_w is not None:
                evs.append(b.last_w)
            if getattr(b, "is_psum", False):
                evs.extend(b.readers.items())
        for b in writes:
            if b.last_w is not None:
                evs.append(b.last_w)
            evs.extend(b.readers.items())
        return evs

    def _mark(self, ev, reads, writes):
        for b in writes:
            b.last_w = ev
            b.readers = {}
        for b in reads:
            if b in writes:
                continue
            if b.readers.get(ev[0], 0) < ev[1]:
                b.readers[ev[0]] = ev[1]

    def op(self, eng, fn, reads=(), writes=()):
        self._wait(eng, self._deps(reads, writes))
        ins = fn(self.engs[eng])
        self.ecnt[eng] += 1
        ins.then_inc(self.sems[eng], 1)
        self._mark((eng, self.ecnt[eng]), reads, writes)

    def dma(self, q, out, in_, dsem, reads=(), writes=(), slow=False):
        evs = self._deps(reads, writes)
        if dsem.count:
            evs.append((dsem.key, dsem.count))
        self._wait(q, evs)
        kw = {"allow_slow_non_contiguous": True} if slow else {}
        ins = self.engs[q].dma_start(out=out, in_=in_, **kw)
        dsem.count += 16
        ins.then_inc(self.sems[dsem.key], 16)
        self._mark((dsem.key, dsem.count), reads, writes)

    def finish(self):
        allv = {}
        for k in ("pe", "act", "dve", "pool"):
            if self.ecnt[k]:
                allv[k] = self.ecnt[k]
        for k, s in self.sems.items():
            if k not in self.ecnt:
                allv[k] = self._dcount[k].count if k in self._dcount else 0
        self._wait("sp", [(k, v) for k, v in allv.items() if v])


def host_tables():
    tabs = {}
    lg = np.log(1.0 - 2.0 ** (-5.0 - np.arange(8, dtype=np.float64)))
    j = np.arange(128)[:, None]
    i = np.arange(128)[None, :]
    intraT = np.zeros((128, 8, 128), np.float32)
    for h in range(8):
        intraT[:, h, :] = np.where(i >= j, np.exp(lg[h] * np.maximum(i - j, 0)), 0.0) * (128 ** -0.5)
    tabs["intraT"] = intraT
    readB = np.zeros((128, 8, 128), np.float32)
    for h in range(8):
        readB[:, h, :] = np.exp(lg[h] * (np.arange(128) + 1.0))[None, :]
    tabs["readB"] = readB
    wr = np.zeros((128, 16), np.float32)
    for h in range(8):
        wr[:, h] = np.exp(lg[h] * (127.0 - np.arange(128))) * (128 ** -0.5)
        wr[:4, 8 + h] = np.exp(lg[h] * (3.0 - np.arange(4))) * (128 ** -0.5)
    tabs["writeT"] = wr
    q = np.arange(128)[:, None]
    k = np.arange(256)[None, :]
    dist = 128 + q - k
    valid = (dist >= 0) & (dist <= 128)
    nd = np.where(valid, -dist.astype(np.float32), -1.0e7).astype(np.float32)
    nd2 = nd.copy()
    nd2[:, :128] = -1.0e7
    tabs["negdist"] = np.concatenate([nd, nd2], axis=1).astype(np.float32)
    tabs["ident"] = np.eye(128, dtype=np.float32)
    return tabs


RET_DECAY = [float(1.0 - 2.0 ** (-5.0 - h)) for h in range(8)]
SLOPES = [float(2.0 ** (-8.0 * (h + 1) / 16)) for h in range(16)]


_CFG = dict(ntiles=NPT, depth=DEPTH, do_sample=True)
_DBG = dict(cut=99, nblk=21)


def build(ntiles=NPT, depth=DEPTH, do_sample=True):
    nc = bass.Bass("TRN2", target_bir_lowering=False)
    cx = Ctx(nc)
    nl4, nle, nlo = depth, (depth + 1) // 2, max(depth // 2, 1)

    def din(name, shape):
        return nc.dram_tensor(name, list(shape), F32, kind="ExternalInput").ap()

    def dout(name, shape):
        return nc.dram_tensor(name, list(shape), F32, kind="ExternalOutput").ap()

    xp = din("xp", [SEQ, D])
    xs = din("xs", [4, 4, D])
    st_ret = din("st_ret", [2, 4, 8, 128, 128])
    st_k = din("st_k", [2, 4, 128, 128])
    st_v = din("st_v", [2, 4, 128, 128])
    st_sre = din("st_sre", [2, 4, 128, 64])
    st_sim = din("st_sim", [2, 4, 128, 64])
    st_conv = din("st_conv", [4, 4, 2, DFF])
    g_mix_pre = din("g_mix_pre", [4, D])
    g_mix_post = din("g_mix_post", [4, D])
    g_ffn_pre = din("g_ffn_pre", [4, D])
    g_ffn_post = din("g_ffn_post", [4, D])
    w_in = din("w_in", [nle, D, EVEN_IN])
    w_out = din("w_out", [nle, D, D])
    sinks = din("sinks", [2, 16])
    lam_re = din("lam_re", [2, 128, 64])
    lam_im = din("lam_im", [2, 128, 64])
    log_step = din("log_step", [2, 128])
    b_re = din("b_re", [2, 128, 64, 16])
    b_im = din("b_im", [2, 128, 64, 16])
    c_re = din("c_re", [2, 128, 16, 64])
    c_im = din("c_im", [2, 128, 16, 64])
    ssm_d = din("ssm_d", [2, D])
    w_glu = din("w_glu", [nlo, D, 2 * D])
    w_a = din("w_a", [nl4, D, DFF])
    w_g = din("w_g", [nl4, D, DFF])
    conv_w = din("conv_w", [4, 3, DFF])
    conv_b = din("conv_b", [4, DFF])
    w_down = din("w_down", [nl4, DFF, D])
    t_intraT = din("t_intraT", [128, 8, 128])
    t_readB = din("t_readB", [128, 8, 128])
    t_writeT = din("t_writeT", [128, 16])
    t_negdist = din("t_negdist", [128, 512])
    t_ident = din("t_ident", [128, 128])

    yp = dout("yp", [SEQ, D])
    ys = dout("ys", [4, 4, D])
    o_ret_p = dout("o_ret_p", [2, 8, 128, 128])
    o_ret_s = dout("o_ret_s", [2, 4, 8, 128, 128])
    o_k_p = dout("o_k_p", [2, 128, 128])
    o_k_s = dout("o_k_s", [2, 4, 128, 128])
    o_v_p = dout("o_v_p", [2, 128, 128])
    o_v_s = dout("o_v_s", [2, 4, 128, 128])
    o_sre_p = dout("o_sre_p", [2, 128, 64])
    o_sre_s = dout("o_sre_s", [2, 4, 128, 64])
    o_sim_p = dout("o_sim_p", [2, 128, 64])
    o_sim_s = dout("o_sim_s", [2, 4, 128, 64])
    o_conv_p = dout("o_conv_p", [4, 2, DFF])
    o_conv_s = dout("o_conv_s", [4, 4, 2, DFF])

    xT = cx.sb("xT", [128, KT, TP], F32)
    hT = cx.sb("hT", [128, KT, TP], BF16)
    mixT = cx.sb("mixT", [128, KT, TP], F32)
    wsl = [cx.sb("wsl%d" % i, [128, WSLOT], BF16) for i in range(NSLOT)]
    wsem = [DSem(cx, "wsem%d" % i) for i in range(NSLOT)]
    wstate = {"i": 0}
    misc = DSem(cx, "misc")
    osem = DSem(cx, "osem")
    gains = cx.sb("gains", [128, 4, 4, KT], F32)
    cw = cx.sb("cw", [128, 4, 3, MFF], F32)
    cb = cx.sb("cb", [128, 4, MFF], F32)
    dsk = cx.sb("dsk", [128, 2, KT], F32)
    sinkB = cx.sb("sinkB", [128, 32], F32)
    intraT = cx.sb("intraT", [128, 8, 128], F32)
    readB = cx.sb("readB", [128, 8, 128], F32)
    writeT = cx.sb("writeT", [128, 16], F32)
    negdist = cx.sb("negdist", [128, 512], F32)
    ident32 = cx.sb("ident32", [128, 128], F32)
    ident = cx.sb("ident", [128, 128], BF16)
    onesD = cx.sb("onesD", [128, 128], F32)
    ones128 = cx.sb("ones128", [128, 128], F32)

    def ld(dst, dst_ap, src_ap, slow=False):
        cx.dma("sp", dst_ap, src_ap, misc, writes=[dst], slow=slow)

    for li in range(4):
        for ni, g in enumerate((g_mix_pre, g_mix_post, g_ffn_pre, g_ffn_post)):
            ld(gains, gains[:, ni, li, :], g[li].rearrange("(k p) -> p k", p=128), slow=True)
        for j in range(3):
            ld(cw, cw[:, li, j, :], conv_w[li, j].rearrange("(m p) -> p m", p=128), slow=True)
        ld(cb, cb[:, li, :], conv_b[li].rearrange("(m p) -> p m", p=128), slow=True)
    for i in range(2):
        ld(dsk, dsk[:, i, :], ssm_d[i].rearrange("(k p) -> p k", p=128), slow=True)
    ld(sinkB, sinkB[:, :], sinks.rearrange("a b -> (a b)").partition_broadcast(128), slow=True)
    ld(intraT, intraT[:], t_intraT)
    ld(readB, readB[:], t_readB)
    ld(writeT, writeT[:], t_writeT)
    ld(negdist, negdist[:], t_negdist)
    ld(ident32, ident32[:], t_ident)
    cx.op("dve", lambda e: e.tensor_copy(out=ident[:], in_=ident32[:]), reads=[ident32], writes=[ident])
    cx.op("dve", lambda e: e.memset(onesD[:], 1.0 / D), writes=[onesD])
    cx.op("dve", lambda e: e.memset(ones128[:], 1.0 / 128), writes=[ones128])

    Sp = [cx.sb("Sp", [128, 8, 128], F32) for i in range(2)]
    Sb = cx.sb("Sb", [128, 8, 128], BF16)
    kTe = [cx.sb("kTe", [64, 2, 128 + TP], BF16) for i in range(2)]
    vsw = [cx.sb("vsw", [128, 1 + TP // 128, 128], BF16) for i in range(2)]
    H3p = [cx.sb("H3p", [64, 3, 128], F32) for i in range(2)]
    A2 = [cx.sb("A2", [64, 2, 128], F32) for i in range(2)]
    B2 = [cx.sb("B2", [64, 2, 128], F32) for i in range(2)]
    C64 = [cx.sb("C64", [64, 128, 2, 16], BF16) for i in range(2)]
    gd = cx.sb("gd", [128, 2, KT], F32)
    negpi = cx.sb("negpi", [128, 1], F32)
    xin = cx.sb("xin", [128, D], F32)
    usem = [DSem(cx, "usem%d" % j) for j in range(8)]
    ysem = [DSem(cx, "ysem%d" % j) for j in range(8)]
    B16d = [nc.dram_tensor("B16d%d" % i, [16, 128 * 128], BF16) for i in range(2)]
    aprev = [[cx.sb("aprev", [128, MFF, 2], F32) for s in range(4)] for l in range(4)]

    PB = [cx.ps("pb%d" % i, [128, 512], F32) for i in range(8)]

    def scr(name, nblk, P, n):
        return Buf(name, nc.dram_tensor(name, [nblk, P, n], BF16))
    sc_ag = [scr("sc_ag%d" % l, MFF, 128, 4096) for l in range(depth)]
    sc_dn = [scr("sc_dn%d" % l, KT, 128, MFF * 128) for l in range(depth)]
    sc_in = [scr("sc_in%d" % i, 21, 128, 4096) for i in range(nle)]
    sc_or = [scr("sc_or%d" % i, KT, 128, 1024) for i in range(nle)]
    sc_os = [scr("sc_os%d" % i, KT, 64, 2048) for i in range(nle)]
    sc_gl = [scr("sc_gl%d" % i, KT, 128, 4096) for i in range(depth // 2)]
    ssem = [DSem(cx, "ssem%d" % i) for i in range(NSLOT)]
    lsem = [DSem(cx, "lsem%d" % i) for i in range(NSLOT)]

    def convert(scb, blk, P, parts):
        i = wstate["i"] % NSLOT
        wstate["i"] += 1
        b = wsl[i]
        off = 0
        for (src_ap, nk, ncols) in parts:
            view = b.t[0:P, off:off + nk * ncols].rearrange("p (k n) -> p k n", n=ncols)
            cx.dma("pool", view, src_ap.rearrange("(k p) n -> p k n", p=P), wsem[i], writes=[b])
            off += nk * ncols
        cx.dma("sp", scb.t.ap()[blk, 0:P, 0:off], b.t[0:P, 0:off], ssem[i], reads=[b], writes=[scb])

    def weight_prologue():
        for l in range(depth):
            if l % 2 == 0:
                i = l // 2
                for blk in range(21):
                    convert(sc_in[i], blk, 128, [(w_in[i][:, blk * 256:(blk + 1) * 256], KT, 256)])
                for dm in range(KT):
                    convert(sc_or[i], dm, 128, [(w_out[i][0:1024, dm * 128:(dm + 1) * 128], 8, 128)])
                    convert(sc_os[i], dm, 64, [(w_out[i][1024:2048, dm * 128:(dm + 1) * 128], 16, 128)])
            else:
                i = l // 2
                for dm in range(KT):
                    convert(sc_gl[i], dm, 128, [(w_glu[i][:, dm * 128:(dm + 1) * 128], KT, 128),
                                                (w_glu[i][:, D + dm * 128:D + (dm + 1) * 128], KT, 128)])
            for m in range(MFF):
                convert(sc_ag[l], m, 128, [(w_a[l][:, m * 128:(m + 1) * 128], KT, 128), (w_g[l][:, m * 128:(m + 1) * 128], KT, 128)])
            for dm in range(KT):
                convert(sc_dn[l], dm, 128, [(w_down[l][:, dm * 128:(dm + 1) * 128], MFF, 128)])

    def wget(scb, blk, P, n):
        i = wstate["i"] % NSLOT
        wstate["i"] += 1
        b = wsl[i]
        cx.dma("sp", b.t[0:P, 0:n], scb.t.ap()[blk, 0:P, 0:n], lsem[i], reads=[scb], writes=[b])
        return b

    tmpn = [cx.sb("tmpn", [128, TP], F32) for _ in range(2)]
    ssq = cx.sb("ssq", [128, TP], F32)
    rstd = cx.sb("rstd", [128, TP], F32)

    def rms_stats(src, T, nk, ones, pbank):
        for k in range(nk):
            tb = tmpn[k % 2]
            cx.op("act", lambda e, k=k, tb=tb: e.activation(out=tb[:, 0:T], in_=src[:, k, 0:T], func=AF.Square),
                  reads=[src], writes=[tb])
            if k == 0:
                cx.op("dve", lambda e, tb=tb: e.tensor_copy(out=ssq[:, 0:T], in_=tb[:, 0:T]), reads=[tb], writes=[ssq])
            else:
                cx.op("dve", lambda e, tb=tb: e.tensor_add(out=ssq[:, 0:T], in0=ssq[:, 0:T], in1=tb[:, 0:T]),
                      reads=[tb, ssq], writes=[ssq])
        cx.op("pe", lambda e: e.matmul(pbank[:, 0:T], ones[:, :], ssq[:, 0:T], start=True, stop=True),
              reads=[ones, ssq], writes=[pbank])
        cx.op("act", lambda e: e.activation(out=rstd[:, 0:T], in_=pbank[:, 0:T], func=AF.Sqrt, bias=EPS_AP[:, 0:1]),
              reads=[pbank, EPS_B], writes=[rstd])
        cx.op("dve", lambda e: e.reciprocal(out=rstd[:, 0:T], in_=rstd[:, 0:T]), reads=[rstd], writes=[rstd])

    EPS_B = cx.sb("epsb", [128, 1], F32)
    EPS_AP = EPS_B
    cx.op("dve", lambda e: e.memset(EPS_B[:], EPS), writes=[EPS_B])

    def prenorm(ni, li, T):
        rms_stats(xT, T, KT, onesD, PB[7])
        for k in range(KT):
            cx.op("dve", lambda e, k=k: e.scalar_tensor_tensor(
                out=hT[:, k, 0:T], in0=xT[:, k, 0:T], scalar=gains[:, ni, li, k:k + 1], in1=rstd[:, 0:T],
                op0=ALU.mult, op1=ALU.mult), reads=[xT, gains, rstd], writes=[hT])

    def postnorm_add(ni, li, T):
        rms_stats(mixT, T, KT, onesD, PB[7])
        for k in range(KT):
            tb = tmpn[k % 2]
            cx.op("dve", lambda e, k=k, tb=tb: e.scalar_tensor_tensor(
                out=tb[:, 0:T], in0=mixT[:, k, 0:T], scalar=gains[:, ni, li, k:k + 1], in1=rstd[:, 0:T],
                op0=ALU.mult, op1=ALU.mult), reads=[mixT, gains, rstd], writes=[tb])
            cx.op("dve", lambda e, k=k, tb=tb: e.tensor_add(out=xT[:, k, 0:T], in0=xT[:, k, 0:T], in1=tb[:, 0:T]),
                  reads=[tb, xT], writes=[xT])

    def mm_fm(pb, wview, c0, M, rhs_buf, rhs_fn, nk, T, extra_reads=()):
        def fn(e):
            ins = None
            for k in range(nk):
                ins = e.matmul(pb[0:M, 0:T], wview[0][:, k, c0:c0 + M], rhs_fn(k), start=(k == 0), stop=(k == nk - 1))
            return ins
        cx.op("pe", fn, reads=[wview[1], rhs_buf] + list(extra_reads), writes=[pb])

    def ffn(li, seqs, T):
        cx.phase()
        act = cx.ar("act", [128, MFF, TP], BF16)
        aext = [cx.ar("aext", [128, TP + 8], F32) for _ in range(2)]
        cacc = [cx.ar("cacc", [128, TP + 8], F32) for _ in range(2)]
        prenorm(2, li, T)
        TE = T + 2 * len(seqs)
        for m in range(MFF):
            bA = bG = wget(sc_ag[li], m, 128, 4096)
            vAG = bA.t[:, 0:4096].rearrange("p (j k n) -> p j k n", j=2, n=128)
            vA, vG = vAG[:, 0], vAG[:, 1]
            pa = PB[(2 * m) % 6]
            pg = PB[(2 * m + 1) % 6]
            mm_fm(pa, (vA, bA), 0, 128, hT, lambda k: hT[:, k, 0:T], KT, T)
            mm_fm(pg, (vG, bG), 0, 128, hT, lambda k: hT[:, k, 0:T], KT, T)
            ae = aext[m % 2]
            ca = cacc[m % 2]
            for si, (slot, c0, L) in enumerate(seqs):
                e0 = c0 + 2 * si
                cx.op("act", lambda e, e0=e0, slot=slot: e.copy(out=ae[:, e0:e0 + 2], in_=aprev[li][slot][:, m, :]),
                      reads=[aprev[li][slot]], writes=[ae])
                cx.op("act", lambda e, e0=e0, c0=c0, L=L: e.copy(out=ae[:, e0 + 2:e0 + 2 + L], in_=pa[:, c0:c0 + L]),
                      reads=[pa], writes=[ae])
            cx.op("dve", lambda e: e.tensor_scalar(out=ca[:, 2:TE], in0=ae[:, 2:TE], scalar1=cw[:, li, 2, m:m + 1],
                                                   scalar2=cb[:, li, m:m + 1], op0=ALU.mult, op1=ALU.add),
                  reads=[ae, cw, cb], writes=[ca])
            cx.op("dve", lambda e: e.scalar_tensor_tensor(out=ca[:, 2:TE], in0=ae[:, 1:TE - 1], scalar=cw[:, li, 1, m:m + 1],
                                                          in1=ca[:, 2:TE], op0=ALU.mult, op1=ALU.add),
                  reads=[ae, cw, ca], writes=[ca])
            cx.op("dve", lambda e: e.scalar_tensor_tensor(out=ca[:, 2:TE], in0=ae[:, 0:TE - 2], scalar=cw[:, li, 0, m:m + 1],
                                                          in1=ca[:, 2:TE], op0=ALU.mult, op1=ALU.add),
                  reads=[ae, cw, ca], writes=[ca])
            cx.op("act", lambda e: e.activation(out=ca[:, 2:TE], in_=ca[:, 2:TE], func=AF.Gelu), reads=[ca], writes=[ca])
            for si, (slot, c0, L) in enumerate(seqs):
                e0 = c0 + 2 * si
                cx.op("dve", lambda e, e0=e0, c0=c0, L=L: e.tensor_tensor(
                    out=act[:, m, c0:c0 + L], in0=ca[:, e0 + 2:e0 + 2 + L], in1=pg[:, c0:c0 + L], op=ALU.mult),
                    reads=[ca, pg], writes=[act])
                cx.op("act", lambda e, e0=e0, L=L, slot=slot: e.copy(out=aprev[li][slot][:, m, :], in_=ae[:, e0 + L:e0 + L + 2]),
                      reads=[ae], writes=[aprev[li][slot]])
        for dm in range(KT):
            bW, vW = wload_down(li, dm)
            pb = PB[dm % 6]
            mm_fm(pb, (vW, bW), 0, 128, act, lambda k: act[:, k, 0:T], MFF, T)
            cx.op("act", lambda e, pb=pb, dm=dm: e.copy(out=mixT[:, dm, 0:T], in_=pb[:, 0:T]), reads=[pb], writes=[mixT])
        postnorm_add(3, li, T)

    def wload_down(li, dm):
        b = wget(sc_dn[li], dm, 128, MFF * 128)
        return b, b.t[:, 0:MFF * 128].rearrange("p (k n) -> p k n", n=128)

    def even_mixer(i, li, seqs, T, prompt, first, last):
        cx.phase()
        rqT = cx.ar("rqT", [128, 8, TP], BF16)
        rkT = cx.ar("rkT", [128, 8, TP], BF16)
        rgT = cx.ar("rgT", [128, 8, TP], BF16)
        ktok = cx.ar("ktok", [128, 4, 1024], BF16)
        vtok = cx.ar("vtok", [128, 4, 1024], BF16)
        aqT = cx.ar("aqT", [64, 16, TP], BF16)
        kvtok32 = cx.ar("kvtok32", [128, 4, 256], F32)
        vcur = cx.ar("vcur", [128, 4, 128], BF16)
        oret = cx.ar("oret", [128, 8, TP], F32)
        oTret = cx.ar("oTret", [128, 8, TP], BF16)
        oTswa = cx.ar("oTswa", [64, 16, TP], BF16)
        scT = cx.ar("scT", [128, 128], BF16)
        qr = cx.ar("qr", [128, 128], BF16)
        sb32 = cx.ar("sb32", [128, 256], F32)
        p32 = cx.ar("p32", [128, 256], F32)
        pn = cx.ar("pn", [128, 256], BF16)
        pT = cx.ar("pT", [128, 256], BF16)
        sm = cx.ar("sm", [128, 8], F32)
        if prompt:
            Ssl = [Sp[i]]
        else:
            Ssl = [cx.ar("Ss", [128, 8, 128], F32) for _ in range(4)]
            kTs = cx.ar("kTs", [64, 2, 4 * 132], BF16)
            kst32 = cx.ar("kst32", [64, 2, 128], F32)
            vst32 = cx.ar("vst32", [128, 128], F32)
            vprev_s = [cx.ar("vprev_s", [128, 128], BF16) for _ in range(4)]
            cx.op("dve", lambda e: e.memset(kTs[:], 0.0), writes=[kTs])
            for s_ in range(4):
                cx.dma("sp", Ssl[s_][:], st_ret[i, s_].rearrange("h d e -> d h e"), misc, writes=[Ssl[s_]])
                for kvh in range(2):
                    cx.dma("sp", kst32[:, kvh, :], st_k[i, s_][:, kvh * 64:(kvh + 1) * 64].rearrange("t d -> d t"), misc,
                           writes=[kst32], slow=True)
                cx.op("act", lambda e, s_=s_: e.copy(out=kTs[:, :, s_ * 132:s_ * 132 + 128], in_=kst32[:]), reads=[kst32], writes=[kTs])
                cx.dma("sp", vst32[:], st_v[i, s_], misc, writes=[vst32])
                cx.op("act", lambda e, s_=s_: e.copy(out=vprev_s[s_][:], in_=vst32[:]), reads=[vst32], writes=[vprev_s[s_]])
                cx.dma("sp", o_v_s[i, s_, 0:124, :], vst32[4:128, :], osem, reads=[vst32])
                cx.dma("sp", vst32[:], st_k[i, s_], misc, writes=[vst32])
                cx.dma("sp", o_k_s[i, s_, 0:124, :], vst32[4:128, :], osem, reads=[vst32])
        cx.op("dve", lambda e: e.memset(vcur[:], 0.0), writes=[vcur])
        cx.op("dve", lambda e: e.memset(ktok[:], 0.0), writes=[ktok])
        cx.op("dve", lambda e: e.memset(vtok[:], 0.0), writes=[vtok])

        def keyview(li_, prompt_, si, L):
            if prompt_:
                return kTe[li_], kTe[li_].t[:, :, 0:128 + L]
            return kTs, kTs.t[:, :, si * 132:si * 132 + 128 + L]

        prenorm(0, li, T)
        wcol = 0 if prompt else 8

        def cut_here():
            cx.op("dve", lambda e: e.memset(mixT[:], 0.0), writes=[mixT])
            postnorm_add(1, li, T)
        if _DBG["cut"] <= 0:
            return cut_here()
        for blk in range(min(21, _DBG['nblk'])):
            c = blk * 256
            bW = wget(sc_in[i], blk, 128, 4096)
            vW = bW.t[:, 0:4096].rearrange("p (k n) -> p k n", n=256)
            if c < 1024 or 1024 <= c < 2048 or 3072 <= c < 4096:
                for hh in range(2):
                    h = (c % 1024) // 128 + hh
                    pb = PB[(2 * blk + hh) % 4]
                    mm_fm(pb, (vW, bW), hh * 128, 128, hT, lambda k: hT[:, k, 0:T], KT, T)
                    if c < 1024:
                        cx.op("act", lambda e, pb=pb, h=h: e.copy(out=rqT[:, h, 0:T], in_=pb[:, 0:T]), reads=[pb], writes=[rqT])
                    elif c < 2048:
                        cx.op("act", lambda e, pb=pb, h=h: e.copy(out=rkT[:, h, 0:T], in_=pb[:, 0:T]), reads=[pb], writes=[rkT])
                    else:
                        cx.op("act", lambda e, pb=pb, h=h: e.activation(out=rgT[:, h, 0:T], in_=pb[:, 0:T], func=AF.Silu),
                              reads=[pb], writes=[rgT])
            if 1024 <= c < 3072 or c >= 5120:
                for si, (slot, c0, L) in enumerate(seqs):
                    pb = PB[4 + (si % 2)]

                    def fn(e, pb=pb, c0=c0, L=L):
                        ins = None
                        for k in range(KT):
                            ins = e.matmul(pb[0:L, 0:256], hT[:, k, c0:c0 + L], vW[:, k, 0:256], start=(k == 0), stop=(k == KT - 1))
                        return ins
                    cx.op("pe", fn, reads=[bW, hT], writes=[pb])
                    if c < 2048:
                        off = c - 1024
                        for hh in range(2):
                            h = off // 128 + hh
                            cx.op("dve", lambda e, pb=pb, L=L, si=si, off=off, hh=hh, h=h: e.tensor_scalar(
                                out=ktok[0:L, si, off + hh * 128:off + (hh + 1) * 128], in0=pb[0:L, hh * 128:(hh + 1) * 128],
                                scalar1=writeT[0:L, wcol + h:wcol + h + 1], scalar2=None, op0=ALU.mult),
                                reads=[pb, writeT], writes=[ktok])
                    elif c < 3072:
                        off = c - 2048
                        cx.op("act", lambda e, pb=pb, L=L, si=si, off=off: e.copy(out=vtok[0:L, si, off:off + 256], in_=pb[0:L, 0:256]),
                              reads=[pb], writes=[vtok])
                    else:
                        cx.op("act", lambda e, pb=pb, L=L, si=si: e.copy(out=kvtok32[0:L, si, :], in_=pb[0:L, 0:256]),
                              reads=[pb], writes=[kvtok32])
                        cx.op("dve", lambda e, L=L, si=si: e.tensor_copy(out=vcur[0:L, si, :], in_=kvtok32[0:L, si, 128:256]),
                              reads=[kvtok32], writes=[vcur])
            if 4096 <= c < 5120:
                for hh in range(4):
                    hq = (c - 4096) // 64 + hh
                    pb = PB[(blk + hh) % 4]
                    mm_fm(pb, (vW, bW), hh * 64, 64, hT, lambda k: hT[:, k, 0:T], KT, T)
                    cx.op("act", lambda e, pb=pb, hq=hq: e.mul(out=aqT[0:64, hq, 0:T], in_=pb[0:64, 0:T], mul=0.125),
                          reads=[pb], writes=[aqT])
            if c >= 5120:
                for kvh in range(2):
                    pb = PB[kvh]
                    mm_fm(pb, (vW, bW), kvh * 64, 64, hT, lambda k: hT[:, k, 0:T], KT, T)
                    for si, (slot, c0, L) in enumerate(seqs):
                        kb, kv = keyview(i, prompt, si, L)
                        cx.op("act", lambda e, pb=pb, kv=kv, kvh=kvh, c0=c0, L=L: e.copy(out=kv[0:64, kvh, 128:128 + L], in_=pb[0:64, c0:c0 + L]),
                              reads=[pb], writes=[kb])
        if _DBG["cut"] <= 1:
            return cut_here()
        for si, (slot, c0, L) in enumerate(seqs):
            Sst = Ssl[si]
            cx.op("act", lambda e, Sst=Sst: e.copy(out=Sb[:], in_=Sst[:]), reads=[Sst], writes=[Sb])
            for h in range(8):
                p0, p1, p2 = PB[0 + (h % 2) * 3], PB[1 + (h % 2) * 3], PB[2 + (h % 2) * 3]
                cx.op("pe", lambda e, p0=p0, h=h, c0=c0, L=L: e.matmul(p0[0:L, 0:L], rkT[:, h, c0:c0 + L], rqT[:, h, c0:c0 + L], start=True, stop=True),
                      reads=[rkT, rqT], writes=[p0])
                cx.op("dve", lambda e, p0=p0, h=h, L=L: e.tensor_tensor(out=scT[0:L, 0:L], in0=p0[0:L, 0:L], in1=intraT[0:L, h, 0:L], op=ALU.mult),
                      reads=[p0, intraT], writes=[scT])
                cx.op("dve", lambda e, h=h, c0=c0, L=L: e.tensor_tensor(out=qr[:, 0:L], in0=rqT[:, h, c0:c0 + L], in1=readB[:, h, 0:L], op=ALU.mult),
                      reads=[rqT, readB], writes=[qr])

                def fo(e, p1=p1, h=h, si=si, L=L):
                    e.matmul(p1[:, 0:L], vtok[0:L, si, h * 128:(h + 1) * 128], scT[0:L, 0:L], start=True, stop=False)
                    return e.matmul(p1[:, 0:L], Sb[:, h, :], qr[:, 0:L], start=False, stop=True)
                cx.op("pe", fo, reads=[vtok, scT, Sb, qr], writes=[p1])
                cx.op("act", lambda e, p1=p1, h=h, c0=c0, L=L: e.copy(out=oret[:, h, c0:c0 + L], in_=p1[:, 0:L]), reads=[p1], writes=[oret])
                cx.op("pe", lambda e, p2=p2, h=h, si=si, L=L: e.matmul(p2[:, 0:128], ktok[0:L, si, h * 128:(h + 1) * 128],
                                                                     vtok[0:L, si, h * 128:(h + 1) * 128], start=True, stop=True),
                      reads=[ktok, vtok], writes=[p2])
                cx.op("dve", lambda e, p2=p2, h=h, Sst=Sst, L=L: e.scalar_tensor_tensor(
                    out=Sst[:, h, :], in0=Sst[:, h, :], scalar=float(RET_DECAY[h] ** L), in1=p2[:, 0:128], op0=ALU.mult, op1=ALU.add),
                    reads=[p2, Sst], writes=[Sst])
        if _DBG["cut"] <= 2:
            return cut_here()
        for h in range(8):
            tb = tmpn[h % 2]
            cx.op("act", lambda e, tb=tb, h=h: e.activation(out=tb[:, 0:T], in_=oret[:, h, 0:T], func=AF.Square), reads=[oret], writes=[tb])
            cx.op("pe", lambda e, tb=tb: e.matmul(PB[7][:, 0:T], ones128[:, :], tb[:, 0:T], start=True, stop=True),
                  reads=[ones128, tb], writes=[PB[7]])
            cx.op("act", lambda e: e.activation(out=rstd[:, 0:T], in_=PB[7][:, 0:T], func=AF.Sqrt, bias=EPS_B[:, 0:1]),
                  reads=[PB[7], EPS_B], writes=[rstd])
            cx.op("dve", lambda e: e.reciprocal(out=rstd[:, 0:T], in_=rstd[:, 0:T]), reads=[rstd], writes=[rstd])
            cx.op("dve", lambda e, tb=tb, h=h: e.tensor_tensor(out=tb[:, 0:T], in0=oret[:, h, 0:T], in1=rstd[:, 0:T], op=ALU.mult),
                  reads=[oret, rstd], writes=[tb])
            cx.op("dve", lambda e, tb=tb, h=h: e.tensor_tensor(out=oTret[:, h, 0:T], in0=tb[:, 0:T], in1=rgT[:, h, 0:T], op=ALU.mult),
                  reads=[tb, rgT], writes=[oTret])
        if _DBG["cut"] <= 3:
            return cut_here()
        ndoff = 256 if first else 0
        for si, (slot, c0, L) in enumerate(seqs):
            kb, kv = keyview(i, prompt, si, L)
            vpb = vsw[i] if prompt else vprev_s[si]
            vpv = vsw[i].t[:, 0, :] if prompt else vprev_s[si].t[:, :]
            NK = 128 + L
            for hq in range(16):
                kvh = hq // 8
                sc = i * 16 + hq
                pS, pTt, pO = PB[(hq % 2) * 3], PB[(hq % 2) * 3 + 1], PB[(hq % 2) * 3 + 2]
                cx.op("pe", lambda e, pS=pS, hq=hq, kvh=kvh, c0=c0, L=L, kv=kv: e.matmul(
                    pS[0:L, 0:NK], aqT[0:64, hq, c0:c0 + L], kv[0:64, kvh, 0:NK], start=True, stop=True), reads=[aqT, kb], writes=[pS])
                cx.op("dve", lambda e, pS=pS, hq=hq, L=L: e.scalar_tensor_tensor(
                    out=sb32[0:L, 0:NK], in0=negdist[0:L, ndoff:ndoff + NK], scalar=SLOPES[hq], in1=pS[0:L, 0:NK], op0=ALU.mult, op1=ALU.add),
                    reads=[negdist, pS], writes=[sb32])
                cx.op("dve", lambda e, L=L: e.tensor_reduce(out=sm[0:L, 0:1], in_=sb32[0:L, 0:NK], axis=AX.X, op=ALU.max), reads=[sb32], writes=[sm])
                cx.op("dve", lambda e, L=L, sc=sc: e.tensor_scalar(out=sm[0:L, 1:2], in0=sm[0:L, 0:1], scalar1=sinkB[0:L, sc:sc + 1], scalar2=-1.0,
                                                                  op0=ALU.max, op1=ALU.mult), reads=[sm, sinkB], writes=[sm])
                cx.op("act", lambda e, L=L: e.activation(out=p32[0:L, 0:NK], in_=sb32[0:L, 0:NK], func=AF.Exp, bias=sm[0:L, 1:2]),
                      reads=[sb32, sm], writes=[p32])
                cx.op("act", lambda e, L=L, sc=sc: e.activation(out=sm[0:L, 2:3], in_=sinkB[0:L, sc:sc + 1], func=AF.Exp, bias=sm[0:L, 1:2]),
                      reads=[sinkB, sm], writes=[sm])
                cx.op("dve", lambda e, L=L: e.tensor_reduce(out=sm[0:L, 3:4], in_=p32[0:L, 0:NK], axis=AX.X, op=ALU.add), reads=[p32, sm], writes=[sm])
                cx.op("dve", lambda e, L=L: e.tensor_add(out=sm[0:L, 4:5], in0=sm[0:L, 3:4], in1=sm[0:L, 2:3]), reads=[sm], writes=[sm])
                cx.op("dve", lambda e, L=L: e.reciprocal(out=sm[0:L, 5:6], in_=sm[0:L, 4:5]), reads=[sm], writes=[sm])
                cx.op("dve", lambda e, L=L: e.tensor_scalar(out=pn[0:L, 0:NK], in0=p32[0:L, 0:NK], scalar1=sm[0:L, 5:6], scalar2=None, op0=ALU.mult),
                      reads=[p32, sm], writes=[pn])

                def ft(e, pTt=pTt, L=L):
                    e.matmul(pTt[:, 0:L], pn[0:L, 0:128], ident[0:L, 0:L], start=True, stop=True)
                    return e.matmul(pTt[0:L, 128:128 + L], pn[0:L, 128:128 + L], ident[0:L, 0:L], start=True, stop=True)
                cx.op("pe", ft, reads=[pn, ident], writes=[pTt])
                cx.op("act", lambda e, pTt=pTt, L=L: e.copy(out=pT[:, 0:L], in_=pTt[:, 0:L]), reads=[pTt], writes=[pT])
                cx.op("act", lambda e, pTt=pTt, L=L: e.copy(out=pT[0:L, 128:128 + L], in_=pTt[0:L, 128:128 + L]), reads=[pTt], writes=[pT])

                def fv(e, pO=pO, kvh=kvh, si=si, L=L, vpv=vpv):
                    e.matmul(pO[0:64, 0:L], vpv[:, kvh * 64:(kvh + 1) * 64], pT[:, 0:L], start=True, stop=False)
                    return e.matmul(pO[0:64, 0:L], vcur[0:L, si, kvh * 64:(kvh + 1) * 64], pT[0:L, 128:128 + L], start=False, stop=True)
                cx.op("pe", fv, reads=[vpb, vcur, pT], writes=[pO])
                cx.op("act", lambda e, pO=pO, hq=hq, c0=c0, L=L: e.copy(out=oTswa[0:64, hq, c0:c0 + L], in_=pO[0:64, 0:L]), reads=[pO], writes=[oTswa])
        if _DBG["cut"] <= 4:
            return cut_here()
        if prompt:
            cx.op("act", lambda e: e.copy(out=kTe[i][:, :, 0:128], in_=kTe[i][:, :, TP:TP + 128]), reads=[kTe[i]], writes=[kTe[i]])
            cx.op("act", lambda e: e.copy(out=vsw[i][:, 0, :], in_=vcur[:, 0, :]), reads=[vcur], writes=[vsw[i]])
        if prompt and last:
            cx.dma("sp", o_ret_p[i].rearrange("h d e -> d h e"), Sp[i][:], osem, reads=[Sp[i]])
            cx.dma("sp", o_k_p[i], kvtok32[:, 0, 0:128], osem, reads=[kvtok32])
            cx.dma("sp", o_v_p[i], kvtok32[:, 0, 128:256], osem, reads=[kvtok32])
        if not prompt:
            for s_ in range(4):
                cx.dma("sp", o_ret_s[i, s_].rearrange("h d e -> d h e"), Ssl[s_][:], osem, reads=[Ssl[s_]])
                cx.dma("sp", o_k_s[i, s_, 124:128, :], kvtok32[0:4, s_, 0:128], osem, reads=[kvtok32])
                cx.dma("sp", o_v_s[i, s_, 124:128, :], kvtok32[0:4, s_, 128:256], osem, reads=[kvtok32])
        for dm in range(KT):
            bR = wget(sc_or[i], dm, 128, 1024)
            vR = bR.t[:, 0:1024].rearrange("p (k n) -> p k n", n=128)
            bS_ = wget(sc_os[i], dm, 64, 2048)
            vS = bS_.t[0:64, 0:2048].rearrange("p (k n) -> p k n", n=128)
            pb = PB[dm % 6]

            def fw(e, pb=pb, vR=vR, vS=vS):
                ins = None
                for k in range(8):
                    ins = e.matmul(pb[:, 0:T], vR[:, k, :], oTret[:, k, 0:T], start=(k == 0), stop=False)
                for hq in range(16):
                    ins = e.matmul(pb[:, 0:T], vS[0:64, hq, :], oTswa[0:64, hq, 0:T], start=False, stop=(hq == 15))
                return ins
            cx.op("pe", fw, reads=[bR, bS_, oTret, oTswa], writes=[pb])
            cx.op("act", lambda e, pb=pb, dm=dm: e.copy(out=mixT[:, dm, 0:T], in_=pb[:, 0:T]), reads=[pb], writes=[mixT])
        postnorm_add(1, li, T)

    PI = math.pi

    def s5_prologue(i):
        cx.phase()
        lr = cx.ar("lr", [64, 128], F32)
        lim = cx.ar("lim", [64, 128], F32)
        dl = cx.ar("dl", [64, 128], F32)
        ar_ = cx.ar("ar_", [64, 128], F32)
        ai = cx.ar("ai", [64, 128], F32)
        t1 = cx.ar("t1", [64, 128], F32)
        t2 = cx.ar("t2", [64, 128], F32)
        lbre = cx.ar("lbre", [64, 128], F32)
        lbim = cx.ar("lbim", [64, 128], F32)
        cre = cx.ar("cre", [64, 128], F32)
        cim = cx.ar("cim", [64, 128], F32)
        bre = cx.ar("bre", [64, 128, 16], F32)
        bim = cx.ar("bim", [64, 128, 16], F32)
        Bre = cx.ar("Bre", [64, 128, 16], F32)
        Bim = cx.ar("Bim", [64, 128, 16], F32)
        tb = cx.ar("tb", [64, 128, 16], F32)
        B16 = cx.ar("B16", [16, 128, 2, 64], BF16)
        cx.dma("sp", lr[:], lam_re[i].rearrange("g p -> p g"), misc, writes=[lr], slow=True)
        cx.dma("sp", lim[:], lam_im[i].rearrange("g p -> p g"), misc, writes=[lim], slow=True)
        cx.dma("sp", dl[:], log_step[i].partition_broadcast(64), misc, writes=[dl], slow=True)
        for g0 in range(0, 128, 32):
            cx.dma("sp", bre[:, g0:g0 + 32, :], b_re[i][g0:g0 + 32].rearrange("g p h -> p g h"), misc, writes=[bre], slow=True)
            cx.dma("sp", bim[:, g0:g0 + 32, :], b_im[i][g0:g0 + 32].rearrange("g p h -> p g h"), misc, writes=[bim], slow=True)
        cx.op("dve", lambda e: e.memset(negpi[:], -PI), writes=[negpi])
        cx.op("act", lambda e: e.activation(out=dl[:], in_=dl[:], func=AF.Exp), reads=[dl], writes=[dl])
        cx.op("dve", lambda e: e.tensor_tensor(out=ar_[:], in0=lr[:], in1=dl[:], op=ALU.mult), reads=[lr, dl], writes=[ar_])
        cx.op("dve", lambda e: e.tensor_tensor(out=ai[:], in0=lim[:], in1=dl[:], op=ALU.mult), reads=[lim, dl], writes=[ai])
        cx.op("act", lambda e: e.activation(out=ar_[:], in_=ar_[:], func=AF.Exp), reads=[ar_], writes=[ar_])
        ti32 = cx.ar("ti32", [64, 128], mybir.dt.int32)
        tf = cx.ar("tf", [64, 128], F32)
        tm = cx.ar("tm", [64, 128], F32)

        def sin_of(dst, shift):
            cx.op("dve", lambda e: e.tensor_scalar(out=dst[:], in0=ai[:], scalar1=shift, scalar2=None, op0=ALU.add), reads=[ai], writes=[dst])
            cx.op("dve", lambda e: e.tensor_scalar(out=tf[:], in0=dst[:], scalar1=1.0 / (2 * PI), scalar2=None, op0=ALU.mult), reads=[dst], writes=[tf])
            cx.op("dve", lambda e: e.tensor_copy(out=ti32[:], in_=tf[:]), reads=[tf], writes=[ti32])
            cx.op("dve", lambda e: e.tensor_copy(out=tf[:], in_=ti32[:]), reads=[ti32], writes=[tf])
            cx.op("dve", lambda e: e.scalar_tensor_tensor(out=dst[:], in0=tf[:], scalar=-2 * PI, in1=dst[:], op0=ALU.mult, op1=ALU.add),
                  reads=[tf, dst], writes=[dst])
            cx.op("dve", lambda e: e.tensor_scalar(out=tm[:], in0=dst[:], scalar1=PI, scalar2=-2 * PI, op0=ALU.is_gt, op1=ALU.mult), reads=[dst], writes=[tm])
            cx.op("dve", lambda e: e.tensor_add(out=dst[:], in0=dst[:], in1=tm[:]), reads=[dst, tm], writes=[dst])
            cx.op("dve", lambda e: e.tensor_scalar(out=tm[:], in0=dst[:], scalar1=-PI, scalar2=2 * PI, op0=ALU.is_lt, op1=ALU.mult), reads=[dst], writes=[tm])
            cx.op("dve", lambda e: e.tensor_add(out=dst[:], in0=dst[:], in1=tm[:]), reads=[dst, tm], writes=[dst])
            cx.op("act", lambda e: e.activation(out=dst[:], in_=dst[:], func=AF.Sin), reads=[dst], writes=[dst])
        sin_of(t1, 0.0)
        sin_of(t2, 0.5 * PI)
        cx.op("dve", lambda e: e.tensor_tensor(out=lbim[:], in0=ar_[:], in1=t1[:], op=ALU.mult), reads=[ar_, t1], writes=[lbim])
        cx.op("dve", lambda e: e.tensor_tensor(out=lbre[:], in0=ar_[:], in1=t2[:], op=ALU.mult), reads=[ar_, t2], writes=[lbre])
        for r_ in range(2):
            cx.op("dve", lambda e, r_=r_: e.tensor_copy(out=A2[i][:, r_, :], in_=lbre[:]), reads=[lbre], writes=[A2[i]])
        cx.op("dve", lambda e: e.tensor_scalar(out=B2[i][:, 0, :], in0=lbim[:], scalar1=-1.0, scalar2=None, op0=ALU.mult),
              reads=[lbim], writes=[B2[i]])
        cx.op("dve", lambda e: e.tensor_copy(out=B2[i][:, 1, :], in_=lbim[:]), reads=[lbim], writes=[B2[i]])
        cx.op("dve", lambda e: e.tensor_scalar(out=t1[:], in0=lbre[:], scalar1=-1.0, scalar2=None, op0=ALU.add), reads=[lbre], writes=[t1])
        cx.op("dve", lambda e: e.tensor_tensor(out=t2[:], in0=lr[:], in1=lr[:], op=ALU.mult), reads=[lr], writes=[t2])
        cx.op("dve", lambda e: e.tensor_tensor(out=ai[:], in0=lim[:], in1=lim[:], op=ALU.mult), reads=[lim], writes=[ai])
        cx.op("dve", lambda e: e.tensor_add(out=t2[:], in0=t2[:], in1=ai[:]), reads=[t2, ai], writes=[t2])
        cx.op("dve", lambda e: e.reciprocal(out=t2[:], in_=t2[:]), reads=[t2], writes=[t2])
        cx.op("dve", lambda e: e.tensor_tensor(out=cre[:], in0=t1[:], in1=lr[:], op=ALU.mult), reads=[t1, lr], writes=[cre])
        cx.op("dve", lambda e: e.tensor_tensor(out=ai[:], in0=lbim[:], in1=lim[:], op=ALU.mult), reads=[lbim, lim], writes=[ai])
        cx.op("dve", lambda e: e.tensor_add(out=cre[:], in0=cre[:], in1=ai[:]), reads=[cre, ai], writes=[cre])
        cx.op("dve", lambda e: e.tensor_tensor(out=cre[:], in0=cre[:], in1=t2[:], op=ALU.mult), reads=[cre, t2], writes=[cre])
        cx.op("dve", lambda e: e.tensor_tensor(out=cim[:], in0=lbim[:], in1=lr[:], op=ALU.mult), reads=[lbim, lr], writes=[cim])
        cx.op("dve", lambda e: e.tensor_tensor(out=ai[:], in0=t1[:], in1=lim[:], op=ALU.mult), reads=[t1, lim], writes=[ai])
        cx.op("dve", lambda e: e.tensor_sub(out=cim[:], in0=cim[:], in1=ai[:]), reads=[cim, ai], writes=[cim])
        cx.op("dve", lambda e: e.tensor_tensor(out=cim[:], in0=cim[:], in1=t2[:], op=ALU.mult), reads=[cim, t2], writes=[cim])
        creB = cre.t[:].unsqueeze(2).to_broadcast([64, 128, 16])
        cimB = cim.t[:].unsqueeze(2).to_broadcast([64, 128, 16])
        cx.op("dve", lambda e: e.tensor_tensor(out=Bre[:], in0=bre[:], in1=creB, op=ALU.mult), reads=[bre, cre], writes=[Bre])
        cx.op("dve", lambda e: e.tensor_tensor(out=tb[:], in0=bim[:], in1=cimB, op=ALU.mult), reads=[bim, cim], writes=[tb])
        cx.op("dve", lambda e: e.tensor_sub(out=Bre[:], in0=Bre[:], in1=tb[:]), reads=[Bre, tb], writes=[Bre])
        cx.op("dve", lambda e: e.tensor_tensor(out=Bim[:], in0=bim[:], in1=creB, op=ALU.mult), reads=[bim, cre], writes=[Bim])
        cx.op("dve", lambda e: e.tensor_tensor(out=tb[:], in0=bre[:], in1=cimB, op=ALU.mult), reads=[bre, cim], writes=[tb])
        cx.op("dve", lambda e: e.tensor_add(out=Bim[:], in0=Bim[:], in1=tb[:]), reads=[Bim, tb], writes=[Bim])
        for g0 in range(0, 128, 4):
            pb = PB[(g0 // 4) % 4]

            def ftr(e, pb=pb, g0=g0):
                ins = None
                for gg in range(4):
                    for ri, src in enumerate((Bre, Bim)):
                        ins = e.transpose(pb[0:16, (gg * 2 + ri) * 64:(gg * 2 + ri + 1) * 64], src[:, g0 + gg, :], ident32[0:64, 0:64])
                return ins
            cx.op("pe", ftr, reads=[Bre, Bim, ident32], writes=[pb])
            cx.op("act", lambda e, pb=pb, g0=g0: e.copy(out=B16[0:16, g0:g0 + 4, :, :], in_=pb[0:16, 0:512]), reads=[pb], writes=[B16])
        cx.dma("sp", B16d[i].ap(), B16[:], osem, reads=[B16])
        cx.phase()
        cnat = cx.ar("cnat", [128, 2, 16, 64], F32)
        cx.dma("sp", cnat[:, 0, :, :], c_re[i], misc, writes=[cnat])
        cx.dma("sp", cnat[:, 1, :, :], c_im[i], misc, writes=[cnat])
        for h in range(16):
            pb = PB[4 + h % 2]

            def ftc(e, pb=pb, h=h):
                e.transpose(pb[0:64, 0:128], cnat[:, 0, h, :], ident32[:, :])
                return e.transpose(pb[0:64, 128:256], cnat[:, 1, h, :], ident32[:, :])
            cx.op("pe", ftc, reads=[cnat, ident32], writes=[pb])
            cx.op("act", lambda e, pb=pb, h=h: e.copy(out=C64[i][:, :, 0, h], in_=pb[0:64, 0:128]), reads=[pb], writes=[C64[i]])
            cx.op("act", lambda e, pb=pb, h=h: e.mul(out=C64[i][:, :, 1, h], in_=pb[0:64, 128:256], mul=-1.0), reads=[pb], writes=[C64[i]])
        cx.op("dve", lambda e: e.tensor_tensor(out=gd[:, i, :], in0=gains[:, 0, 2 * i + 1, :], in1=dsk[:, i, :], op=ALU.mult),
              reads=[gains, dsk], writes=[gd])

    def s5_mixer(i, li, seqs, T, prompt, last):
        cx.phase()
        W = 16 if prompt else 4
        B16 = cx.ar("B16", [16, 128, 2, 64], BF16)
        u16w = cx.ar("u16w", [16, 128, W], BF16)
        Z = cx.ar("Z", [64, 2, 128, W], F32)
        Hh = cx.ar("Hh", [64, 2, 128, W], BF16)
        y16w = cx.ar("y16w", [16, 128, W], F32)
        yT = cx.ar("yT", [128, KT, TP], F32)
        s1 = cx.ar("s1", [64, 2, 128], F32)
        s2 = cx.ar("s2", [64, 2, 128], F32)
        sig = [cx.ar("sig", [128, TP], F32) for _ in range(2)]
        cx.dma("sp", B16[:], B16d[i].ap(), misc, writes=[B16])
        if prompt:
            H3s = [H3p[i]]
        else:
            H3s = [cx.ar("H3s", [64, 3, 128], F32) for _ in range(4)]
            for s_ in range(4):
                cx.dma("sp", H3s[s_][:, 0, :], st_sre[i, s_].rearrange("g p -> p g"), misc, writes=[H3s[s_]], slow=True)
                cx.dma("sp", H3s[s_][:, 1, :], st_sim[i, s_].rearrange("g p -> p g"), misc, writes=[H3s[s_]], slow=True)
                cx.op("act", lambda e, s_=s_: e.copy(out=H3s[s_][:, 2, :], in_=H3s[s_][:, 0, :]), reads=[H3s[s_]], writes=[H3s[s_]])
        prenorm(0, li, T)
        u16v = u16w.t[:].rearrange("h (k g) w -> h k g w", g=8)
        y16v = y16w.t[:].rearrange("h (k g) w -> h k g w", g=8)
        for si, (slot, c0, L) in enumerate(seqs):
            H3 = H3s[si]
            for t0 in range(0, L, W):
                a0 = c0 + t0
                for gl in range(8):
                    cx.dma("sp", u16v[:, :, gl, :], hT[gl * 16:(gl + 1) * 16, :, a0:a0 + W], usem[gl], reads=[hT], writes=[u16w])
                for gc in range(8):
                    pa, pbk = PB[(gc % 2) * 2], PB[(gc % 2) * 2 + 1]

                    def fb(e, pa=pa, pbk=pbk, gc=gc):
                        ins = None
                        for gg in range(16):
                            g = gc * 16 + gg
                            e.matmul(pa[0:64, gg * W:(gg + 1) * W], B16[0:16, g, 0, :], u16w[0:16, g, :], start=True, stop=True)
                            ins = e.matmul(pbk[0:64, gg * W:(gg + 1) * W], B16[0:16, g, 1, :], u16w[0:16, g, :], start=True, stop=True)
                        return ins
                    cx.op("pe", fb, reads=[B16, u16w], writes=[pa, pbk])
                    cx.op("act", lambda e, pa=pa, gc=gc: e.copy(out=Z[:, 0, gc * 16:(gc + 1) * 16, :], in_=pa[0:64, 0:16 * W]), reads=[pa], writes=[Z])
                    cx.op("act", lambda e, pbk=pbk, gc=gc: e.copy(out=Z[:, 1, gc * 16:(gc + 1) * 16, :], in_=pbk[0:64, 0:16 * W]), reads=[pbk], writes=[Z])
                for t in range(W):
                    cx.op("dve", lambda e: e.tensor_tensor(out=s1[:], in0=A2[i][:], in1=H3[:, 0:2, :], op=ALU.mult), reads=[A2[i], H3], writes=[s1])
                    cx.op("dve", lambda e: e.tensor_tensor(out=s2[:], in0=B2[i][:], in1=H3[:, 1:3, :], op=ALU.mult), reads=[B2[i], H3], writes=[s2])
                    cx.op("dve", lambda e: e.tensor_add(out=s1[:], in0=s1[:], in1=s2[:]), reads=[s1, s2], writes=[s1])
                    cx.op("dve", lambda e, t=t: e.tensor_add(out=H3[:, 0:2, :], in0=s1[:], in1=Z[:, :, :, t]), reads=[s1, Z], writes=[H3])
                    cx.op("dve", lambda e, t=t: e.tensor_add(out=H3[:, 2, :], in0=s1[:, 0, :], in1=Z[:, 0, :, t]), reads=[s1, Z], writes=[H3])
                    cx.op("act", lambda e, t=t: e.copy(out=Hh[:, :, :, t], in_=H3[:, 0:2, :]), reads=[H3], writes=[Hh])
                for gc in range(8):
                    pc = PB[4 + gc % 2]

                    def fc(e, pc=pc, gc=gc):
                        ins = None
                        for gg in range(16):
                            g = gc * 16 + gg
                            e.matmul(pc[0:16, gg * W:(gg + 1) * W], C64[i][:, g, 0, :], Hh[:, 0, g, :], start=True, stop=False)
                            ins = e.matmul(pc[0:16, gg * W:(gg + 1) * W], C64[i][:, g, 1, :], Hh[:, 1, g, :], start=False, stop=True)
                        return ins
                    cx.op("pe", fc, reads=[C64[i], Hh], writes=[pc])
                    cx.op("act", lambda e, pc=pc, gc=gc: e.copy(out=y16w[0:16, gc * 16:(gc + 1) * 16, :], in_=pc[0:16, 0:16 * W]), reads=[pc], writes=[y16w])
                for gl in range(8):
                    cx.dma("sp", yT[gl * 16:(gl + 1) * 16, :, a0:a0 + W], y16v[:, :, gl, :], ysem[gl], reads=[y16w], writes=[yT])
        if prompt and last:
            cx.dma("sp", o_sre_p[i].rearrange("g p -> p g"), H3p[i][:, 0, :], osem, reads=[H3p[i]], slow=True)
            cx.dma("sp", o_sim_p[i].rearrange("g p -> p g"), H3p[i][:, 1, :], osem, reads=[H3p[i]], slow=True)
        if not prompt:
            for s_ in range(4):
                cx.dma("sp", o_sre_s[i, s_].rearrange("g p -> p g"), H3s[s_][:, 0, :], osem, reads=[H3s[s_]], slow=True)
                cx.dma("sp", o_sim_s[i, s_].rearrange("g p -> p g"), H3s[s_][:, 1, :], osem, reads=[H3s[s_]], slow=True)
        for k in range(KT):
            tb = tmpn[k % 2]
            cx.op("dve", lambda e, k=k, tb=tb: e.scalar_tensor_tensor(
                out=tb[:, 0:T], in0=xT[:, k, 0:T], scalar=gd[:, i, k:k + 1], in1=rstd[:, 0:T], op0=ALU.mult, op1=ALU.mult),
                reads=[xT, gd, rstd], writes=[tb])
            cx.op("dve", lambda e, k=k, tb=tb: e.tensor_add(out=tb[:, 0:T], in0=tb[:, 0:T], in1=yT[:, k, 0:T]), reads=[tb, yT], writes=[tb])
            cx.op("act", lambda e, k=k, tb=tb: e.activation(out=hT[:, k, 0:T], in_=tb[:, 0:T], func=AF.Gelu), reads=[tb], writes=[hT])
        for dm in range(KT):
            bA = bB = wget(sc_gl[i], dm, 128, 4096)
            vAB = bA.t[:, 0:4096].rearrange("p (j k n) -> p j k n", j=2, n=128)
            vA, vB = vAB[:, 0], vAB[:, 1]
            pa, pbk = PB[(2 * dm) % 4], PB[(2 * dm + 1) % 4]
            mm_fm(pa, (vA, bA), 0, 128, hT, lambda k: hT[:, k, 0:T], KT, T)
            mm_fm(pbk, (vB, bB), 0, 128, hT, lambda k: hT[:, k, 0:T], KT, T)
            sg = sig[dm % 2]
            cx.op("act", lambda e, pbk=pbk, sg=sg: e.activation(out=sg[:, 0:T], in_=pbk[:, 0:T], func=AF.Sigmoid), reads=[pbk], writes=[sg])
            cx.op("dve", lambda e, pa=pa, sg=sg, dm=dm: e.tensor_tensor(out=mixT[:, dm, 0:T], in0=sg[:, 0:T], in1=pa[:, 0:T], op=ALU.mult),
                  reads=[sg, pa], writes=[mixT])
        postnorm_add(1, li, T)

    def load_x(src2d, T):
        cx.dma("sp", xin[0:T, :], src2d, misc, writes=[xin])
        for k4 in range(0, KT, 4):
            pb = PB[(k4 // 4) % 2]

            def ft(e, pb=pb, k4=k4):
                ins = None
                for j in range(4):
                    ins = e.transpose(pb[:, j * 128:j * 128 + T], xin[0:T, (k4 + j) * 128:(k4 + j + 1) * 128], ident32[0:T, 0:T])
                return ins
            cx.op("pe", ft, reads=[xin, ident32], writes=[pb])
            for j in range(4):
                cx.op("act", lambda e, pb=pb, k4=k4, j=j: e.copy(out=xT[:, k4 + j, 0:T], in_=pb[:, j * 128:j * 128 + T]), reads=[pb], writes=[xT])

    def store_y(dst2d, T):
        for k4 in range(0, KT, 4):
            pb = PB[(k4 // 4) % 2]

            def ft(e, pb=pb, k4=k4):
                ins = None
                for j in range(4):
                    ins = e.transpose(pb[0:T, j * 128:(j + 1) * 128], xT[:, k4 + j, 0:T], ident32[:, :])
                return ins
            cx.op("pe", ft, reads=[xT, ident32], writes=[pb])
            cx.op("act", lambda e, pb=pb, k4=k4: e.copy(out=xin[0:T, k4 * 128:(k4 + 4) * 128], in_=pb[0:T, 0:512]), reads=[pb], writes=[xin])
        cx.dma("sp", dst2d, xin[0:T, :], osem, reads=[xin])

    def zero_states():
        for l in range(4):
            cx.op("dve", lambda e, l=l: e.memset(aprev[l][0][:], 0.0), writes=[aprev[l][0]])
        for i in range(2):
            cx.op("dve", lambda e, i=i: e.memset(Sp[i][:], 0.0), writes=[Sp[i]])
            cx.op("dve", lambda e, i=i: e.memset(kTe[i][:], 0.0), writes=[kTe[i]])
            cx.op("dve", lambda e, i=i: e.memset(vsw[i][:], 0.0), writes=[vsw[i]])
            cx.op("dve", lambda e, i=i: e.memset(H3p[i][:], 0.0), writes=[H3p[i]])

    def run_layers(seqs, T, first, prompt, last):
        for l in range(depth):
            if l % 2 == 0:
                even_mixer(l // 2, l, seqs, T, prompt, first, last)
            else:
                s5_mixer(l // 2, l, seqs, T, prompt, last)
            ffn(l, seqs, T)

    cx.init_arena(ARENA)
    zero_states()
    weight_prologue()
    for i in range(depth // 2):
        s5_prologue(i)
    for ti in range(ntiles):
        load_x(xp[ti * TP:(ti + 1) * TP, :], TP)
        run_layers([(0, 0, TP)], TP, ti == 0, True, ti == ntiles - 1)
        store_y(yp[ti * TP:(ti + 1) * TP, :], TP)
    for l in range(depth):
        for j in range(2):
            cx.dma("sp", o_conv_p[l, j].rearrange("(m p) -> p m", p=128), aprev[l][0][:, :, j], osem,
                   reads=[aprev[l][0]], slow=True)
    if do_sample:
        for l in range(depth):
            for s_ in range(4):
                for j in range(2):
                    cx.dma("sp", aprev[l][s_][:, :, j], st_conv[l, s_, j].rearrange("(m p) -> p m", p=128), misc,
                           writes=[aprev[l][s_]], slow=True)
        load_x(xs.rearrange("s t d -> (s t) d"), 16)
        run_layers([(s_, 4 * s_, 4) for s_ in range(4)], 16, False, False, True)
        store_y(ys.rearrange("s t d -> (s t) d"), 16)
        for l in range(depth):
            for s_ in range(4):
                for j in range(2):
                    cx.dma("sp", o_conv_s[l, s_, j].rearrange("(m p) -> p m", p=128), aprev[l][s_][:, :, j], osem,
                           reads=[aprev[l][s_]], slow=True)
    cx.barrier()
    return nc


_TABS = None


def kernel(**inp):
    global _TABS
    if _TABS is None:
        _TABS = host_tables()
    f = lambda a: np.ascontiguousarray(np.asarray(a, dtype=np.float32))
    nc = build(**_CFG)
    dp_ = _CFG['depth']
    nl4, nle, nlo = dp_, (dp_ + 1) // 2, max(dp_ // 2, 1)
    in_maps = []
    for c in range(8):
        sl = slice(4 * c, 4 * c + 4)
        m = {
            "xp": f(inp["x_prompt"][c % 2]), "xs": f(inp["x_sample"][sl]),
            "st_ret": f(inp["state_ret"][:, sl]),
            "st_k": f(np.asarray(inp["cache_swa_k"])[:, sl].reshape(2, 4, 128, 128)),
            "st_v": f(np.asarray(inp["cache_swa_v"])[:, sl].reshape(2, 4, 128, 128)),
            "st_sre": f(inp["state_ssm_re"][:, sl]), "st_sim": f(inp["state_ssm_im"][:, sl]),
            "st_conv": f(inp["state_ffn_conv"][:, sl]),
            "g_mix_pre": f(inp["norm_mix_pre"]), "g_mix_post": f(inp["norm_mix_post"]),
            "g_ffn_pre": f(inp["norm_ffn_pre"]), "g_ffn_post": f(inp["norm_ffn_post"]),
            "w_in": f(inp["w_in_even"][:nle]), "w_out": f(inp["w_out_even"][:nle]), "sinks": f(inp["swa_sinks"]),
            "lam_re": f(inp["ssm_lam_re"]), "lam_im": f(inp["ssm_lam_im"]), "log_step": f(inp["ssm_log_step"]),
            "b_re": f(inp["ssm_b_re"]), "b_im": f(inp["ssm_b_im"]), "c_re": f(inp["ssm_c_re"]), "c_im": f(inp["ssm_c_im"]),
            "ssm_d": f(inp["ssm_d"]), "w_glu": f(inp["w_glu"][:nlo]),
            "w_a": f(inp["ffn_w_a"][:nl4]), "w_g": f(inp["ffn_w_g"][:nl4]), "conv_w": f(inp["ffn_conv_w"]),
            "conv_b": f(inp["ffn_conv_b"]), "w_down": f(inp["ffn_w_down"][:nl4]),
            "t_intraT": _TABS["intraT"], "t_readB": _TABS["readB"], "t_writeT": _TABS["writeT"],
            "t_negdist": _TABS["negdist"], "t_ident": _TABS["ident"],
        }
        in_maps.append(m)
    res = run_bass_kernel_spmd(nc, in_maps, core_ids=list(range(8)))
    r = res.results
    st = lambda name, cores: np.stack([r[c][name] for c in cores], axis=1)
    cat = lambda name: np.concatenate([r[c][name] for c in range(8)], axis=1 if r[0][name].ndim > 3 or name != "ys" else 0)
    y_prompt = np.stack([r[0]["yp"], r[1]["yp"]], 0)
    y_sample = np.concatenate([r[c]["ys"] for c in range(8)], 0)
    ret_p = st("o_ret_p", [0, 1])
    ret_s = np.concatenate([r[c]["o_ret_s"] for c in range(8)], 1)
    k_p = st("o_k_p", [0, 1]).reshape(2, 2, 128, 2, 64)
    k_s = np.concatenate([r[c]["o_k_s"] for c in range(8)], 1).reshape(2, 32, 128, 2, 64)
    v_p = st("o_v_p", [0, 1]).reshape(2, 2, 128, 2, 64)
    v_s = np.concatenate([r[c]["o_v_s"] for c in range(8)], 1).reshape(2, 32, 128, 2, 64)
    sre_p = st("o_sre_p", [0, 1])
    sre_s = np.concatenate([r[c]["o_sre_s"] for c in range(8)], 1)
    sim_p = st("o_sim_p", [0, 1])
    sim_s = np.concatenate([r[c]["o_sim_s"] for c in range(8)], 1)
    conv_p = st("o_conv_p", [0, 1])
    conv_s = np.concatenate([r[c]["o_conv_s"] for c in range(8)], 1)
    outs = (y_prompt, y_sample, ret_p, ret_s, k_p, k_s, v_p, v_s, sre_p, sre_s, sim_p, sim_s, conv_p, conv_s)
    return tuple(np.ascontiguousarray(o, dtype=np.float32) for o in outs)
```

```python
import math
import numpy as np
import concourse.bass as bass
import concourse.mybir as mybir
from concourse.bass_utils import run_bass_kernel_spmd

F32 = mybir.dt.float32
BF16 = mybir.dt.bfloat16
AF = mybir.ActivationFunctionType
ALU = mybir.AluOpType
AX = mybir.AxisListType

D = 2048
KT = 16
DEPTH = 4
DFF = 5632
MFF = 44
EVEN_IN = 5376
TP = 128
SEQ = 4096
NPT = SEQ // TP
EPS = 1e-6
WSLOT = 5632
NSLOT = 4
ARENA = 83 * 1024


class Buf:
    def __init__(self, name, t):
        self.name = name
        self.t = t
        self.last_w = None
        self.readers = {}

    def __getitem__(self, k):
        return self.t[k]


class DSem:
    def __init__(self, ctx, name):
        self.key = name
        ctx.sems[name] = ctx.nc.alloc_semaphore(name)
        self.count = 0
        ctx.dsems.append(self)


class Ctx:
    def __init__(self, nc):
        self.nc = nc
        self.engs = {"pe": nc.tensor, "act": nc.scalar, "dve": nc.vector, "pool": nc.gpsimd, "sp": nc.sync}
        self.sems = {}
        self.ecnt = {}
        for k in ("pe", "act", "dve", "pool"):
            self.sems[k] = nc.alloc_semaphore("e_" + k)
            self.ecnt[k] = 0
        self.known = {k: {} for k in self.engs}
        self.nbuf = 0
        self.dsems = []
        self.ar_base = None
        self.ar_off = 0
        self.ar_size = 0

    def init_arena(self, nbytes):
        base = (self.nc.sbuf_base + 31) // 32 * 32
        self.nc.alloc_sbuf_tensor("arena", [128, nbytes // 4], F32)
        self.ar_base = base
        self.ar_size = nbytes
        self.ar_off = 0

    def ar(self, name, shape, dt):
        esz = 2 if dt == BF16 else 4
        n = esz
        for d in shape[1:]:
            n *= d
        n = (n + 31) // 32 * 32
        assert self.ar_off + n <= self.ar_size, (name, self.ar_off, n, self.ar_size)
        self.nbuf += 1
        t = self.nc.alloc_sbuf_tensor_at("%s_%d" % (name, self.nbuf), list(shape), dt, offset=self.ar_base + self.ar_off)
        self.ar_off += n
        return Buf(name, t)

    def barrier(self):
        evs = [(k, v) for k, v in self.ecnt.items() if v] + [(d.key, d.count) for d in self.dsems if d.count]
        for e in self.engs:
            self._wait(e, evs)

    def phase(self):
        self.barrier()
        self.ar_off = 0

    def sb(self, name, shape, dt):
        self.nbuf += 1
        return Buf(name, self.nc.alloc_sbuf_tensor("%s_%d" % (name, self.nbuf), list(shape), dt))

    def ps(self, name, shape, dt=F32):
        self.nbuf += 1
        b = Buf(name, self.nc.alloc_psum_tensor("%s_%d" % (name, self.nbuf), list(shape), dt))
        b.is_psum = True
        return b

    def _wait(self, eng, evs):
        need = {}
        for (k, v) in evs:
            if eng == "pe" and k == "pe":
                continue
            if self.known[eng].get(k, 0) >= v:
                continue
            if need.get(k, 0) < v:
                need[k] = v
        for k, v in need.items():
            self.engs[eng].wait_ge(self.sems[k], v)
            self.known[eng][k] = v

    def _deps(self, reads, writes):
        evs = []
        for b in reads:
            if b.last_w is not None:
                evs.append(b.last_w)
            if getattr(b, "is_psum", False):
                evs.extend(b.readers.items())
        for b in writes:
            if b.last_w is not None:
                evs.append(b.last_w)
            evs.extend(b.readers.items())
        return evs

    def _mark(self, ev, reads, writes):
        for b in writes:
            b.last_w = ev
            b.readers = {}
        for b in reads:
            if b in writes:
                continue
            if b.readers.get(ev[0], 0) < ev[1]:
                b.readers[ev[0]] = ev[1]

    def op(self, eng, fn, reads=(), writes=()):
        self._wait(eng, self._deps(reads, writes))
        ins = fn(self.engs[eng])
        self.ecnt[eng] += 1
        ins.then_inc(self.sems[eng], 1)
        self._mark((eng, self.ecnt[eng]), reads, writes)

    def dma(self, q, out, in_, dsem, reads=(), writes=(), slow=False):
        evs = self._deps(reads, writes)
        if dsem.count:
            evs.append((dsem.key, dsem.count))
        self._wait(q, evs)
        kw = {"allow_slow_non_contiguous": True} if slow else {}
        ins = self.engs[q].dma_start(out=out, in_=in_, **kw)
        dsem.count += 16
        ins.then_inc(self.sems[dsem.key], 16)
        self._mark((dsem.key, dsem.count), reads, writes)

    def finish(self):
        allv = {}
        for k in ("pe", "act", "dve", "pool"):
            if self.ecnt[k]:
                allv[k] = self.ecnt[k]
        for k, s in self.sems.items():
            if k not in self.ecnt:
                allv[k] = self._dcount[k].count if k in self._dcount else 0
        self._wait("sp", [(k, v) for k, v in allv.items() if v])


def host_tables():
    tabs = {}
    lg = np.log(1.0 - 2.0 ** (-5.0 - np.arange(8, dtype=np.float64)))
    j = np.arange(128)[:, None]
    i = np.arange(128)[None, :]
    intraT = np.zeros((128, 8, 128), np.float32)
    for h in range(8):
        intraT[:, h, :] = np.where(i >= j, np.exp(lg[h] * np.maximum(i - j, 0)), 0.0) * (128 ** -0.5)
    tabs["intraT"] = intraT
    readB = np.zeros((128, 8, 128), np.float32)
    for h in range(8):
        readB[:, h, :] = np.exp(lg[h] * (np.arange(128) + 1.0))[None, :]
    tabs["readB"] = readB
    wr = np.zeros((128, 16), np.float32)
    for h in range(8):
        wr[:, h] = np.exp(lg[h] * (127.0 - np.arange(128))) * (128 ** -0.5)
        wr[:4, 8 + h] = np.exp(lg[h] * (3.0 - np.arange(4))) * (128 ** -0.5)
    tabs["writeT"] = wr
    q = np.arange(128)[:, None]
    k = np.arange(256)[None, :]
    dist = 128 + q - k
    valid = (dist >= 0) & (dist <= 128)
    nd = np.where(valid, -dist.astype(np.float32), -1.0e7).astype(np.float32)
    nd2 = nd.copy()
    nd2[:, :128] = -1.0e7
    tabs["negdist"] = np.concatenate([nd, nd2], axis=1).astype(np.float32)
    tabs["ident"] = np.eye(128, dtype=np.float32)
    return tabs


RET_DECAY = [float(1.0 - 2.0 ** (-5.0 - h)) for h in range(8)]
SLOPES = [float(2.0 ** (-8.0 * (h + 1) / 16)) for h in range(16)]


_CFG = dict(ntiles=NPT, depth=DEPTH, do_sample=True)
_DBG = dict(cut=99, nblk=21)


def build(ntiles=NPT, depth=DEPTH, do_sample=True):
    nc = bass.Bass("TRN2", target_bir_lowering=False)
    cx = Ctx(nc)
    nl4, nle, nlo = depth, (depth + 1) // 2, max(depth // 2, 1)

    def din(name, shape):
        return nc.dram_tensor(name, list(shape), F32, kind="ExternalInput").ap()

    def dout(name, shape):
        return nc.dram_tensor(name, list(shape), F32, kind="ExternalOutput").ap()

    xp = din("xp", [SEQ, D])
    xs = din("xs", [4, 4, D])
    st_ret = din("st_ret", [2, 4, 8, 128, 128])
    st_k = din("st_k", [2, 4, 128, 128])
    st_v = din("st_v", [2, 4, 128, 128])
    st_sre = din("st_sre", [2, 4, 128, 64])
    st_sim = din("st_sim", [2, 4, 128, 64])
    st_conv = din("st_conv", [4, 4, 2, DFF])
    g_mix_pre = din("g_mix_pre", [4, D])
    g_mix_post = din("g_mix_post", [4, D])
    g_ffn_pre = din("g_ffn_pre", [4, D])
    g_ffn_post = din("g_ffn_post", [4, D])
    w_in = din("w_in", [nle, D, EVEN_IN])
    w_out = din("w_out", [nle, D, D])
    sinks = din("sinks", [2, 16])
    lam_re = din("lam_re", [2, 128, 64])
    lam_im = din("lam_im", [2, 128, 64])
    log_step = din("log_step", [2, 128])
    b_re = din("b_re", [2, 128, 64, 16])
    b_im = din("b_im", [2, 128, 64, 16])
    c_re = din("c_re", [2, 128, 16, 64])
    c_im = din("c_im", [2, 128, 16, 64])
    ssm_d = din("ssm_d", [2, D])
    w_glu = din("w_glu", [nlo, D, 2 * D])
    w_a = din("w_a", [nl4, D, DFF])
    w_g = din("w_g", [nl4, D, DFF])
    conv_w = din("conv_w", [4, 3, DFF])
    conv_b = din("conv_b", [4, DFF])
    w_down = din("w_down", [nl4, DFF, D])
    t_intraT = din("t_intraT", [128, 8, 128])
    t_readB = din("t_readB", [128, 8, 128])
    t_writeT = din("t_writeT", [128, 16])
    t_negdist = din("t_negdist", [128, 512])
    t_ident = din("t_ident", [128, 128])

    yp = dout("yp", [SEQ, D])
    ys = dout("ys", [4, 4, D])
    o_ret_p = dout("o_ret_p", [2, 8, 128, 128])
    o_ret_s = dout("o_ret_s", [2, 4, 8, 128, 128])
    o_k_p = dout("o_k_p", [2, 128, 128])
    o_k_s = dout("o_k_s", [2, 4, 128, 128])
    o_v_p = dout("o_v_p", [2, 128, 128])
    o_v_s = dout("o_v_s", [2, 4, 128, 128])
    o_sre_p = dout("o_sre_p", [2, 128, 64])
    o_sre_s = dout("o_sre_s", [2, 4, 128, 64])
    o_sim_p = dout("o_sim_p", [2, 128, 64])
    o_sim_s = dout("o_sim_s", [2, 4, 128, 64])
    o_conv_p = dout("o_conv_p", [4, 2, DFF])
    o_conv_s = dout("o_conv_s", [4, 4, 2, DFF])

    xT = cx.sb("xT", [128, KT, TP], F32)
    hT = cx.sb("hT", [128, KT, TP], BF16)
    mixT = cx.sb("mixT", [128, KT, TP], F32)
    wsl = [cx.sb("wsl%d" % i, [128, WSLOT], BF16) for i in range(NSLOT)]
    wsem = [DSem(cx, "wsem%d" % i) for i in range(NSLOT)]
    wstate = {"i": 0}
    misc = DSem(cx, "misc")
    osem = DSem(cx, "osem")
    gains = cx.sb("gains", [128, 4, 4, KT], F32)
    cw = cx.sb("cw", [128, 4, 3, MFF], F32)
    cb = cx.sb("cb", [128, 4, MFF], F32)
    dsk = cx.sb("dsk", [128, 2, KT], F32)
    sinkB = cx.sb("sinkB", [128, 32], F32)
    intraT = cx.sb("intraT", [128, 8, 128], F32)
    readB = cx.sb("readB", [128, 8, 128], F32)
    writeT = cx.sb("writeT", [128, 16], F32)
    negdist = cx.sb("negdist", [128, 512], F32)
    ident32 = cx.sb("ident32", [128, 128], F32)
    ident = cx.sb("ident", [128, 128], BF16)
    onesD = cx.sb("onesD", [128, 128], F32)
    ones128 = cx.sb("ones128", [128, 128], F32)

    def ld(dst, dst_ap, src_ap, slow=False):
        cx.dma("sp", dst_ap, src_ap, misc, writes=[dst], slow=slow)

    for li in range(4):
        for ni, g in enumerate((g_mix_pre, g_mix_post, g_ffn_pre, g_ffn_post)):
            ld(gains, gains[:, ni, li, :], g[li].rearrange("(k p) -> p k", p=128), slow=True)
        for j in range(3):
            ld(cw, cw[:, li, j, :], conv_w[li, j].rearrange("(m p) -> p m", p=128), slow=True)
        ld(cb, cb[:, li, :], conv_b[li].rearrange("(m p) -> p m", p=128), slow=True)
    for i in range(2):
        ld(dsk, dsk[:, i, :], ssm_d[i].rearrange("(k p) -> p k", p=128), slow=True)
    ld(sinkB, sinkB[:, :], sinks.rearrange("a b -> (a b)").partition_broadcast(128), slow=True)
    ld(intraT, intraT[:], t_intraT)
    ld(readB, readB[:], t_readB)
    ld(writeT, writeT[:], t_writeT)
    ld(negdist, negdist[:], t_negdist)
    ld(ident32, ident32[:], t_ident)
    cx.op("dve", lambda e: e.tensor_copy(out=ident[:], in_=ident32[:]), reads=[ident32], writes=[ident])
    cx.op("dve", lambda e: e.memset(onesD[:], 1.0 / D), writes=[onesD])
    cx.op("dve", lambda e: e.memset(ones128[:], 1.0 / 128), writes=[ones128])

    Sp = [cx.sb("Sp", [128, 8, 128], F32) for i in range(2)]
    Sb = cx.sb("Sb", [128, 8, 128], BF16)
    kTe = [cx.sb("kTe", [64, 2, 128 + TP], BF16) for i in range(2)]
    vsw = [cx.sb("vsw", [128, 1 + TP // 128, 128], BF16) for i in range(2)]
    H3p = [cx.sb("H3p", [64, 3, 128], F32) for i in range(2)]
    A2 = [cx.sb("A2", [64, 2, 128], F32) for i in range(2)]
    B2 = [cx.sb("B2", [64, 2, 128], F32) for i in range(2)]
    C64 = [cx.sb("C64", [64, 128, 2, 16], BF16) for i in range(2)]
    gd = cx.sb("gd", [128, 2, KT], F32)
    negpi = cx.sb("negpi", [128, 1], F32)
    usem = [DSem(cx, "usem%d" % j) for j in range(8)]
    ysem = [DSem(cx, "ysem%d" % j) for j in range(8)]
    B16d = [nc.dram_tensor("B16d%d" % i, [16, 128 * 128], BF16) for i in range(2)]
    aprev = [[cx.sb("aprev", [128, MFF, 2], F32) for s in range(4)] for l in range(4)]

    PB = [cx.ps("pb%d" % i, [128, 512], F32) for i in range(8)]

    def scr(name, nblk, P, n):
        return Buf(name, nc.dram_tensor(name, [nblk, P, n], BF16))
    sc_ag = [scr("sc_ag%d" % l, MFF, 128, 4096) for l in range(depth)]
    sc_dn = [scr("sc_dn%d" % l, KT, 128, MFF * 128) for l in range(depth)]
    sc_in = [scr("sc_in%d" % i, 21, 128, 4096) for i in range(nle)]
    sc_or = [scr("sc_or%d" % i, KT, 128, 1024) for i in range(nle)]
    sc_os = [scr("sc_os%d" % i, KT, 64, 2048) for i in range(nle)]
    sc_gl = [scr("sc_gl%d" % i, KT, 128, 4096) for i in range(depth // 2)]
    ssem = [DSem(cx, "ssem%d" % i) for i in range(NSLOT)]
    lsem = [DSem(cx, "lsem%d" % i) for i in range(NSLOT)]

    def convert(scb, blk, P, parts):
        i = wstate["i"] % NSLOT
        wstate["i"] += 1
        b = wsl[i]
        off = 0
        for (src_ap, nk, ncols) in parts:
            view = b.t[0:P, off:off + nk * ncols].rearrange("p (k n) -> p k n", n=ncols)
            cx.dma("pool", view, src_ap.rearrange("(k p) n -> p k n", p=P), wsem[i], writes=[b])
            off += nk * ncols
        cx.dma("sp", scb.t.ap()[blk, 0:P, 0:off], b.t[0:P, 0:off], ssem[i], reads=[b], writes=[scb])

    def weight_prologue():
        for l in range(depth):
            if l % 2 == 0:
                i = l // 2
                for blk in range(21):
                    convert(sc_in[i], blk, 128, [(w_in[i][:, blk * 256:(blk + 1) * 256], KT, 256)])
                for dm in range(KT):
                    convert(sc_or[i], dm, 128, [(w_out[i][0:1024, dm * 128:(dm + 1) * 128], 8, 128)])
                    convert(sc_os[i], dm, 64, [(w_out[i][1024:2048, dm * 128:(dm + 1) * 128], 16, 128)])
            else:
                i = l // 2
                for dm in range(KT):
                    convert(sc_gl[i], dm, 128, [(w_glu[i][:, dm * 128:(dm + 1) * 128], KT, 128),
                                                (w_glu[i][:, D + dm * 128:D + (dm + 1) * 128], KT, 128)])
            for m in range(MFF):
                convert(sc_ag[l], m, 128, [(w_a[l][:, m * 128:(m + 1) * 128], KT, 128), (w_g[l][:, m * 128:(m + 1) * 128], KT, 128)])
            for dm in range(KT):
                convert(sc_dn[l], dm, 128, [(w_down[l][:, dm * 128:(dm + 1) * 128], MFF, 128)])

    def wget(scb, blk, P, n):
        i = wstate["i"] % NSLOT
        wstate["i"] += 1
        b = wsl[i]
        cx.dma("sp", b.t[0:P, 0:n], scb.t.ap()[blk, 0:P, 0:n], lsem[i], reads=[scb], writes=[b])
        return b

    tmpn = [cx.sb("tmpn", [128, TP], F32) for _ in range(2)]
    ssq = cx.sb("ssq", [128, TP], F32)
    rstd = cx.sb("rstd", [128, TP], F32)

    def rms_stats(src, T, nk, ones, pbank):
        for k in range(nk):
            tb = tmpn[k % 2]
            cx.op("act", lambda e, k=k, tb=tb: e.activation(out=tb[:, 0:T], in_=src[:, k, 0:T], func=AF.Square),
                  reads=[src], writes=[tb])
            if k == 0:
                cx.op("dve", lambda e, tb=tb: e.tensor_copy(out=ssq[:, 0:T], in_=tb[:, 0:T]), reads=[tb], writes=[ssq])
            else:
                cx.op("dve", lambda e, tb=tb: e.tensor_add(out=ssq[:, 0:T], in0=ssq[:, 0:T], in1=tb[:, 0:T]),
                      reads=[tb, ssq], writes=[ssq])
        cx.op("pe", lambda e: e.matmul(pbank[:, 0:T], ones[:, :], ssq[:, 0:T], start=True, stop=True),
              reads=[ones, ssq], writes=[pbank])
        cx.op("act", lambda e: e.activation(out=rstd[:, 0:T], in_=pbank[:, 0:T], func=AF.Sqrt, bias=EPS_AP[:, 0:1]),
              reads=[pbank, EPS_B], writes=[rstd])
        cx.op("dve", lambda e: e.reciprocal(out=rstd[:, 0:T], in_=rstd[:, 0:T]), reads=[rstd], writes=[rstd])

    EPS_B = cx.sb("epsb", [128, 1], F32)
    EPS_AP = EPS_B
    cx.op("dve", lambda e: e.memset(EPS_B[:], EPS), writes=[EPS_B])

    def prenorm(ni, li, T):
        rms_stats(xT, T, KT, onesD, PB[7])
        for k in range(KT):
            cx.op("dve", lambda e, k=k: e.scalar_tensor_tensor(
                out=hT[:, k, 0:T], in0=xT[:, k, 0:T], scalar=gains[:, ni, li, k:k + 1], in1=rstd[:, 0:T],
                op0=ALU.mult, op1=ALU.mult), reads=[xT, gains, rstd], writes=[hT])

    def postnorm_add(ni, li, T):
        rms_stats(mixT, T, KT, onesD, PB[7])
        for k in range(KT):
            tb = tmpn[k % 2]
            cx.op("dve", lambda e, k=k, tb=tb: e.scalar_tensor_tensor(
                out=tb[:, 0:T], in0=mixT[:, k, 0:T], scalar=gains[:, ni, li, k:k + 1], in1=rstd[:, 0:T],
                op0=ALU.mult, op1=ALU.mult), reads=[mixT, gains, rstd], writes=[tb])
            cx.op("dve", lambda e, k=k, tb=tb: e.tensor_add(out=xT[:, k, 0:T], in0=xT[:, k, 0:T], in1=tb[:, 0:T]),
                  reads=[tb, xT], writes=[xT])

    def mm_fm(pb, wview, c0, M, rhs_buf, rhs_fn, nk, T, extra_reads=()):
        def fn(e):
            ins = None
            for k in range(nk):
                ins = e.matmul(pb[0:M, 0:T], wview[0][:, k, c0:c0 + M], rhs_fn(k), start=(k == 0), stop=(k == nk - 1))
            return ins
        cx.op("pe", fn, reads=[wview[1], rhs_buf] + list(extra_reads), writes=[pb])

    def ffn(li, seqs, T):
        cx.phase()
        act = cx.ar("act", [128, MFF, TP], BF16)
        aext = [cx.ar("aext", [128, TP + 8], F32) for _ in range(2)]
        cacc = [cx.ar("cacc", [128, TP + 8], F32) for _ in range(2)]
        prenorm(2, li, T)
        TE = T + 2 * len(seqs)
        for m in range(MFF):
            bA = bG = wget(sc_ag[li], m, 128, 4096)
            vAG = bA.t[:, 0:4096].rearrange("p (j k n) -> p j k n", j=2, n=128)
            vA, vG = vAG[:, 0], vAG[:, 1]
            pa = PB[(2 * m) % 6]
            pg = PB[(2 * m + 1) % 6]
            mm_fm(pa, (vA, bA), 0, 128, hT, lambda k: hT[:, k, 0:T], KT, T)
            mm_fm(pg, (vG, bG), 0, 128, hT, lambda k: hT[:, k, 0:T], KT, T)
            ae = aext[m % 2]
            ca = cacc[m % 2]
            for si, (slot, c0, L) in enumerate(seqs):
                e0 = c0 + 2 * si
                cx.op("act", lambda e, e0=e0, slot=slot: e.copy(out=ae[:, e0:e0 + 2], in_=aprev[li][slot][:, m, :]),
                      reads=[aprev[li][slot]], writes=[ae])
                cx.op("act", lambda e, e0=e0, c0=c0, L=L: e.copy(out=ae[:, e0 + 2:e0 + 2 + L], in_=pa[:, c0:c0 + L]),
                      reads=[pa], writes=[ae])
            cx.op("dve", lambda e: e.tensor_scalar(out=ca[:, 2:TE], in0=ae[:, 2:TE], scalar1=cw[:, li, 2, m:m + 1],
                                                   scalar2=cb[:, li, m:m + 1], op0=ALU.mult, op1=ALU.add),
                  reads=[ae, cw, cb], writes=[ca])
            cx.op("dve", lambda e: e.scalar_tensor_tensor(out=ca[:, 2:TE], in0=ae[:, 1:TE - 1], scalar=cw[:, li, 1, m:m + 1],
                                                          in1=ca[:, 2:TE], op0=ALU.mult, op1=ALU.add),
                  reads=[ae, cw, ca], writes=[ca])
            cx.op("dve", lambda e: e.scalar_tensor_tensor(out=ca[:, 2:TE], in0=ae[:, 0:TE - 2], scalar=cw[:, li, 0, m:m + 1],
                                                          in1=ca[:, 2:TE], op0=ALU.mult, op1=ALU.add),
                  reads=[ae, cw, ca], writes=[ca])
            cx.op("act", lambda e: e.activation(out=ca[:, 2:TE], in_=ca[:, 2:TE], func=AF.Gelu), reads=[ca], writes=[ca])
            for si, (slot, c0, L) in enumerate(seqs):
                e0 = c0 + 2 * si
                cx.op("dve", lambda e, e0=e0, c0=c0, L=L: e.tensor_tensor(
                    out=act[:, m, c0:c0 + L], in0=ca[:, e0 + 2:e0 + 2 + L], in1=pg[:, c0:c0 + L], op=ALU.mult),
                    reads=[ca, pg], writes=[act])
                cx.op("act", lambda e, e0=e0, L=L, slot=slot: e.copy(out=aprev[li][slot][:, m, :], in_=ae[:, e0 + L:e0 + L + 2]),
                      reads=[ae], writes=[aprev[li][slot]])
        for dm in range(KT):
            bW, vW = wload_down(li, dm)
            pb = PB[dm % 6]
            mm_fm(pb, (vW, bW), 0, 128, act, lambda k: act[:, k, 0:T], MFF, T)
            cx.op("act", lambda e, pb=pb, dm=dm: e.copy(out=mixT[:, dm, 0:T], in_=pb[:, 0:T]), reads=[pb], writes=[mixT])
        postnorm_add(3, li, T)

    def wload_down(li, dm):
        b = wget(sc_dn[li], dm, 128, MFF * 128)
        return b, b.t[:, 0:MFF * 128].rearrange("p (k n) -> p k n", n=128)

    def even_mixer(i, li, seqs, T, prompt, first, last):
        cx.phase()
        rqT = cx.ar("rqT", [128, 8, TP], BF16)
        rkT = cx.ar("rkT", [128, 8, TP], BF16)
        rgT = cx.ar("rgT", [128, 8, TP], BF16)
        ktok = cx.ar("ktok", [128, 4, 1024], BF16)
        vtok = cx.ar("vtok", [128, 4, 1024], BF16)
        aqT = cx.ar("aqT", [64, 16, TP], BF16)
        kvtok32 = cx.ar("kvtok32", [128, 4, 256], F32)
        vcur = cx.ar("vcur", [128, 4, 128], BF16)
        oret = cx.ar("oret", [128, 8, TP], F32)
        oTret = cx.ar("oTret", [128, 8, TP], BF16)
        oTswa = cx.ar("oTswa", [64, 16, TP], BF16)
        scT = cx.ar("scT", [128, 128], BF16)
        qr = cx.ar("qr", [128, 128], BF16)
        sb32 = cx.ar("sb32", [128, 256], F32)
        p32 = cx.ar("p32", [128, 256], F32)
        pn = cx.ar("pn", [128, 256], BF16)
        pT = cx.ar("pT", [128, 256], BF16)
        sm = cx.ar("sm", [128, 8], F32)
        if prompt:
            Ssl = [Sp[i]]
        else:
            Ssl = [cx.ar("Ss", [128, 8, 128], F32) for _ in range(4)]
            kTs = cx.ar("kTs", [64, 2, 4 * 132], BF16)
            kst32 = cx.ar("kst32", [64, 2, 128], F32)
            vst32 = cx.ar("vst32", [128, 128], F32)
            vprev_s = [cx.ar("vprev_s", [128, 128], BF16) for _ in range(4)]
            cx.op("dve", lambda e: e.memset(kTs[:], 0.0), writes=[kTs])
            for s_ in range(4):
                cx.dma("sp", Ssl[s_][:], st_ret[i, s_].rearrange("h d e -> d h e"), misc, writes=[Ssl[s_]])
                for kvh in range(2):
                    cx.dma("sp", kst32[:, kvh, :], st_k[i, s_][:, kvh * 64:(kvh + 1) * 64].rearrange("t d -> d t"), misc,
                           writes=[kst32], slow=True)
                cx.op("act", lambda e, s_=s_: e.copy(out=kTs[:, :, s_ * 132:s_ * 132 + 128], in_=kst32[:]), reads=[kst32], writes=[kTs])
                cx.dma("sp", vst32[:], st_v[i, s_], misc, writes=[vst32])
                cx.op("act", lambda e, s_=s_: e.copy(out=vprev_s[s_][:], in_=vst32[:]), reads=[vst32], writes=[vprev_s[s_]])
                cx.dma("sp", o_v_s[i, s_, 0:124, :], vst32[4:128, :], osem, reads=[vst32])
                cx.dma("sp", vst32[:], st_k[i, s_], misc, writes=[vst32])
                cx.dma("sp", o_k_s[i, s_, 0:124, :], vst32[4:128, :], osem, reads=[vst32])
        cx.op("dve", lambda e: e.memset(vcur[:], 0.0), writes=[vcur])
        cx.op("dve", lambda e: e.memset(ktok[:], 0.0), writes=[ktok])
        cx.op("dve", lambda e: e.memset(vtok[:], 0.0), writes=[vtok])

        def keyview(li_, prompt_, si, L):
            if prompt_:
                return kTe[li_], kTe[li_].t[:, :, 0:128 + L]
            return kTs, kTs.t[:, :, si * 132:si * 132 + 128 + L]

        prenorm(0, li, T)
        wcol = 0 if prompt else 8

        def cut_here():
            cx.op("dve", lambda e: e.memset(mixT[:], 0.0), writes=[mixT])
            postnorm_add(1, li, T)
        if _DBG["cut"] <= 0:
            return cut_here()
        for blk in range(min(21, _DBG['nblk'])):
            c = blk * 256
            bW = wget(sc_in[i], blk, 128, 4096)
            vW = bW.t[:, 0:4096].rearrange("p (k n) -> p k n", n=256)
            if c < 1024 or 1024 <= c < 2048 or 3072 <= c < 4096:
                for hh in range(2):
                    h = (c % 1024) // 128 + hh
                    pb = PB[(2 * blk + hh) % 4]
                    mm_fm(pb, (vW, bW), hh * 128, 128, hT, lambda k: hT[:, k, 0:T], KT, T)
                    if c < 1024:
                        cx.op("act", lambda e, pb=pb, h=h: e.copy(out=rqT[:, h, 0:T], in_=pb[:, 0:T]), reads=[pb], writes=[rqT])
                    elif c < 2048:
                        cx.op("act", lambda e, pb=pb, h=h: e.copy(out=rkT[:, h, 0:T], in_=pb[:, 0:T]), reads=[pb], writes=[rkT])
                    else:
                        cx.op("act", lambda e, pb=pb, h=h: e.activation(out=rgT[:, h, 0:T], in_=pb[:, 0:T], func=AF.Silu),
                              reads=[pb], writes=[rgT])
            if 1024 <= c < 3072 or c >= 5120:
                for si, (slot, c0, L) in enumerate(seqs):
                    pb = PB[4 + (si % 2)]

                    def fn(e, pb=pb, c0=c0, L=L):
                        ins = None
                        for k in range(KT):
                            ins = e.matmul(pb[0:L, 0:256], hT[:, k, c0:c0 + L], vW[:, k, 0:256], start=(k == 0), stop=(k == KT - 1))
                        return ins
                    cx.op("pe", fn, reads=[bW, hT], writes=[pb])
                    if c < 2048:
                        off = c - 1024
                        for hh in range(2):
                            h = off // 128 + hh
                            cx.op("dve", lambda e, pb=pb, L=L, si=si, off=off, hh=hh, h=h: e.tensor_scalar(
                                out=ktok[0:L, si, off + hh * 128:off + (hh + 1) * 128], in0=pb[0:L, hh * 128:(hh + 1) * 128],
                                scalar1=writeT[0:L, wcol + h:wcol + h + 1], scalar2=None, op0=ALU.mult),
                                reads=[pb, writeT], writes=[ktok])
                    elif c < 3072:
                        off = c - 2048
                        cx.op("act", lambda e, pb=pb, L=L, si=si, off=off: e.copy(out=vtok[0:L, si, off:off + 256], in_=pb[0:L, 0:256]),
                              reads=[pb], writes=[vtok])
                    else:
                        cx.op("act", lambda e, pb=pb, L=L, si=si: e.copy(out=kvtok32[0:L, si, :], in_=pb[0:L, 0:256]),
                              reads=[pb], writes=[kvtok32])
                        cx.op("dve", lambda e, L=L, si=si: e.tensor_copy(out=vcur[0:L, si, :], in_=kvtok32[0:L, si, 128:256]),
                              reads=[kvtok32], writes=[vcur])
            if 4096 <= c < 5120:
                for hh in range(4):
                    hq = (c - 4096) // 64 + hh
                    pb = PB[(blk + hh) % 4]
                    mm_fm(pb, (vW, bW), hh * 64, 64, hT, lambda k: hT[:, k, 0:T], KT, T)
                    cx.op("act", lambda e, pb=pb, hq=hq: e.mul(out=aqT[0:64, hq, 0:T], in_=pb[0:64, 0:T], mul=0.125),
                          reads=[pb], writes=[aqT])
            if c >= 5120:
                for kvh in range(2):
                    pb = PB[kvh]
                    mm_fm(pb, (vW, bW), kvh * 64, 64, hT, lambda k: hT[:, k, 0:T], KT, T)
                    for si, (slot, c0, L) in enumerate(seqs):
                        kb, kv = keyview(i, prompt, si, L)
                        cx.op("act", lambda e, pb=pb, kv=kv, kvh=kvh, c0=c0, L=L: e.copy(out=kv[0:64, kvh, 128:128 + L], in_=pb[0:64, c0:c0 + L]),
                              reads=[pb], writes=[kb])
        if _DBG["cut"] <= 1:
            return cut_here()
        for si, (slot, c0, L) in enumerate(seqs):
            Sst = Ssl[si]
            cx.op("act", lambda e, Sst=Sst: e.copy(out=Sb[:], in_=Sst[:]), reads=[Sst], writes=[Sb])
            for h in range(8):
                p0, p1, p2 = PB[0 + (h % 2) * 3], PB[1 + (h % 2) * 3], PB[2 + (h % 2) * 3]
                cx.op("pe", lambda e, p0=p0, h=h, c0=c0, L=L: e.matmul(p0[0:L, 0:L], rkT[:, h, c0:c0 + L], rqT[:, h, c0:c0 + L], start=True, stop=True),
                      reads=[rkT, rqT], writes=[p0])
                cx.op("dve", lambda e, p0=p0, h=h, L=L: e.tensor_tensor(out=scT[0:L, 0:L], in0=p0[0:L, 0:L], in1=intraT[0:L, h, 0:L], op=ALU.mult),
                      reads=[p0, intraT], writes=[scT])
                cx.op("dve", lambda e, h=h, c0=c0, L=L: e.tensor_tensor(out=qr[:, 0:L], in0=rqT[:, h, c0:c0 + L], in1=readB[:, h, 0:L], op=ALU.mult),
                      reads=[rqT, readB], writes=[qr])

                def fo(e, p1=p1, h=h, si=si, L=L):
                    e.matmul(p1[:, 0:L], vtok[0:L, si, h * 128:(h + 1) * 128], scT[0:L, 0:L], start=True, stop=False)
                    return e.matmul(p1[:, 0:L], Sb[:, h, :], qr[:, 0:L], start=False, stop=True)
                cx.op("pe", fo, reads=[vtok, scT, Sb, qr], writes=[p1])
                cx.op("act", lambda e, p1=p1, h=h, c0=c0, L=L: e.copy(out=oret[:, h, c0:c0 + L], in_=p1[:, 0:L]), reads=[p1], writes=[oret])
                cx.op("pe", lambda e, p2=p2, h=h, si=si, L=L: e.matmul(p2[:, 0:128], ktok[0:L, si, h * 128:(h + 1) * 128],
                                                                     vtok[0:L, si, h * 128:(h + 1) * 128], start=True, stop=True),
                      reads=[ktok, vtok], writes=[p2])
                cx.op("dve", lambda e, p2=p2, h=h, Sst=Sst, L=L: e.scalar_tensor_tensor(
                    out=Sst[:, h, :], in0=Sst[:, h, :], scalar=float(RET_DECAY[h] ** L), in1=p2[:, 0:128], op0=ALU.mult, op1=ALU.add),
                    reads=[p2, Sst], writes=[Sst])
        if _DBG["cut"] <= 2:
            return cut_here()
        for h in range(8):
            tb = tmpn[h % 2]
            cx.op("act", lambda e, tb=tb, h=h: e.activation(out=tb[:, 0:T], in_=oret[:, h, 0:T], func=AF.Square), reads=[oret], writes=[tb])
            cx.op("pe", lambda e, tb=tb: e.matmul(PB[7][:, 0:T], ones128[:, :], tb[:, 0:T], start=True, stop=True),
                  reads=[ones128, tb], writes=[PB[7]])
            cx.op("act", lambda e: e.activation(out=rstd[:, 0:T], in_=PB[7][:, 0:T], func=AF.Sqrt, bias=EPS_B[:, 0:1]),
                  reads=[PB[7], EPS_B], writes=[rstd])
            cx.op("dve", lambda e: e.reciprocal(out=rstd[:, 0:T], in_=rstd[:, 0:T]), reads=[rstd], writes=[rstd])
            cx.op("dve", lambda e, tb=tb, h=h: e.tensor_tensor(out=tb[:, 0:T], in0=oret[:, h, 0:T], in1=rstd[:, 0:T], op=ALU.mult),
                  reads=[oret, rstd], writes=[tb])
            cx.op("dve", lambda e, tb=tb, h=h: e.tensor_tensor(out=oTret[:, h, 0:T], in0=tb[:, 0:T], in1=rgT[:, h, 0:T], op=ALU.mult),
                  reads=[tb, rgT], writes=[oTret])
        if _DBG["cut"] <= 3:
            return cut_here()
        ndoff = 256 if first else 0
        for si, (slot, c0, L) in enumerate(seqs):
            kb, kv = keyview(i, prompt, si, L)
            vpb = vsw[i] if prompt else vprev_s[si]
            vpv = vsw[i].t[:, 0, :] if prompt else vprev_s[si].t[:, :]
            NK = 128 + L
            for hq in range(16):
                kvh = hq // 8
                sc = i * 16 + hq
                pS, pTt, pO = PB[(hq % 2) * 3], PB[(hq % 2) * 3 + 1], PB[(hq % 2) * 3 + 2]
                cx.op("pe", lambda e, pS=pS, hq=hq, kvh=kvh, c0=c0, L=L, kv=kv: e.matmul(
                    pS[0:L, 0:NK], aqT[0:64, hq, c0:c0 + L], kv[0:64, kvh, 0:NK], start=True, stop=True), reads=[aqT, kb], writes=[pS])
                cx.op("dve", lambda e, pS=pS, hq=hq, L=L: e.scalar_tensor_tensor(
                    out=sb32[0:L, 0:NK], in0=negdist[0:L, ndoff:ndoff + NK], scalar=SLOPES[hq], in1=pS[0:L, 0:NK], op0=ALU.mult, op1=ALU.add),
                    reads=[negdist, pS], writes=[sb32])
                cx.op("dve", lambda e, L=L: e.tensor_reduce(out=sm[0:L, 0:1], in_=sb32[0:L, 0:NK], axis=AX.X, op=ALU.max), reads=[sb32], writes=[sm])
                cx.op("dve", lambda e, L=L, sc=sc: e.tensor_scalar(out=sm[0:L, 1:2], in0=sm[0:L, 0:1], scalar1=sinkB[0:L, sc:sc + 1], scalar2=-1.0,
                                                                  op0=ALU.max, op1=ALU.mult), reads=[sm, sinkB], writes=[sm])
                cx.op("act", lambda e, L=L: e.activation(out=p32[0:L, 0:NK], in_=sb32[0:L, 0:NK], func=AF.Exp, bias=sm[0:L, 1:2]),
                      reads=[sb32, sm], writes=[p32])
                cx.op("act", lambda e, L=L, sc=sc: e.activation(out=sm[0:L, 2:3], in_=sinkB[0:L, sc:sc + 1], func=AF.Exp, bias=sm[0:L, 1:2]),
                      reads=[sinkB, sm], writes=[sm])
                cx.op("dve", lambda e, L=L: e.tensor_reduce(out=sm[0:L, 3:4], in_=p32[0:L, 0:NK], axis=AX.X, op=ALU.add), reads=[p32, sm], writes=[sm])
                cx.op("dve", lambda e, L=L: e.tensor_add(out=sm[0:L, 4:5], in0=sm[0:L, 3:4], in1=sm[0:L, 2:3]), reads=[sm], writes=[sm])
                cx.op("dve", lambda e, L=L: e.reciprocal(out=sm[0:L, 5:6], in_=sm[0:L, 4:5]), reads=[sm], writes=[sm])
                cx.op("dve", lambda e, L=L: e.tensor_scalar(out=pn[0:L, 0:NK], in0=p32[0:L, 0:NK], scalar1=sm[0:L, 5:6], scalar2=None, op0=ALU.mult),
                      reads=[p32, sm], writes=[pn])

                def ft(e, pTt=pTt, L=L):
                    e.matmul(pTt[:, 0:L], pn[0:L, 0:128], ident[0:L, 0:L], start=True, stop=True)
                    return e.matmul(pTt[0:L, 128:128 + L], pn[0:L, 128:128 + L], ident[0:L, 0:L], start=True, stop=True)
                cx.op("pe", ft, reads=[pn, ident], writes=[pTt])
                cx.op("act", lambda e, pTt=pTt, L=L: e.copy(out=pT[:, 0:L], in_=pTt[:, 0:L]), reads=[pTt], writes=[pT])
                cx.op("act", lambda e, pTt=pTt, L=L: e.copy(out=pT[0:L, 128:128 + L], in_=pTt[0:L, 128:128 + L]), reads=[pTt], writes=[pT])

                def fv(e, pO=pO, kvh=kvh, si=si, L=L, vpv=vpv):
                    e.matmul(pO[0:64, 0:L], vpv[:, kvh * 64:(kvh + 1) * 64], pT[:, 0:L], start=True, stop=False)
                    return e.matmul(pO[0:64, 0:L], vcur[0:L, si, kvh * 64:(kvh + 1) * 64], pT[0:L, 128:128 + L], start=False, stop=True)
                cx.op("pe", fv, reads=[vpb, vcur, pT], writes=[pO])
                cx.op("act", lambda e, pO=pO, hq=hq, c0=c0, L=L: e.copy(out=oTswa[0:64, hq, c0:c0 + L], in_=pO[0:64, 0:L]), reads=[pO], writes=[oTswa])
        if _DBG["cut"] <= 4:
            return cut_here()
        if prompt:
            cx.op("act", lambda e: e.copy(out=kTe[i][:, :, 0:128], in_=kTe[i][:, :, TP:TP + 128]), reads=[kTe[i]], writes=[kTe[i]])
            cx.op("act", lambda e: e.copy(out=vsw[i][:, 0, :], in_=vcur[:, 0, :]), reads=[vcur], writes=[vsw[i]])
        if prompt and last:
            cx.dma("sp", o_ret_p[i].rearrange("h d e -> d h e"), Sp[i][:], osem, reads=[Sp[i]])
            cx.dma("sp", o_k_p[i], kvtok32[:, 0, 0:128], osem, reads=[kvtok32])
            cx.dma("sp", o_v_p[i], kvtok32[:, 0, 128:256], osem, reads=[kvtok32])
        if not prompt:
            for s_ in range(4):
                cx.dma("sp", o_ret_s[i, s_].rearrange("h d e -> d h e"), Ssl[s_][:], osem, reads=[Ssl[s_]])
                cx.dma("sp", o_k_s[i, s_, 124:128, :], kvtok32[0:4, s_, 0:128], osem, reads=[kvtok32])
                cx.dma("sp", o_v_s[i, s_, 124:128, :], kvtok32[0:4, s_, 128:256], osem, reads=[kvtok32])
        for dm in range(KT):
            bR = wget(sc_or[i], dm, 128, 1024)
            vR = bR.t[:, 0:1024].rearrange("p (k n) -> p k n", n=128)
            bS_ = wget(sc_os[i], dm, 64, 2048)
            vS = bS_.t[0:64, 0:2048].rearrange("p (k n) -> p k n", n=128)
            pb = PB[dm % 6]

            def fw(e, pb=pb, vR=vR, vS=vS):
                ins = None
                for k in range(8):
                    ins = e.matmul(pb[:, 0:T], vR[:, k, :], oTret[:, k, 0:T], start=(k == 0), stop=False)
                for hq in range(16):
                    ins = e.matmul(pb[:, 0:T], vS[0:64, hq, :], oTswa[0:64, hq, 0:T], start=False, stop=(hq == 15))
                return ins
            cx.op("pe", fw, reads=[bR, bS_, oTret, oTswa], writes=[pb])
            cx.op("act", lambda e, pb=pb, dm=dm: e.copy(out=mixT[:, dm, 0:T], in_=pb[:, 0:T]), reads=[pb], writes=[mixT])
        postnorm_add(1, li, T)

    PI = math.pi

    def s5_prologue(i):
        cx.phase()
        lr = cx.ar("lr", [64, 128], F32)
        lim = cx.ar("lim", [64, 128], F32)
        dl = cx.ar("dl", [64, 128], F32)
        ar_ = cx.ar("ar_", [64, 128], F32)
        ai = cx.ar("ai", [64, 128], F32)
        t1 = cx.ar("t1", [64, 128], F32)
        t2 = cx.ar("t2", [64, 128], F32)
        lbre = cx.ar("lbre", [64, 128], F32)
        lbim = cx.ar("lbim", [64, 128], F32)
        cre = cx.ar("cre", [64, 128], F32)
        cim = cx.ar("cim", [64, 128], F32)
        bre = cx.ar("bre", [64, 128, 16], F32)
        bim = cx.ar("bim", [64, 128, 16], F32)
        Bre = cx.ar("Bre", [64, 128, 16], F32)
        Bim = cx.ar("Bim", [64, 128, 16], F32)
        tb = cx.ar("tb", [64, 128, 16], F32)
        B16 = cx.ar("B16", [16, 128, 2, 64], BF16)
        cx.dma("sp", lr[:], lam_re[i].rearrange("g p -> p g"), misc, writes=[lr], slow=True)
        cx.dma("sp", lim[:], lam_im[i].rearrange("g p -> p g"), misc, writes=[lim], slow=True)
        cx.dma("sp", dl[:], log_step[i].partition_broadcast(64), misc, writes=[dl], slow=True)
        for g0 in range(0, 128, 32):
            cx.dma("sp", bre[:, g0:g0 + 32, :], b_re[i][g0:g0 + 32].rearrange("g p h -> p g h"), misc, writes=[bre], slow=True)
            cx.dma("sp", bim[:, g0:g0 + 32, :], b_im[i][g0:g0 + 32].rearrange("g p h -> p g h"), misc, writes=[bim], slow=True)
        cx.op("dve", lambda e: e.memset(negpi[:], -PI), writes=[negpi])
        cx.op("act", lambda e: e.activation(out=dl[:], in_=dl[:], func=AF.Exp), reads=[dl], writes=[dl])
        cx.op("dve", lambda e: e.tensor_tensor(out=ar_[:], in0=lr[:], in1=dl[:], op=ALU.mult), reads=[lr, dl], writes=[ar_])
        cx.op("dve", lambda e: e.tensor_tensor(out=ai[:], in0=lim[:], in1=dl[:], op=ALU.mult), reads=[lim, dl], writes=[ai])
        cx.op("act", lambda e: e.activation(out=ar_[:], in_=ar_[:], func=AF.Exp), reads=[ar_], writes=[ar_])
        ti32 = cx.ar("ti32", [64, 128], mybir.dt.int32)
        tf = cx.ar("tf", [64, 128], F32)
        tm = cx.ar("tm", [64, 128], F32)

        def sin_of(dst, shift):
            cx.op("dve", lambda e: e.tensor_scalar(out=dst[:], in0=ai[:], scalar1=shift, scalar2=None, op0=ALU.add), reads=[ai], writes=[dst])
            cx.op("dve", lambda e: e.tensor_scalar(out=tf[:], in0=dst[:], scalar1=1.0 / (2 * PI), scalar2=None, op0=ALU.mult), reads=[dst], writes=[tf])
            cx.op("dve", lambda e: e.tensor_copy(out=ti32[:], in_=tf[:]), reads=[tf], writes=[ti32])
            cx.op("dve", lambda e: e.tensor_copy(out=tf[:], in_=ti32[:]), reads=[ti32], writes=[tf])
            cx.op("dve", lambda e: e.scalar_tensor_tensor(out=dst[:], in0=tf[:], scalar=-2 * PI, in1=dst[:], op0=ALU.mult, op1=ALU.add),
                  reads=[tf, dst], writes=[dst])
            cx.op("dve", lambda e: e.tensor_scalar(out=tm[:], in0=dst[:], scalar1=PI, scalar2=-2 * PI, op0=ALU.is_gt, op1=ALU.mult), reads=[dst], writes=[tm])
            cx.op("dve", lambda e: e.tensor_add(out=dst[:], in0=dst[:], in1=tm[:]), reads=[dst, tm], writes=[dst])
            cx.op("dve", lambda e: e.tensor_scalar(out=tm[:], in0=dst[:], scalar1=-PI, scalar2=2 * PI, op0=ALU.is_lt, op1=ALU.mult), reads=[dst], writes=[tm])
            cx.op("dve", lambda e: e.tensor_add(out=dst[:], in0=dst[:], in1=tm[:]), reads=[dst, tm], writes=[dst])
            cx.op("act", lambda e: e.activation(out=dst[:], in_=dst[:], func=AF.Sin), reads=[dst], writes=[dst])
        sin_of(t1, 0.0)
        sin_of(t2, 0.5 * PI)
        cx.op("dve", lambda e: e.tensor_tensor(out=lbim[:], in0=ar_[:], in1=t1[:], op=ALU.mult), reads=[ar_, t1], writes=[lbim])
        cx.op("dve", lambda e: e.tensor_tensor(out=lbre[:], in0=ar_[:], in1=t2[:], op=ALU.mult), reads=[ar_, t2], writes=[lbre])
        for r_ in range(2):
            cx.op("dve", lambda e, r_=r_: e.tensor_copy(out=A2[i][:, r_, :], in_=lbre[:]), reads=[lbre], writes=[A2[i]])
        cx.op("dve", lambda e: e.tensor_scalar(out=B2[i][:, 0, :], in0=lbim[:], scalar1=-1.0, scalar2=None, op0=ALU.mult),
              reads=[lbim], writes=[B2[i]])
        cx.op("dve", lambda e: e.tensor_copy(out=B2[i][:, 1, :], in_=lbim[:]), reads=[lbim], writes=[B2[i]])
        cx.op("dve", lambda e: e.tensor_scalar(out=t1[:], in0=lbre[:], scalar1=-1.0, scalar2=None, op0=ALU.add), reads=[lbre], writes=[t1])
        cx.op("dve", lambda e: e.tensor_tensor(out=t2[:], in0=lr[:], in1=lr[:], op=ALU.mult), reads=[lr], writes=[t2])
        cx.op("dve", lambda e: e.tensor_tensor(out=ai[:], in0=lim[:], in1=lim[:], op=ALU.mult), reads=[lim], writes=[ai])
        cx.op("dve", lambda e: e.tensor_add(out=t2[:], in0=t2[:], in1=ai[:]), reads=[t2, ai], writes=[t2])
        cx.op("dve", lambda e: e.reciprocal(out=t2[:], in_=t2[:]), reads=[t2], writes=[t2])
        cx.op("dve", lambda e: e.tensor_tensor(out=cre[:], in0=t1[:], in1=lr[:], op=ALU.mult), reads=[t1, lr], writes=[cre])
        cx.op("dve", lambda e: e.tensor_tensor(out=ai[:], in0=lbim[:], in1=lim[:], op=ALU.mult), reads=[lbim, lim], writes=[ai])
        cx.op("dve", lambda e: e.tensor_add(out=cre[:], in0=cre[:], in1=ai[:]), reads=[cre, ai], writes=[cre])
        cx.op("dve", lambda e: e.tensor_tensor(out=cre[:], in0=cre[:], in1=t2[:], op=ALU.mult), reads=[cre, t2], writes=[cre])
        cx.op("dve", lambda e: e.tensor_tensor(out=cim[:], in0=lbim[:], in1=lr[:], op=ALU.mult), reads=[lbim, lr], writes=[cim])
        cx.op("dve", lambda e: e.tensor_tensor(out=ai[:], in0=t1[:], in1=lim[:], op=ALU.mult), reads=[t1, lim], writes=[ai])
        cx.op("dve", lambda e: e.tensor_sub(out=cim[:], in0=cim[:], in1=ai[:]), reads=[cim, ai], writes=[cim])
        cx.op("dve", lambda e: e.tensor_tensor(out=cim[:], in0=cim[:], in1=t2[:], op=ALU.mult), reads=[cim, t2], writes=[cim])
        creB = cre.t[:].unsqueeze(2).to_broadcast([64, 128, 16])
        cimB = cim.t[:].unsqueeze(2).to_broadcast([64, 128, 16])
        cx.op("dve", lambda e: e.tensor_tensor(out=Bre[:], in0=bre[:], in1=creB, op=ALU.mult), reads=[bre, cre], writes=[Bre])
        cx.op("dve", lambda e: e.tensor_tensor(out=tb[:], in0=bim[:], in1=cimB, op=ALU.mult), reads=[bim, cim], writes=[tb])
        cx.op("dve", lambda e: e.tensor_sub(out=Bre[:], in0=Bre[:], in1=tb[:]), reads=[Bre, tb], writes=[Bre])
        cx.op("dve", lambda e: e.tensor_tensor(out=Bim[:], in0=bim[:], in1=creB, op=ALU.mult), reads=[bim, cre], writes=[Bim])
        cx.op("dve", lambda e: e.tensor_tensor(out=tb[:], in0=bre[:], in1=cimB, op=ALU.mult), reads=[bre, cim], writes=[tb])
        cx.op("dve", lambda e: e.tensor_add(out=Bim[:], in0=Bim[:], in1=tb[:]), reads=[Bim, tb], writes=[Bim])
        for g0 in range(0, 128, 4):
            pb = PB[(g0 // 4) % 4]

            def ftr(e, pb=pb, g0=g0):
                ins = None
                for gg in range(4):
                    for ri, src in enumerate((Bre, Bim)):
                        ins = e.transpose(pb[0:16, (gg * 2 + ri) * 64:(gg * 2 + ri + 1) * 64], src[:, g0 + gg, :], ident32[0:64, 0:64])
                return ins
            cx.op("pe", ftr, reads=[Bre, Bim, ident32], writes=[pb])
            cx.op("act", lambda e, pb=pb, g0=g0: e.copy(out=B16[0:16, g0:g0 + 4, :, :], in_=pb[0:16, 0:512]), reads=[pb], writes=[B16])
        cx.dma("sp", B16d[i].ap(), B16[:], osem, reads=[B16])
        cx.phase()
        cnat = cx.ar("cnat", [128, 2, 16, 64], F32)
        cx.dma("sp", cnat[:, 0, :, :], c_re[i], misc, writes=[cnat])
        cx.dma("sp", cnat[:, 1, :, :], c_im[i], misc, writes=[cnat])
        for h in range(16):
            pb = PB[4 + h % 2]

            def ftc(e, pb=pb, h=h):
                e.transpose(pb[0:64, 0:128], cnat[:, 0, h, :], ident32[:, :])
                return e.transpose(pb[0:64, 128:256], cnat[:, 1, h, :], ident32[:, :])
            cx.op("pe", ftc, reads=[cnat, ident32], writes=[pb])
            cx.op("act", lambda e, pb=pb, h=h: e.copy(out=C64[i][:, :, 0, h], in_=pb[0:64, 0:128]), reads=[pb], writes=[C64[i]])
            cx.op("act", lambda e, pb=pb, h=h: e.mul(out=C64[i][:, :, 1, h], in_=pb[0:64, 128:256], mul=-1.0), reads=[pb], writes=[C64[i]])
        cx.op("dve", lambda e: e.tensor_tensor(out=gd[:, i, :], in0=gains[:, 0, 2 * i + 1, :], in1=dsk[:, i, :], op=ALU.mult),
              reads=[gains, dsk], writes=[gd])

    def s5_mixer(i, li, seqs, T, prompt, last):
        cx.phase()
        W = 16 if prompt else 4
        B16 = cx.ar("B16", [16, 128, 2, 64], BF16)
        u16w = cx.ar("u16w", [16, 128, W], BF16)
        Z = cx.ar("Z", [64, 2, 128, W], F32)
        Hh = cx.ar("Hh", [64, 2, 128, W], BF16)
        y16w = cx.ar("y16w", [16, 128, W], F32)
        yT = cx.ar("yT", [128, KT, TP], F32)
        s1 = cx.ar("s1", [64, 2, 128], F32)
        s2 = cx.ar("s2", [64, 2, 128], F32)
        sig = [cx.ar("sig", [128, TP], F32) for _ in range(2)]
        cx.dma("sp", B16[:], B16d[i].ap(), misc, writes=[B16])
        if prompt:
            H3s = [H3p[i]]
        else:
            H3s = [cx.ar("H3s", [64, 3, 128], F32) for _ in range(4)]
            for s_ in range(4):
                cx.dma("sp", H3s[s_][:, 0, :], st_sre[i, s_].rearrange("g p -> p g"), misc, writes=[H3s[s_]], slow=True)
                cx.dma("sp", H3s[s_][:, 1, :], st_sim[i, s_].rearrange("g p -> p g"), misc, writes=[H3s[s_]], slow=True)
                cx.op("act", lambda e, s_=s_: e.copy(out=H3s[s_][:, 2, :], in_=H3s[s_][:, 0, :]), reads=[H3s[s_]], writes=[H3s[s_]])
        prenorm(0, li, T)
        u16v = u16w.t[:].rearrange("h (k g) w -> h k g w", g=8)
        y16v = y16w.t[:].rearrange("h (k g) w -> h k g w", g=8)
        for si, (slot, c0, L) in enumerate(seqs):
            H3 = H3s[si]
            for t0 in range(0, L, W):
                a0 = c0 + t0
                for gl in range(8):
                    cx.dma("sp", u16v[:, :, gl, :], hT[gl * 16:(gl + 1) * 16, :, a0:a0 + W], usem[gl], reads=[hT], writes=[u16w])
                for gc in range(8):
                    pa, pbk = PB[(gc % 2) * 2], PB[(gc % 2) * 2 + 1]

                    def fb(e, pa=pa, pbk=pbk, gc=gc):
                        ins = None
                        for gg in range(16):
                            g = gc * 16 + gg
                            e.matmul(pa[0:64, gg * W:(gg + 1) * W], B16[0:16, g, 0, :], u16w[0:16, g, :], start=True, stop=True)
                            ins = e.matmul(pbk[0:64, gg * W:(gg + 1) * W], B16[0:16, g, 1, :], u16w[0:16, g, :], start=True, stop=True)
                        return ins
                    cx.op("pe", fb, reads=[B16, u16w], writes=[pa, pbk])
                    cx.op("act", lambda e, pa=pa, gc=gc: e.copy(out=Z[:, 0, gc * 16:(gc + 1) * 16, :], in_=pa[0:64, 0:16 * W]), reads=[pa], writes=[Z])
                    cx.op("act", lambda e, pbk=pbk, gc=gc: e.copy(out=Z[:, 1, gc * 16:(gc + 1) * 16, :], in_=pbk[0:64, 0:16 * W]), reads=[pbk], writes=[Z])
                for t in range(W):
                    cx.op("dve", lambda e: e.tensor_tensor(out=s1[:], in0=A2[i][:], in1=H3[:, 0:2, :], op=ALU.mult), reads=[A2[i], H3], writes=[s1])
                    cx.op("dve", lambda e: e.tensor_tensor(out=s2[:], in0=B2[i][:], in1=H3[:, 1:3, :], op=ALU.mult), reads=[B2[i], H3], writes=[s2])
                    cx.op("dve", lambda e: e.tensor_add(out=s1[:], in0=s1[:], in1=s2[:]), reads=[s1, s2], writes=[s1])
                    cx.op("dve", lambda e, t=t: e.tensor_add(out=H3[:, 0:2, :], in0=s1[:], in1=Z[:, :, :, t]), reads=[s1, Z], writes=[H3])
                    cx.op("dve", lambda e, t=t: e.tensor_add(out=H3[:, 2, :], in0=s1[:, 0, :], in1=Z[:, 0, :, t]), reads=[s1, Z], writes=[H3])
                    cx.op("act", lambda e, t=t: e.copy(out=Hh[:, :, :, t], in_=H3[:, 0:2, :]), reads=[H3], writes=[Hh])
                for gc in range(8):
                    pc = PB[4 + gc % 2]

                    def fc(e, pc=pc, gc=gc):
                        ins = None
                        for gg in range(16):
                            g = gc * 16 + gg
                            e.matmul(pc[0:16, gg * W:(gg + 1) * W], C64[i][:, g, 0, :], Hh[:, 0, g, :], start=True, stop=False)
                            ins = e.matmul(pc[0:16, gg * W:(gg + 1) * W], C64[i][:, g, 1, :], Hh[:, 1, g, :], start=False, stop=True)
                        return ins
                    cx.op("pe", fc, reads=[C64[i], Hh], writes=[pc])
                    cx.op("act", lambda e, pc=pc, gc=gc: e.copy(out=y16w[0:16, gc * 16:(gc + 1) * 16, :], in_=pc[0:16, 0:16 * W]), reads=[pc], writes=[y16w])
                for gl in range(8):
                    cx.dma("sp", yT[gl * 16:(gl + 1) * 16, :, a0:a0 + W], y16v[:, :, gl, :], ysem[gl], reads=[y16w], writes=[yT])
        if prompt and last:
            cx.dma("sp", o_sre_p[i].rearrange("g p -> p g"), H3p[i][:, 0, :], osem, reads=[H3p[i]], slow=True)
            cx.dma("sp", o_sim_p[i].rearrange("g p -> p g"), H3p[i][:, 1, :], osem, reads=[H3p[i]], slow=True)
        if not prompt:
            for s_ in range(4):
                cx.dma("sp", o_sre_s[i, s_].rearrange("g p -> p g"), H3s[s_][:, 0, :], osem, reads=[H3s[s_]], slow=True)
                cx.dma("sp", o_sim_s[i, s_].rearrange("g p -> p g"), H3s[s_][:, 1, :], osem, reads=[H3s[s_]], slow=True)
        for k in range(KT):
            tb = tmpn[k % 2]
            cx.op("dve", lambda e, k=k, tb=tb: e.scalar_tensor_tensor(
                out=tb[:, 0:T], in0=xT[:, k, 0:T], scalar=gd[:, i, k:k + 1], in1=rstd[:, 0:T], op0=ALU.mult, op1=ALU.mult),
                reads=[xT, gd, rstd], writes=[tb])
            cx.op("dve", lambda e, k=k, tb=tb: e.tensor_add(out=tb[:, 0:T], in0=tb[:, 0:T], in1=yT[:, k, 0:T]), reads=[tb, yT], writes=[tb])
            cx.op("act", lambda e, k=k, tb=tb: e.activation(out=hT[:, k, 0:T], in_=tb[:, 0:T], func=AF.Gelu), reads=[tb], writes=[hT])
        for dm in range(KT):
            bA = bB = wget(sc_gl[i], dm, 128, 4096)
            vAB = bA.t[:, 0:4096].rearrange("p (j k n) -> p j k n", j=2, n=128)
            vA, vB = vAB[:, 0], vAB[:, 1]
            pa, pbk = PB[(2 * dm) % 4], PB[(2 * dm + 1) % 4]
            mm_fm(pa, (vA, bA), 0, 128, hT, lambda k: hT[:, k, 0:T], KT, T)
            mm_fm(pbk, (vB, bB), 0, 128, hT, lambda k: hT[:, k, 0:T], KT, T)
            sg = sig[dm % 2]
            cx.op("act", lambda e, pbk=pbk, sg=sg: e.activation(out=sg[:, 0:T], in_=pbk[:, 0:T], func=AF.Sigmoid), reads=[pbk], writes=[sg])
            cx.op("dve", lambda e, pa=pa, sg=sg, dm=dm: e.tensor_tensor(out=mixT[:, dm, 0:T], in0=sg[:, 0:T], in1=pa[:, 0:T], op=ALU.mult),
                  reads=[sg, pa], writes=[mixT])
        postnorm_add(1, li, T)

    def load_x(src2d, T):
        cx.phase()
        xin = cx.ar("xin", [128, D], F32)
        cx.dma("sp", xin[0:T, :], src2d, misc, writes=[xin])
        for k4 in range(0, KT, 4):
            pb = PB[(k4 // 4) % 2]

            def ft(e, pb=pb, k4=k4):
                ins = None
                for j in range(4):
                    ins = e.transpose(pb[:, j * 128:j * 128 + T], xin[0:T, (k4 + j) * 128:(k4 + j + 1) * 128], ident32[0:T, 0:T])
                return ins
            cx.op("pe", ft, reads=[xin, ident32], writes=[pb])
            for j in range(4):
                cx.op("act", lambda e, pb=pb, k4=k4, j=j: e.copy(out=xT[:, k4 + j, 0:T], in_=pb[:, j * 128:j * 128 + T]), reads=[pb], writes=[xT])

    def store_y(dst2d, T):
        cx.phase()
        xin = cx.ar("xin", [128, D], F32)
        for k4 in range(0, KT, 4):
            pb = PB[(k4 // 4) % 2]

            def ft(e, pb=pb, k4=k4):
                ins = None
                for j in range(4):
                    ins = e.transpose(pb[0:T, j * 128:(j + 1) * 128], xT[:, k4 + j, 0:T], ident32[:, :])
                return ins
            cx.op("pe", ft, reads=[xT, ident32], writes=[pb])
            cx.op("act", lambda e, pb=pb, k4=k4: e.copy(out=xin[0:T, k4 * 128:(k4 + 4) * 128], in_=pb[0:T, 0:512]), reads=[pb], writes=[xin])
        cx.dma("sp", dst2d, xin[0:T, :], osem, reads=[xin])

    def zero_states():
        for l in range(4):
            cx.op("dve", lambda e, l=l: e.memset(aprev[l][0][:], 0.0), writes=[aprev[l][0]])
        for i in range(2):
            cx.op("dve", lambda e, i=i: e.memset(Sp[i][:], 0.0), writes=[Sp[i]])
            cx.op("dve", lambda e, i=i: e.memset(kTe[i][:], 0.0), writes=[kTe[i]])
            cx.op("dve", lambda e, i=i: e.memset(vsw[i][:], 0.0), writes=[vsw[i]])
            cx.op("dve", lambda e, i=i: e.memset(H3p[i][:], 0.0), writes=[H3p[i]])

    def run_layers(seqs, T, first, prompt, last):
        for l in range(depth):
            if l % 2 == 0:
                even_mixer(l // 2, l, seqs, T, prompt, first, last)
            else:
                s5_mixer(l // 2, l, seqs, T, prompt, last)
            ffn(l, seqs, T)

    cx.init_arena(ARENA)
    zero_states()
    weight_prologue()
    for i in range(depth // 2):
        s5_prologue(i)
    for ti in range(ntiles):
        load_x(xp[ti * TP:(ti + 1) * TP, :], TP)
        run_layers([(0, 0, TP)], TP, ti == 0, True, ti == ntiles - 1)
        store_y(yp[ti * TP:(ti + 1) * TP, :], TP)
    for l in range(depth):
        for j in range(2):
            cx.dma("sp", o_conv_p[l, j].rearrange("(m p) -> p m", p=128), aprev[l][0][:, :, j], osem,
                   reads=[aprev[l][0]], slow=True)
    if do_sample:
        for l in range(depth):
            for s_ in range(4):
                for j in range(2):
                    cx.dma("sp", aprev[l][s_][:, :, j], st_conv[l, s_, j].rearrange("(m p) -> p m", p=128), misc,
                           writes=[aprev[l][s_]], slow=True)
        load_x(xs.rearrange("s t d -> (s t) d"), 16)
        run_layers([(s_, 4 * s_, 4) for s_ in range(4)], 16, False, False, True)
        store_y(ys.rearrange("s t d -> (s t) d"), 16)
        for l in range(depth):
            for s_ in range(4):
                for j in range(2):
                    cx.dma("sp", o_conv_s[l, s_, j].rearrange("(m p) -> p m", p=128), aprev[l][s_][:, :, j], osem,
                           reads=[aprev[l][s_]], slow=True)
    cx.barrier()
    return nc


_TABS = None


def kernel(**inp):
    global _TABS
    if _TABS is None:
        _TABS = host_tables()
    f = lambda a: np.ascontiguousarray(np.asarray(a, dtype=np.float32))
    nc = build(**_CFG)
    dp_ = _CFG['depth']
    nl4, nle, nlo = dp_, (dp_ + 1) // 2, max(dp_ // 2, 1)
    in_maps = []
    for c in range(8):
        sl = slice(4 * c, 4 * c + 4)
        m = {
            "xp": f(inp["x_prompt"][c % 2]), "xs": f(inp["x_sample"][sl]),
            "st_ret": f(inp["state_ret"][:, sl]),
            "st_k": f(np.asarray(inp["cache_swa_k"])[:, sl].reshape(2, 4, 128, 128)),
            "st_v": f(np.asarray(inp["cache_swa_v"])[:, sl].reshape(2, 4, 128, 128)),
            "st_sre": f(inp["state_ssm_re"][:, sl]), "st_sim": f(inp["state_ssm_im"][:, sl]),
            "st_conv": f(inp["state_ffn_conv"][:, sl]),
            "g_mix_pre": f(inp["norm_mix_pre"]), "g_mix_post": f(inp["norm_mix_post"]),
            "g_ffn_pre": f(inp["norm_ffn_pre"]), "g_ffn_post": f(inp["norm_ffn_post"]),
            "w_in": f(inp["w_in_even"][:nle]), "w_out": f(inp["w_out_even"][:nle]), "sinks": f(inp["swa_sinks"]),
            "lam_re": f(inp["ssm_lam_re"]), "lam_im": f(inp["ssm_lam_im"]), "log_step": f(inp["ssm_log_step"]),
            "b_re": f(inp["ssm_b_re"]), "b_im": f(inp["ssm_b_im"]), "c_re": f(inp["ssm_c_re"]), "c_im": f(inp["ssm_c_im"]),
            "ssm_d": f(inp["ssm_d"]), "w_glu": f(inp["w_glu"][:nlo]),
            "w_a": f(inp["ffn_w_a"][:nl4]), "w_g": f(inp["ffn_w_g"][:nl4]), "conv_w": f(inp["ffn_conv_w"]),
            "conv_b": f(inp["ffn_conv_b"]), "w_down": f(inp["ffn_w_down"][:nl4]),
            "t_intraT": _TABS["intraT"], "t_readB": _TABS["readB"], "t_writeT": _TABS["writeT"],
            "t_negdist": _TABS["negdist"], "t_ident": _TABS["ident"],
        }
        in_maps.append(m)
    res = run_bass_kernel_spmd(nc, in_maps, core_ids=list(range(8)))
    r = res.results
    st = lambda name, cores: np.stack([r[c][name] for c in cores], axis=1)
    cat = lambda name: np.concatenate([r[c][name] for c in range(8)], axis=1 if r[0][name].ndim > 3 or name != "ys" else 0)
    y_prompt = np.stack([r[0]["yp"], r[1]["yp"]], 0)
    y_sample = np.concatenate([r[c]["ys"] for c in range(8)], 0)
    ret_p = st("o_ret_p", [0, 1])
    ret_s = np.concatenate([r[c]["o_ret_s"] for c in range(8)], 1)
    k_p = st("o_k_p", [0, 1]).reshape(2, 2, 128, 2, 64)
    k_s = np.concatenate([r[c]["o_k_s"] for c in range(8)], 1).reshape(2, 32, 128, 2, 64)
    v_p = st("o_v_p", [0, 1]).reshape(2, 2, 128, 2, 64)
    v_s = np.concatenate([r[c]["o_v_s"] for c in range(8)], 1).reshape(2, 32, 128, 2, 64)
    sre_p = st("o_sre_p", [0, 1])
    sre_s = np.concatenate([r[c]["o_sre_s"] for c in range(8)], 1)
    sim_p = st("o_sim_p", [0, 1])
    sim_s = np.concatenate([r[c]["o_sim_s"] for c in range(8)], 1)
    conv_p = st("o_conv_p", [0, 1])
    conv_s = np.concatenate([r[c]["o_conv_s"] for c in range(8)], 1)
    outs = (y_prompt, y_sample, ret_p, ret_s, k_p, k_s, v_p, v_s, sre_p, sre_s, sim_p, sim_s, conv_p, conv_s)
    return tuple(np.ascontiguousarray(o, dtype=np.float32) for o in outs)
```
